# Optimizing a Trainium2 kernel written in Bass

```python
import math
import jax, jax.numpy as jnp
from jax import lax
import numpy as np

D_MODEL = 2048
BATCH = 2
SEQ = 8192
DEPTH = 1

CTX_LEN = 256
GRID_W = 64
MIX_WIDTH = D_MODEL
MLSTM_HEAD_DIM = 128
MLSTM_HEADS = (MIX_WIDTH // 2) // MLSTM_HEAD_DIM
MLSTM_WIDTH = MLSTM_HEADS * MLSTM_HEAD_DIM
MLSTM_CHUNK = 128
DIFF_V_DIM = 128
DIFF_QK_DIM = DIFF_V_DIM // 2
DIFF_HEADS = (MIX_WIDTH - MLSTM_WIDTH) // DIFF_V_DIM
DIFF_WIDTH = DIFF_HEADS * DIFF_V_DIM
ROPE_FREQS = DIFF_QK_DIM // 4
ROPE_BASE = 10000.0
Q_BLOCK = 128
D_FF = 5632
CONV_W = 3
N_MOD = 6
EPS = 1e-6

OFF_MQ = 0
OFF_MK = MLSTM_WIDTH
OFF_MV = 2 * MLSTM_WIDTH
OFF_MO = 3 * MLSTM_WIDTH
OFF_MG = 4 * MLSTM_WIDTH
OFF_DQ = OFF_MG + 4 * MLSTM_HEADS
OFF_DK = OFF_DQ + DIFF_HEADS * 2 * DIFF_QK_DIM
OFF_DV = OFF_DK + DIFF_HEADS * 2 * DIFF_QK_DIM
IN_COLS = OFF_DV + DIFF_WIDTH

kernel_name = "hybrid_mlstm_diffattn_convffn_dit_block"


def rmsnorm(a, g):
    a32 = a.astype(jnp.float32)
    y = a32 * lax.rsqrt(jnp.mean(a32 * a32, axis=-1, keepdims=True) + EPS)
    return (y * g.astype(jnp.float32)).astype(a.dtype)


def modulate(h, shift, scale):
    return h * (1 + scale) + shift


def dwconv3(a, w, b):
    ap = jnp.pad(a, ((0, 0), (1, 1), (0, 0)))
    return ap[:, :-2] * w[0] + ap[:, 1:-1] * w[1] + ap[:, 2:] * w[2] + b


def axial_rope_tables(rows, n_lat):
    row = jnp.broadcast_to(jnp.arange(rows, dtype=jnp.float32)[:, None], (rows, GRID_W)).reshape(-1)
    col = jnp.broadcast_to(jnp.arange(GRID_W, dtype=jnp.float32)[None, :], (rows, GRID_W)).reshape(-1)
    inv_freq = ROPE_BASE ** (-jnp.arange(ROPE_FREQS, dtype=jnp.float32) / ROPE_FREQS)
    ang = jnp.stack([row[:, None] * inv_freq, col[:, None] * inv_freq], axis=1)
    ang = jnp.stack([ang, ang], axis=2).reshape(n_lat, 4 * ROPE_FREQS)
    return jnp.cos(ang), jnp.sin(ang)


def rope_2d(x, cos, sin):
    xr = x.reshape(*x.shape[:-1], 2, 2, ROPE_FREQS)
    rot = jnp.stack([-xr[..., 1, :], xr[..., 0, :]], axis=-2).reshape(x.shape)
    return x * cos[:, None, None, :] + rot * sin[:, None, None, :]


def to_mlstm_heads(a):
    B, T, _ = a.shape
    return a.reshape(B, T, MLSTM_HEADS, MLSTM_HEAD_DIM).transpose(0, 2, 1, 3).astype(jnp.float32)


def mlstm_project(p, conv_w, conv_b, b_gate):
    B, T, _ = p.shape
    qk = jax.nn.silu(dwconv3(p[..., OFF_MQ:OFF_MV], conv_w, conv_b))
    q = to_mlstm_heads(qk[..., :MLSTM_WIDTH])
    k = to_mlstm_heads(qk[..., MLSTM_WIDTH:]) * (MLSTM_HEAD_DIM ** -0.5)
    v = to_mlstm_heads(p[..., OFF_MV:OFF_MO])
    g = (p[..., OFF_MG:OFF_DQ].reshape(B, T, 4, MLSTM_HEADS) + b_gate).astype(jnp.float32)
    g = jnp.transpose(g, (2, 0, 3, 1))
    return q, k, v, g[0], jax.nn.log_sigmoid(g[1]), g[2], jax.nn.log_sigmoid(g[3])


def mlstm_chunkwise(q, k, v, i_pre, log_f, state):
    B, H, T, Dh = q.shape
    nc = T // MLSTM_CHUNK

    def to_chunks(a):
        return jnp.moveaxis(a.reshape(B, H, nc, MLSTM_CHUNK, *a.shape[3:]), 2, 0)

    xs = (to_chunks(q), to_chunks(k), to_chunks(v), to_chunks(i_pre), to_chunks(log_f))
    causal = jnp.tril(jnp.ones((MLSTM_CHUNK, MLSTM_CHUNK), dtype=bool))

    def step(carry, inp):
        C, n, m = carry
        qc, kc, vc, ic, fc = inp
        b = jnp.cumsum(fc, axis=-1)
        d_log = jnp.where(causal, b[..., :, None] - b[..., None, :] + ic[..., None, :], -jnp.inf)
        m_inter = b + m[..., None]
        m_t = jnp.maximum(m_inter, jnp.max(d_log, axis=-1))
        w_intra = jnp.exp(d_log - m_t[..., None])
        w_inter = jnp.exp(m_inter - m_t)
        s = jnp.einsum('bhld,bhsd->bhls', qc, kc) * w_intra
        num = w_inter[..., None] * jnp.einsum('bhld,bhde->bhle', qc, C) + jnp.einsum('bhls,bhse->bhle', s, vc)
        den = w_inter * jnp.einsum('bhld,bhd->bhl', qc, n) + jnp.sum(s, axis=-1)
        h = num / jnp.maximum(jnp.abs(den), jnp.exp(-m_t))[..., None]
        b_last = b[..., -1]
        log_w = b_last[..., None] - b + ic
        m_new = jnp.maximum(b_last + m, jnp.max(log_w, axis=-1))
        w_s = jnp.exp(log_w - m_new[..., None])
        decay = jnp.exp(b_last + m - m_new)
        C_new = decay[..., None, None] * C + jnp.einsum('bhs,bhsd,bhse->bhde', w_s, kc, vc)
        n_new = decay[..., None] * n + jnp.einsum('bhs,bhsd->bhd', w_s, kc)
        return (C_new, n_new, m_new), h

    state, hs = lax.scan(step, state, xs)
    return jnp.moveaxis(hs, 0, 2).reshape(B, H, T, Dh), state


def mlstm_output(h, p, g_norm):
    B, H, T, Dh = h.shape
    h = rmsnorm(h.transpose(0, 2, 1, 3), g_norm.reshape(H, Dh)).reshape(B, T, H * Dh)
    return h * jax.nn.sigmoid(p[..., OFF_MO:OFF_MG].astype(jnp.float32))


def mlstm_group(p_lat, p_ctx, conv_w, conv_b, b_gate, g_norm, need_ctx):
    ql, kl, vl, ifl, lfl, ibl, lbl = mlstm_project(p_lat, conv_w, conv_b, b_gate)
    qc, kc, vc, ifc, lfc, ibc, lbc = mlstm_project(p_ctx, conv_w, conv_b, b_gate)
    B = p_lat.shape[0]
    zero = (jnp.zeros((B, MLSTM_HEADS, MLSTM_HEAD_DIM, MLSTM_HEAD_DIM), jnp.float32),
            jnp.zeros((B, MLSTM_HEADS, MLSTM_HEAD_DIM), jnp.float32),
            jnp.zeros((B, MLSTM_HEADS), jnp.float32))

    def rev(*arrs):
        return tuple(jnp.flip(a, axis=2) for a in arrs)

    hc_f, st_f = mlstm_chunkwise(qc, kc, vc, ifc, lfc, zero)
    hl_f, _ = mlstm_chunkwise(ql, kl, vl, ifl, lfl, st_f)
    hc_b, st_b = mlstm_chunkwise(*rev(qc, kc, vc, ibc, lbc), zero)
    hl_b, _ = mlstm_chunkwise(*rev(ql, kl, vl, ibl, lbl), st_b)
    out_lat = mlstm_output(hl_f + jnp.flip(hl_b, axis=2), p_lat, g_norm)
    out_ctx = mlstm_output(hc_f + jnp.flip(hc_b, axis=2), p_ctx, g_norm) if need_ctx else None
    return out_lat, out_ctx


def diff_split(p):
    B, T, _ = p.shape
    q = p[..., OFF_DQ:OFF_DK].reshape(B, T, DIFF_HEADS, 2, DIFF_QK_DIM)
    k = p[..., OFF_DK:OFF_DV].reshape(B, T, DIFF_HEADS, 2, DIFF_QK_DIM)
    v = p[..., OFF_DV:IN_COLS].reshape(B, T, DIFF_HEADS, DIFF_V_DIM)
    return q, k, v


def diff_attend(q, k, v, lam):
    B, Tq = q.shape[:2]
    qb = jnp.moveaxis(q.astype(jnp.float32).reshape(B, Tq // Q_BLOCK, Q_BLOCK, *q.shape[2:]), 1, 0)
    k32 = k.astype(jnp.float32)
    v32 = v.astype(jnp.float32)
    scale = DIFF_QK_DIM ** -0.5

    def one_block(qblk):
        s = jnp.einsum('bqhcd,bkhcd->bhcqk', qblk, k32) * scale
        pr = jax.nn.softmax(s, axis=-1)
        pd = pr[:, :, 0] - lam * pr[:, :, 1]
        return jnp.einsum('bhqk,bkhe->bqhe', pd, v32)

    o = lax.map(one_block, qb)
    return jnp.moveaxis(o, 0, 1).reshape(B, Tq, DIFF_HEADS, DIFF_V_DIM)


def diff_output(o, g_norm, lam_init):
    B, T = o.shape[:2]
    return (rmsnorm(o, g_norm) * (1.0 - lam_init)).reshape(B, T, DIFF_WIDTH)


def diff_group(p_lat, p_ctx, cos, sin, lam, lam_init, g_norm, need_ctx):
    ql, kl, vl = diff_split(p_lat)
    qc, kc, vc = diff_split(p_ctx)
    ql = rope_2d(ql, cos, sin)
    kl = rope_2d(kl, cos, sin)
    k_all = jnp.concatenate([kc.astype(jnp.float32), kl.astype(jnp.float32)], axis=1)
    v_all = jnp.concatenate([vc, vl], axis=1)
    out_lat = diff_output(diff_attend(ql, k_all, v_all, lam), g_norm, lam_init)
    out_ctx = diff_output(diff_attend(qc, kc, vc, lam), g_norm, lam_init) if need_ctx else None
    return out_lat, out_ctx


def conv_ffn(h, w_up, conv_w, conv_b, w_down):
    u = dwconv3(h @ w_up, conv_w, conv_b)
    gate, val = jnp.split(u, 2, axis=-1)
    return (jax.nn.silu(gate) * val) @ w_down


def setup_inputs(seed: int = 0) -> dict:
    key = jax.random.key(seed)
    ks = jax.random.split(key, 32)
    f32 = jnp.float32

    def nrm(k, shape, s):
        return jax.random.normal(k, shape, f32) * s

    x = nrm(ks[0], (BATCH, SEQ, D_MODEL), 1.0)
    c = nrm(ks[1], (BATCH, D_MODEL), 1.0)
    ctx = nrm(ks[2], (BATCH, CTX_LEN, D_MODEL), 1.0)
    c_ctx = nrm(ks[3], (D_MODEL,), 1.0)
    w_mod = nrm(ks[4], (DEPTH, D_MODEL, N_MOD * D_MODEL), 0.5 * D_MODEL ** -0.5)
    b_mod = nrm(ks[5], (DEPTH, N_MOD * D_MODEL), 0.02)
    g_pre_mix = 1.0 + nrm(ks[6], (DEPTH, D_MODEL), 0.05)
    g_post_mix = 1.0 + nrm(ks[7], (DEPTH, D_MODEL), 0.05)
    w_in = nrm(ks[8], (DEPTH, D_MODEL, IN_COLS), D_MODEL ** -0.5)
    i_bias = nrm(ks[9], (DEPTH, 2, MLSTM_HEADS), 0.1)
    f_bias = 3.0 + 3.0 * jax.random.uniform(ks[10], (DEPTH, 2, MLSTM_HEADS), f32)
    b_gate = jnp.stack([i_bias[:, 0], f_bias[:, 0], i_bias[:, 1], f_bias[:, 1]], axis=1)
    conv_qk_w = nrm(ks[11], (DEPTH, CONV_W, 2 * MLSTM_WIDTH), CONV_W ** -0.5)
    conv_qk_b = nrm(ks[12], (DEPTH, 2 * MLSTM_WIDTH), 0.02)
    g_mlstm = 1.0 + nrm(ks[13], (DEPTH, MLSTM_WIDTH), 0.05)
    lambda_q1 = nrm(ks[14], (DEPTH, DIFF_QK_DIM), 0.1)
    lambda_k1 = nrm(ks[15], (DEPTH, DIFF_QK_DIM), 0.1)
    lambda_q2 = nrm(ks[16], (DEPTH, DIFF_QK_DIM), 0.1)
    lambda_k2 = nrm(ks[17], (DEPTH, DIFF_QK_DIM), 0.1)
    g_diff = 1.0 + nrm(ks[18], (DEPTH, DIFF_V_DIM), 0.05)
    w_out = nrm(ks[19], (DEPTH, MIX_WIDTH, D_MODEL), MIX_WIDTH ** -0.5)
    g_pre_ffn = 1.0 + nrm(ks[20], (DEPTH, D_MODEL), 0.05)
    g_post_ffn = 1.0 + nrm(ks[21], (DEPTH, D_MODEL), 0.05)
    w_up = nrm(ks[22], (DEPTH, D_MODEL, 2 * D_FF), D_MODEL ** -0.5)
    conv_ffn_w = nrm(ks[23], (DEPTH, CONV_W, 2 * D_FF), CONV_W ** -0.5)
    conv_ffn_b = nrm(ks[24], (DEPTH, 2 * D_FF), 0.02)
    w_down = nrm(ks[25], (DEPTH, D_FF, D_MODEL), D_FF ** -0.5)
    return {"x": x, "c": c, "ctx": ctx, "c_ctx": c_ctx, "w_mod": w_mod, "b_mod": b_mod,
            "g_pre_mix": g_pre_mix, "g_post_mix": g_post_mix, "w_in": w_in, "b_gate": b_gate,
            "conv_qk_w": conv_qk_w, "conv_qk_b": conv_qk_b, "g_mlstm": g_mlstm,
            "lambda_q1": lambda_q1, "lambda_k1": lambda_k1, "lambda_q2": lambda_q2, "lambda_k2": lambda_k2,
            "g_diff": g_diff, "w_out": w_out, "g_pre_ffn": g_pre_ffn, "g_post_ffn": g_post_ffn,
            "w_up": w_up, "conv_ffn_w": conv_ffn_w, "conv_ffn_b": conv_ffn_b, "w_down": w_down}


def reference(x, c, ctx, c_ctx, w_mod, b_mod, g_pre_mix, g_post_mix, w_in, b_gate,
              conv_qk_w, conv_qk_b, g_mlstm, lambda_q1, lambda_k1, lambda_q2, lambda_k2,
              g_diff, w_out, g_pre_ffn, g_post_ffn, w_up, conv_ffn_w, conv_ffn_b, w_down):
    n_lat = x.shape[1]
    ROWS = n_lat // GRID_W
    cos, sin = axial_rope_tables(ROWS, n_lat)
    for l in range(DEPTH):
        need_ctx = l < DEPTH - 1
        mod = jax.nn.silu(c) @ w_mod[l] + b_mod[l]
        sh1, sc1, ga1, sh2, sc2, ga2 = jnp.split(mod[:, None, :], N_MOD, axis=-1)
        cmod = jax.nn.silu(c_ctx) @ w_mod[l] + b_mod[l]
        csh1, csc1, cga1, csh2, csc2, cga2 = jnp.split(cmod, N_MOD, axis=-1)

        p_lat = modulate(rmsnorm(x, g_pre_mix[l]), sh1, sc1) @ w_in[l]
        p_ctx = modulate(rmsnorm(ctx, g_pre_mix[l]), csh1, csc1) @ w_in[l]
        m_lat, m_ctx = mlstm_group(p_lat, p_ctx, conv_qk_w[l], conv_qk_b[l], b_gate[l], g_mlstm[l], need_ctx)
        lam_init = 0.8 - 0.6 * math.exp(-0.3 * l)
        lam = (jnp.exp(jnp.sum(lambda_q1[l].astype(jnp.float32) * lambda_k1[l].astype(jnp.float32)))
               - jnp.exp(jnp.sum(lambda_q2[l].astype(jnp.float32) * lambda_k2[l].astype(jnp.float32)))
               + lam_init)
        d_lat, d_ctx = diff_group(p_lat, p_ctx, cos, sin, lam, lam_init, g_diff[l], need_ctx)
        y_lat = jnp.concatenate([m_lat, d_lat], axis=-1).astype(x.dtype) @ w_out[l]
        x = x + ga1 * rmsnorm(y_lat, g_post_mix[l])

        h_lat = modulate(rmsnorm(x, g_pre_ffn[l]), sh2, sc2)
        x = x + ga2 * rmsnorm(conv_ffn(h_lat, w_up[l], conv_ffn_w[l], conv_ffn_b[l], w_down[l]), g_post_ffn[l])

        if need_ctx:
            y_ctx = jnp.concatenate([m_ctx, d_ctx], axis=-1).astype(ctx.dtype) @ w_out[l]
            ctx = ctx + cga1 * rmsnorm(y_ctx, g_post_mix[l])
            h_ctx = modulate(rmsnorm(ctx, g_pre_ffn[l]), csh2, csc2)
            ctx = ctx + cga2 * rmsnorm(conv_ffn(h_ctx, w_up[l], conv_ffn_w[l], conv_ffn_b[l], w_down[l]), g_post_ffn[l])
    return x
```

```python
import os
import numpy as np
import ml_dtypes
import concourse.bass as bass
import concourse.mybir as mybir
from concourse.bass_utils import run_bass_kernel_spmd

F32 = mybir.dt.float32
BF16 = mybir.dt.bfloat16
AF = mybir.ActivationFunctionType
ALU = mybir.AluOpType
AX = mybir.AxisListType

PE, ACT, DVE, POOL, SP = "pe", "act", "dve", "pool", "sp"
ENGS = [PE, ACT, DVE, POOL, SP]

D = 2048
KC = 16
T = 8192
TC = 256
TA = T + TC
NCH = TA // 128
DFF = 5632
HC = DFF // 128
EPS = 1e-6
NCOL = 1800
C_MQ, C_MK, C_DQ, C_DK, C_MO, C_MV, C_G, C_DV = 0, 256, 512, 768, 1024, 1280, 1536, 1544
TQ = 2048
LAM_INIT = 0.2


class Buf:
    __slots__ = ("name", "w", "r")

    def __init__(self, name=""):
        self.name = name
        self.w = None
        self.r = []


class DSem:
    def __init__(self, sem):
        self.sem = sem
        self.total = 0


class Sched:
    def __init__(self, nc):
        self.nc = nc
        self.ops = {e: [] for e in ENGS}
        self.dsems = []
        self.bufs = {}

    def buf(self, key):
        b = self.bufs.get(key)
        if b is None:
            b = Buf(str(key))
            self.bufs[key] = b
        return b

    def new_dsem(self, name):
        d = DSem(self.nc.alloc_semaphore(name))
        self.dsems.append(d)
        return d

    def _deps(self, eng, reads, writes, dsem=None, is_dma=False):
        deps = []
        for b in reads:
            if b.w is not None:
                deps.append(b.w)
        for b in writes:
            if b.w is not None:
                if not (dsem is not None and b.w[0] == "d" and b.w[1] is dsem):
                    deps.append(b.w)
            deps.extend(b.r)
        out = []
        for d in deps:
            if d[0] == "e" and d[1] == eng and not is_dma:
                if eng == PE:
                    continue
                if not any((b.w is d) for b in reads):
                    continue
            out.append(d)
        return out

    def op(self, eng, fn, reads=(), writes=()):
        deps = self._deps(eng, reads, writes)
        idx = len(self.ops[eng])
        ev = ("e", eng, idx)
        self.ops[eng].append(dict(fn=fn, deps=deps, kind="c", flag=False))
        for b in reads:
            b.r = [x for x in b.r if not (x[0] == "e" and x[1] == eng)]
            b.r.append(ev)
        for b in writes:
            b.w = ev
            b.r = []
        return ev

    def dma(self, q, fn, dsem, reads=(), writes=(), new_gen=True, inc=16):
        deps = self._deps(q, reads, writes, dsem=dsem, is_dma=True)
        if new_gen and dsem.total > 0:
            deps.append(("d", dsem, dsem.total))
        dsem.total += inc
        ev = ("d", dsem, dsem.total)
        self.ops[q].append(dict(fn=fn, deps=deps, kind="d", dsem=dsem, flag=False, inc=inc))
        for b in reads:
            b.r = [x for x in b.r if not (x[0] == "d" and x[1] is dsem)]
            b.r.append(ev)
        for b in writes:
            b.w = ev
            b.r = []
        return ev

    def barrier(self):
        evs = []
        for e in ENGS:
            for i in range(len(self.ops[e]) - 1, -1, -1):
                if self.ops[e][i]["kind"] == "c":
                    evs.append(("e", e, i))
                    break
        for d in self.dsems:
            if d.total > 0:
                evs.append(("d", d, d.total))
        for e in ENGS:
            self.ops[e].append(dict(fn=None, deps=[x for x in evs if not (x[0] == "e" and x[1] == e)], kind="w", flag=False))
        for b in self.bufs.values():
            b.w = None
            b.r = []

    def wait_all(self, eng, evs):
        self.ops[eng].append(dict(fn=None, deps=list(evs), kind="w", flag=False))

    def finalize(self):
        nc = self.nc
        ops = self.ops
        EPOCH = 30000
        for e in ENGS:
            for o in ops[e]:
                for d in o["deps"]:
                    if d[0] == "e":
                        ops[d[1]][d[2]]["flag"] = True
        val = {}
        esem = {}
        for e in ENGS:
            c = 0
            for i, o in enumerate(ops[e]):
                if o["kind"] == "c" and o["flag"]:
                    val[(e, i)] = (c // EPOCH, c % EPOCH + 1)
                    c += 1
            esem[e] = [nc.alloc_semaphore(f"es_{e}{k}") for k in range(c // EPOCH + 1)]

        def replay(e, eng):
            seen = {}
            for oi_, o in enumerate(ops[e]):
                o["idx"] = oi_
                need = {}
                for d in o["deps"]:
                    if d[0] == "e":
                        key = ("e", d[1])
                        v = val[(d[1], d[2])]
                        sem = esem[d[1]][v[0]]
                    else:
                        key = ("d", id(d[1]))
                        v = (0, d[2])
                        sem = d[1].sem
                    if seen.get(key, (0, 0)) >= v:
                        continue
                    if key not in need or need[key][1] < v:
                        need[key] = (sem, v)
                for key, (sem, v) in need.items():
                    eng.wait_ge(sem, v[1])
                    seen[key] = v
                if o["fn"] is None:
                    continue
                ins = o["fn"](eng)
                if o["kind"] == "d":
                    ins.then_inc(o["dsem"].sem, o["inc"])
                elif o["flag"]:
                    ins.then_inc(esem[e][val[(e, o["idx"])][0]], 1)

        with nc.Block() as block:
            @block.tensor
            def _(eng):
                replay(PE, eng)

            @block.scalar
            def _(eng):
                replay(ACT, eng)

            @block.vector
            def _(eng):
                replay(DVE, eng)

            @block.gpsimd
            def _(eng):
                replay(POOL, eng)

            @block.sync
            def _(eng):
                replay(SP, eng)


def build_program(stage=99):
    nc = bass.Bass("TRN2", target_bir_lowering=False)
    S = Sched(nc)
    B = S.buf

    def IN(name, shape, dt=F32):
        return nc.dram_tensor(name, shape, dt, kind="ExternalInput").ap()

    def SCR(name, shape, dt=F32):
        kind = "ExternalOutput" if (stage < 99 and name in DBG_OUT.get(stage, ())) else "Internal"
        return nc.dram_tensor(name, shape, dt, kind=kind).ap()

    xb = IN("xb", [T, D]); ctxb = IN("ctxb", [TC, D]); xq = IN("xq", [TQ + 2, D])
    cT = IN("cT", [128, KC, 2]); w_mod = IN("w_mod", [D, 6 * D]); b_mod = IN("b_mod", [1, 6 * D])
    gT = IN("gT", [128, 2, KC]); grow = IN("grow", [2, D])
    w_in = IN("w_in", [D, NCOL]); convw = IN("convw", [128, 4, 4]); bgate = IN("bgate", [1, 8 * NCH])
    gml = IN("gml", [1, 256]); gdf = IN("gdf", [128, 1]); gdr = IN("gdr", [1, 128]); lamv = IN("lamv", [1, 256])
    w_out = IN("w_out", [D, D]); w_up = IN("w_up", [D, 2 * DFF]); cfw = IN("cfw", [128, 2 * HC, 4]); w_down = IN("w_down", [DFF, D])
    identf_d = IN("identf", [128, 128]); maskF_d = IN("maskF", [128, 128]); maskB_d = IN("maskB", [128, 128])
    triF_d = IN("triF", [128, 128]); triB_d = IN("triB", [128, 128]); perm_d = IN("perm", [128, 128]); jb_d = IN("jb", [NCH, NCH])
    cos_d = IN("cos", [128, T]); sin_d = IN("sin", [128, T]); hmask_d = IN("hmask", [128, 2]); qsel_d = IN("qsel", [128, 4])
    out_d = nc.dram_tensor("out", [TQ, D], F32, kind="ExternalOutput").ap()

    DBG_OUT = {1: ("modrow", "mqT_l", "mkT_c", "oS", "mvS", "gS", "dqT", "dkT", "dvS"), 2: tuple(f"mixL{p}" for p in range(8)), 3: tuple(f"mixL{p}" for p in range(8)), 4: ("h2S", "xmidS"), 6: ("h2S", "xmidS")}
    modrow = SCR("modrow", [2, 6 * D])
    mqT_c = SCR("mqT_c", [2, 128, TC + 1], BF16); mqT_l = SCR("mqT_l", [2, 128, T + 1], BF16)
    mkT_c = SCR("mkT_c", [2, 128, TC + 1], BF16); mkT_l = SCR("mkT_l", [2, 128, T + 1], BF16)
    oS = SCR("oS", [2, T, 128]); mvS = SCR("mvS", [2, TA, 128], BF16); gS = SCR("gS", [TA, 8])
    dqT = SCR("dqT", [2, 128, T], BF16); dkT = SCR("dkT", [2, 128, TA], BF16); dvS = SCR("dvS", [2, TA, 128], BF16)
    hfS = SCR("hfS", [2, T, 128])
    mixL = [SCR(f"mixL{p}", [512, 1024], BF16) for p in range(8)]
    mixA = [SCR(f"mixA{p}", [2048, 1024], BF16) for p in range(8)]
    xmidS = SCR("xmidS", [TQ, D])
    h2S = SCR("h2S", [D, TQ + 2], BF16)
    wuS = SCR("wuS", [HC, 128, KC * 256], BF16)
    wdS = SCR("wdS", [D // 128, 128, HC * 128], BF16)

    big = nc.alloc_sbuf_tensor("big", [128, 52000], F32)
    apos = [0]

    def carve(n, dt=F32):
        o = apos[0]
        apos[0] += (n + 7) // 8 * 8
        assert apos[0] <= 52000, apos[0]
        v = big[:, o:o + n]
        return v if dt is F32 else v.bitcast(dt)

    def carve_bf(nel):
        return carve((nel + 1) // 2, BF16)[:, 0:nel]

    psp = [nc.alloc_psum_tensor(f"psp{i}", [128, 1024], F32) for i in range(4)]
    psf = [psp[0][:, 0:512], psp[0][:, 512:1024], psp[1][:, 0:512], psp[1][:, 512:1024], psp[2][:, 0:512], psp[2][:, 512:1024]]
    psb = [psp[3][:, 0:512].bitcast(BF16), psp[3][:, 512:1024].bitcast(BF16)]

    def mm(out, lhsT, rhs, start, stop, R, W):
        S.op(PE, lambda e: e.matmul(out, lhsT=lhsT, rhs=rhs, start=start, stop=stop), reads=R, writes=W)

    def tp(out, in_, ident, R, W):
        S.op(PE, lambda e: e.transpose(out=out, in_=in_, identity=ident), reads=R, writes=W)

    def act(out, in_, func, R, W, bias=None, scale=None, accum=None, eng=ACT):
        kw = {}
        if bias is not None:
            kw["bias"] = bias
        if scale is not None:
            kw["scale"] = scale
        if accum is not None:
            kw["accum_out"] = accum
        S.op(eng, lambda e: e.activation(out=out, in_=in_, func=func, **kw), reads=R, writes=W)

    def cp(eng, out, in_, R, W):
        if eng == ACT:
            S.op(ACT, lambda e: e.copy(out=out, in_=in_), reads=R, writes=W)
        else:
            S.op(eng, lambda e: e.tensor_copy(out=out, in_=in_), reads=R, writes=W)

    def tt(eng, out, a, b, op, R, W):
        S.op(eng, lambda e: e.tensor_tensor(out=out, in0=a, in1=b, op=op), reads=R, writes=W)

    def ts(eng, out, a, s1, s2, op0, op1, R, W):
        if s2 is None:
            S.op(eng, lambda e: e.tensor_scalar(out=out, in0=a, scalar1=s1, scalar2=None, op0=op0), reads=R, writes=W)
        else:
            S.op(eng, lambda e: e.tensor_scalar(out=out, in0=a, scalar1=s1, scalar2=s2, op0=op0, op1=op1), reads=R, writes=W)

    def stt(eng, out, a, sc, b, op0, op1, R, W):
        S.op(eng, lambda e: e.scalar_tensor_tensor(out=out, in0=a, scalar=sc, in1=b, op0=op0, op1=op1), reads=R, writes=W)

    def memset(eng, out, v, W):
        S.op(eng, lambda e: e.memset(out, v), writes=W)

    def dma(q, out, in_, dsem, R, W, new_gen=True):
        return S.dma(q, lambda e: e.dma_start(out=out, in_=in_, allow_slow_non_contiguous=True), dsem, reads=R, writes=W, new_gen=new_gen)

    st_sems = [S.new_dsem(f"st{i}") for i in range(8)]
    st_i = [0]

    stp_sems = [S.new_dsem(f"stp{i}") for i in range(8)]
    st_q = [SP]

    def store(out, in_, R, W, q=None):
        q = q or st_q[0]
        pool_ = st_sems if q == SP else stp_sems
        d = pool_[st_i[0] % len(pool_)]
        st_i[0] += 1
        return dma(q, out, in_, d, R, W)

    ld_sems = [S.new_dsem(f"ld{i}") for i in range(8)]
    ld_i = [0]

    ldp_sems = [S.new_dsem(f"ldp{i}") for i in range(4)]

    def load(out, in_, R, W, q=SP):
        pool_ = ld_sems if q == SP else ldp_sems
        d = pool_[ld_i[0] % len(pool_)]
        ld_i[0] += 1
        return dma(q, out, in_, d, R, W)

    c_mark = apos[0]
    identf = carve(128); identb = carve_bf(128)
    maskF = carve(128); maskB = carve(128); triF = carve(128); triB = carve(128); permM = carve(128)
    jb = carve(NCH)
    onesf = carve(128); onesb = carve_bf(128)
    cTs = carve(KC * 2).rearrange("p (k j) -> p k j", j=2)
    gTs = carve(2 * KC).rearrange("p (a k) -> p a k", k=KC)
    fmv = carve(96)
    A1 = carve(KC); A1c = carve(KC); A2 = carve(KC)
    convs = carve(16).rearrange("p (c j) -> p c j", j=4)
    gdfs = carve(1); gdsc = carve(1)
    lam4 = carve(256); lamc = carve(2); lamneg = carve(1)
    hmask = carve(2); qsel = carve(4)
    epsc = carve(1)
    for (dst, src, nm) in [(identf, identf_d, "identf"), (maskF, maskF_d, "maskF"), (maskB, maskB_d, "maskB"), (triF, triF_d, "triF"),
                           (triB, triB_d, "triB"), (permM, perm_d, "perm"), (convs, convw, "convs"), (gdfs, gdf, "gdfs"),
                           (hmask, hmask_d, "hmask"), (qsel, qsel_d, "qsel"), (cTs, cT, "cTs"), (gTs, gT, "gTs")]:
        load(dst, src, [], [B(nm)])
    load(jb[0:NCH, :], jb_d, [], [B("jb")])
    load(lam4, lamv.partition_broadcast(128), [], [B("lam4")])
    cp(DVE, identb, identf, [B("identf")], [B("identb")])
    memset(DVE, onesf, 1.0, [B("onesf")])
    memset(DVE, onesb, 1.0, [B("onesb")])
    memset(DVE, epsc, EPS, [B("epsc")])
    lamt = carve(128)
    tt(DVE, lamt[:, 0:64], lam4[:, 0:64], lam4[:, 64:128], ALU.mult, [B("lam4")], [B("lamt")])
    tt(DVE, lamt[:, 64:128], lam4[:, 128:192], lam4[:, 192:256], ALU.mult, [B("lam4")], [B("lamt")])
    S.op(DVE, lambda e: e.reduce_sum(out=lamc, in_=lamt.rearrange("p (a b) -> p a b", b=64), axis=AX.X), reads=[B("lamt")], writes=[B("lamc")])
    act(lamc, lamc, AF.Exp, [B("lamc")], [B("lamc")])
    tt(DVE, lamneg, lamc[:, 1:2], lamc[:, 0:1], ALU.subtract, [B("lamc")], [B("lamneg")])
    ts(DVE, lamneg, lamneg, -LAM_INIT, None, ALU.add, None, [B("lamneg")], [B("lamneg")])
    ts(DVE, gdsc, gdfs, 1.0 - LAM_INIT, None, ALU.mult, None, [B("gdfs")], [B("gdsc")])

    scs = carve(KC * 2).rearrange("p (k j) -> p k j", j=2)
    act(scs, cTs, AF.Silu, [B("cTs")], [B("scs")])
    p0_mark = apos[0]
    NWT = 256
    wm_sem = [S.new_dsem(f"wm{i}") for i in range(2)]
    w_mod_v = w_mod.rearrange("(k p) c -> p k c", p=128)
    modbuf = {}

    def mod_alloc():
        modbuf["wm"] = [carve(KC * NWT).rearrange("p (k c) -> p k c", c=NWT) for _ in range(2)]
        modbuf["bm"] = [carve(512) for _ in range(2)]
        modbuf["mrow"] = [carve(512) for _ in range(2)]

    mod_issued = set()

    def mod_dma(ct):
        if ct in mod_issued or ct >= 6 * D // 512:
            return
        mod_issued.add(ct)
        wm = modbuf["wm"]; bm = modbuf["bm"]
        for hf in range(2):
            i = ct * 2 + hf
            c0 = i * NWT
            dma(SP, wm[i % 2], w_mod_v[:, :, c0:c0 + NWT], wm_sem[i % 2], [], [B(("wm", i % 2))])
        load(bm[ct % 2][0:2, :], b_mod[:, ct * 512:(ct + 1) * 512].partition_broadcast(2), [], [B(("bm", ct % 2))])

    def mod_tile(ct, pst, pbuf, prefetch=True):
        wm = modbuf["wm"]; bm = modbuf["bm"]; mrow = modbuf["mrow"]
        mod_dma(ct)
        for hf in range(2):
            i = ct * 2 + hf
            for k in range(KC):
                mm(pst[0:2, hf * NWT:(hf + 1) * NWT], scs[:, k, :], wm[i % 2][:, k, :], k == 0, k == KC - 1,
                   [B("scs"), B(("wm", i % 2))], [pbuf])
        tt(DVE, mrow[ct % 2][0:2, :], pst[0:2, :], bm[ct % 2][0:2, :], ALU.add, [pbuf, B(("bm", ct % 2))], [B(("mrow", ct % 2))])
        store(modrow[:, ct * 512:(ct + 1) * 512], mrow[ct % 2][0:2, :], [B(("mrow", ct % 2))], [B(("modrow", ct))])
        if prefetch:
            mod_dma(ct + 1)

    mod_alloc()
    for ct in range(2 * D // 512):
        mod_tile(ct, psf[ct % 2], B(("psf", ct % 2)), prefetch=(ct + 1 < 2 * D // 512))
    v96 = carve(128)
    for j, (row, c0) in enumerate([(0, 0), (0, D), (1, 0), (1, D)]):
        load(v96[j * 16:(j + 1) * 16, :], modrow[row:row + 1, c0:c0 + D].rearrange("o (k p) -> (o k) p", p=128), [B(("modrow", x)) for x in range(8)], [B("v96")])
    tp_out = psf[2]
    S.op(PE, lambda e: e.transpose(out=tp_out[:, 0:64], in_=v96[0:64, :], identity=identf[0:64, 0:64]), reads=[B("v96"), B("identf")], writes=[B(("psf", 2))])
    cp(DVE, fmv[:, 0:64], tp_out[:, 0:64], [B(("psf", 2))], [B("fmv")])
    SH1, SC1, CSH1, CSC1, SH2, SC2 = [fmv[:, j * 16:(j + 1) * 16] for j in range(6)]
    stt(DVE, A1, SC1, 1.0, gTs[:, 0, :], ALU.add, ALU.mult, [B("fmv"), B("gTs")], [B("A1")])
    stt(DVE, A1c, CSC1, 1.0, gTs[:, 0, :], ALU.add, ALU.mult, [B("fmv"), B("gTs")], [B("A1c")])
    S.barrier()
    apos[0] = p0_mark

    st_q[0] = POOL
    p1_mark = apos[0]
    win = carve_bf(KC * NCOL).rearrange("p (k c) -> p k c", c=NCOL)
    win_sem = S.new_dsem("win")
    w_in_v = w_in.rearrange("(k p) c -> p k c", p=128)
    for k in range(KC):
        dma(POOL, win[:, k, :], w_in_v[:, k, :], win_sem, [], [B("win")], new_gen=(k == 0))
    xr = [carve(D) for _ in range(3)]
    xr_sem = [S.new_dsem(f"xr{i}") for i in range(3)]
    xn = [carve_bf(D) for _ in range(4)]
    h1T = [carve_bf(KC * 512).rearrange("p (k t) -> p k t", t=512) for _ in range(2)]
    ss = carve(4); rstd = carve(4)
    stg = [[carve(520) for _ in range(2)] for _ in range(4)]
    ctmp = [carve(512) for _ in range(2)]
    csig = [carve(512) for _ in range(2)]
    obf = [carve_bf(512) for _ in range(4)]
    qf = [carve(512) for _ in range(2)]
    cosb = [carve(512) for _ in range(2)]; sinb = [carve(512) for _ in range(2)]
    rt1 = [carve(512) for _ in range(2)]; rt2 = [carve(512) for _ in range(2)]
    otm = [carve(256) for _ in range(2)]
    vtm = [carve_bf(256) for _ in range(4)]
    gtm = [carve(8) for _ in range(2)]
    flush = carve(8)
    cnt = {"blk": 0, "o": 0, "q": 0, "r": 0, "ot": 0, "vt": 0, "gt": 0, "ct": 0}
    ones512 = carve_bf(512)
    memset(DVE, ones512, 1.0, [B("ones512")])
    sh1hl = carve_bf(2 * KC).rearrange("p (j k) -> p j k", k=KC)
    sh1t = carve(KC)
    c1f = carve(NCOL); c1hl = carve_bf(2 * NCOL).rearrange("p (j c) -> p j c", c=NCOL)
    cp(DVE, sh1hl[:, 0, :], SH1, [B("fmv")], [B("sh1hl")])
    tt(DVE, sh1t, SH1, sh1hl[:, 0, :], ALU.subtract, [B("fmv"), B("sh1hl")], [B("sh1t")])
    cp(DVE, sh1hl[:, 1, :], sh1t, [B("sh1t")], [B("sh1hl")])

    def make_shift_and_fold():
        for gi, c0 in enumerate(range(0, NCOL, 512)):
            n = min(512, NCOL - c0)
            pst = psf[gi % 4]; pbuf = B(("psf", gi % 4))
            i_ = 0
            for j in range(2):
                for k in range(KC):
                    mm(pst[0:1, 0:n], sh1hl[:, j, k:k + 1], win[:, k, c0:c0 + n], i_ == 0, i_ == 2 * KC - 1, [B("sh1hl"), B("win")], [pbuf])
                    i_ += 1
            cp(ACT, c1f[0:1, c0:c0 + n], pst[0:1, 0:n], [pbuf], [B("c1f")])
        cp(DVE, c1hl[0:1, 0, :], c1f[0:1, :], [B("c1f")], [B("c1hl")])
        tt(DVE, c1f[0:1, :], c1f[0:1, :], c1hl[0:1, 0, :], ALU.subtract, [B("c1f"), B("c1hl")], [B("c1f")])
        cp(DVE, c1hl[0:1, 1, :], c1f[0:1, :], [B("c1f")], [B("c1hl")])
        for k in range(KC):
            ts(DVE, win[:, k, :], win[:, k, :], A1[:, k:k + 1], None, ALU.mult, None, [B("win"), B("A1")], [B("win")])

    def shift_fm(pst, c0, nt, pbuf):
        for j in range(1):
            mm(pst[:, 0:nt], c1hl[0:1, j, c0:c0 + 128], ones512[0:1, 0:nt], j == 0, False, [B("c1hl"), B("ones512")], [pbuf])

    def shift_tm(pst, c0, n, pbuf):
        for j in range(1):
            mm(pst[:, 0:n], ones512[0:1, 0:128], c1hl[0:1, j, c0:c0 + n], j == 0, False, [B("c1hl"), B("ones512")], [pbuf])

    tiles = [(True, 0, TC)] + [(False, i * 512, 512) for i in range(T // 512)]

    pendA1 = []

    def stageA1_block(ti, bl):
        isc, t0, nt = tiles[ti]
        src = ctxb if isc else xb
        i = cnt["blk"]; cnt["blk"] += 1
        xs = xr[i % 3]
        dma(SP, xs, src[t0 + bl * 128:t0 + (bl + 1) * 128, :], xr_sem[i % 3], [], [B(("xr", i % 3))])
        memset(DVE, ss[:, 0:1], 0.0, [B("ss")])
        act(xn[bl], xs, AF.Square, [B(("xr", i % 3)), B("ss")], [B(("xn", bl)), B("ss")], accum=ss[:, 0:1])
        act(rstd[:, 0:1], ss[:, 0:1], AF.Ln, [B("ss"), B("epsc")], [B("rstd")], scale=1.0 / D, bias=epsc)
        act(rstd[:, 0:1], rstd[:, 0:1], AF.Exp, [B("rstd")], [B("rstd")], scale=-0.5)
        ts(DVE, xn[bl], xs, rstd[:, 0:1], None, ALU.mult, None, [B(("xr", i % 3)), B("rstd")], [B(("xn", bl))])

    def stageA1(ti):
        for bl in range(tiles[ti][2] // 128):
            stageA1_block(ti, bl)

    def queueA1(ti):
        for bl in range(tiles[ti][2] // 128):
            pendA1.append((ti, bl))

    def popA1():
        if pendA1:
            stageA1_block(*pendA1.pop(0))

    def stageA2(ti):
        isc, t0, nt = tiles[ti]
        hT = h1T[ti % 2]
        Asc = A1c if isc else A1
        Ash = CSH1 if isc else SH1
        for bl in range(nt // 128):
            for g in range(2):
                pb = psb[g]
                for kk in range(8):
                    k = g * 8 + kk
                    tp(pb[:, kk * 128:(kk + 1) * 128], xn[bl][:, k * 128:(k + 1) * 128], identb, [B(("xn", bl)), B("identb")], [B(("psb", g))])
                if not isc:
                    cp(ACT if g == 0 else DVE, hT[:, g * 8:(g + 1) * 8, bl * 128:(bl + 1) * 128], pb.rearrange("p (k t) -> p k t", t=128), [B(("psb", g))], [B(("h1T", ti % 2))])
                    continue
                for kk in range(8):
                    k = g * 8 + kk
                    eng = ACT if kk % 2 == 0 else DVE
                    if eng == ACT:
                        act(hT[:, k, bl * 128:(bl + 1) * 128], pb[:, kk * 128:(kk + 1) * 128], AF.Identity, [B(("psb", g)), B("A1"), B("A1c"), B("fmv")], [B(("h1T", ti % 2))],
                            scale=Asc[:, k:k + 1], bias=Ash[:, k:k + 1])
                    else:
                        ts(DVE, hT[:, k, bl * 128:(bl + 1) * 128], pb[:, kk * 128:(kk + 1) * 128], Asc[:, k:k + 1], Ash[:, k:k + 1], ALU.mult, ALU.add,
                           [B(("psb", g)), B("A1"), B("A1c"), B("fmv")], [B(("h1T", ti % 2))])

    def conv_chunk(ci, ti, pst, psbuf, isc, t0, nt, first, last):
        sg = stg[ci][ti % 2]; sgp = stg[ci][(ti - 1) % 2]
        sb = B(("stg", ci, ti % 2)); sbp = B(("stg", ci, (ti - 1) % 2))
        pnt = tiles[ti - 1][2] if ti > 0 else 0
        if first:
            memset(DVE, sg[:, 0:2], 0.0, [sb])
        else:
            cp(DVE, sg[:, 0:2], sgp[:, pnt:pnt + 2], [sbp], [sb])
        cp(ACT, sg[:, 2:2 + nt], pst[:, 0:nt], [psbuf], [sb])
        w0, w1, w2, bb = [convs[:, ci, j:j + 1] for j in range(4)]
        head = ci % 2
        isk = ci >= 2
        dst = (mkT_c if isc else mkT_l) if isk else (mqT_c if isc else mqT_l)

        def conv_out(n, src0, col0, zero_next=False):
            j = cnt["ct"]; cnt["ct"] += 1
            t = ctmp[j % 2]; tb = B(("ctmp", j % 2))
            ts(DVE, t[:, 0:n], sg[:, src0 + 1:src0 + 1 + n], w1, bb, ALU.mult, ALU.add, [sb, B("convs")], [tb])
            stt(DVE, t[:, 0:n], sg[:, src0:src0 + n], w0, t[:, 0:n], ALU.mult, ALU.add, [sb, B("convs"), tb], [tb])
            if not zero_next:
                stt(DVE, t[:, 0:n], sg[:, src0 + 2:src0 + 2 + n], w2, t[:, 0:n], ALU.mult, ALU.add, [sb, B("convs"), tb], [tb])
            oi = cnt["o"]; cnt["o"] += 1
            ob = obf[oi % 4]; obb = B(("obf", oi % 4))
            if not isk:
                act(ob[:, 0:n], t[:, 0:n], AF.Silu, [tb], [obb])
            else:
                sgm = csig[j % 2]; sgb = B(("csig", j % 2))
                act(sgm[:, 0:n], t[:, 0:n], AF.Sigmoid, [tb], [sgb])
                stt(DVE, ob[:, 0:n], t[:, 0:n], 128.0 ** -0.5, sgm[:, 0:n], ALU.mult, ALU.mult, [tb, sgb], [obb])
            store(dst[head, :, col0:col0 + n], ob[:, 0:n], [obb], [B((("k" if isk else "q"), head, isc, col0))])

        conv_out(nt, 0, t0)
        if last:
            conv_out(1, nt, t0 + nt, zero_next=True)

    def stageB(ti):
        isc, t0, nt = tiles[ti]
        hT = h1T[ti % 2]; hb = B(("h1T", ti % 2))
        first = ti in (0, 1)
        last = ti in (0, len(tiles) - 1)
        fm = [(C_MQ, "c", 0), (C_MQ + 128, "c", 1), (C_MK, "c", 2), (C_MK + 128, "c", 3),
              (C_DQ, "dq", 0), (C_DQ + 128, "dq", 1), (C_DK, "dk", 0), (C_DK + 128, "dk", 1)]
        for fi, (c0, kind, idx) in enumerate(fm):
            if isc and kind == "dq":
                continue
            pi = fi % 4
            pst = psf[pi]; pbuf = B(("psf", pi))
            if not isc:
                shift_fm(pst, c0, nt, pbuf)
            for k in range(KC):
                mm(pst[:, 0:nt], win[:, k, c0:c0 + 128], hT[:, k, 0:nt], (k == 0) and isc, k == KC - 1, [B("win"), hb], [pbuf])
            if kind == "c":
                conv_chunk(idx, ti, pst, pbuf, isc, t0, nt, first, last)
            else:
                dst = dqT if kind == "dq" else dkT
                col0 = t0 if (kind == "dq" or isc) else TC + t0
                oi = cnt["o"]; cnt["o"] += 1
                ob = obf[oi % 4]; obb = B(("obf", oi % 4))
                if isc:
                    cp(ACT, ob[:, 0:nt], pst[:, 0:nt], [pbuf], [obb])
                else:
                    j = cnt["r"]; cnt["r"] += 1
                    q_ = qf[j % 2]; qb = B(("qf", j % 2))
                    cp(ACT, q_, pst[:, 0:nt], [pbuf], [qb])
                    if fi == 4:
                        jj = ti % 2
                        load(cosb[jj], cos_d[:, t0:t0 + nt], [], [B(("cos", jj))])
                        load(sinb[jj], sin_d[:, t0:t0 + nt], [], [B(("sin", jj))])
                    jj = ti % 2
                    prot = psf[4 + (j % 2)]; prb = B(("psf", 4 + (j % 2)))
                    mm(prot[:, 0:nt], permM, q_, True, True, [B("perm"), qb], [prb])
                    tt(DVE, rt1[j % 2], q_, cosb[jj], ALU.mult, [qb, B(("cos", jj))], [B(("rt1", j % 2))])
                    tt(DVE, rt2[j % 2], prot[:, 0:nt], sinb[jj], ALU.mult, [prb, B(("sin", jj))], [B(("rt2", j % 2))])
                    tt(DVE, ob[:, 0:nt], rt1[j % 2], rt2[j % 2], ALU.add, [B(("rt1", j % 2)), B(("rt2", j % 2))], [obb])
                store(dst[idx, :, col0:col0 + nt], ob[:, 0:nt], [obb], [B((kind, idx, isc, t0))])
            if fi % 2 == 1:
                popA1()
        while pendA1:
            popA1()
        for bl in range(nt // 128):
            tg0 = (0 if isc else TC) + t0 + bl * 128
            lhs = lambda k: hT[:, k, bl * 128:(bl + 1) * 128]
            if not isc:
                pst = psf[0]; pbuf = B(("psf", 0))
                shift_tm(pst, C_MO, 256, pbuf)
                for k in range(KC):
                    mm(pst[:, 0:256], lhs(k), win[:, k, C_MO:C_MO + 256], False, k == KC - 1, [B("win"), hb], [pbuf])
                j = cnt["ot"]; cnt["ot"] += 1
                act(otm[j % 2], pst[:, 0:256], AF.Sigmoid, [pbuf], [B(("otm", j % 2))])
                tl = t0 + bl * 128
                store(oS[:, tl:tl + 128, :].rearrange("h t e -> t h e"), otm[j % 2].rearrange("p (h e) -> p h e", e=128), [B(("otm", j % 2))], [B(("oS", tl))])
            pst = psf[1]; pbuf = B(("psf", 1))
            if not isc:
                shift_tm(pst, C_MV, 264, pbuf)
            for k in range(KC):
                mm(pst[:, 0:264], lhs(k), win[:, k, C_MV:C_MV + 264], (k == 0) and isc, k == KC - 1, [B("win"), hb], [pbuf])
            j = cnt["vt"]; cnt["vt"] += 1
            cp(DVE, vtm[j % 4], pst[:, 0:256], [pbuf], [B(("vtm", j % 4))])
            store(mvS[:, tg0:tg0 + 128, :].rearrange("h t e -> t h e"), vtm[j % 4].rearrange("p (h e) -> p h e", e=128), [B(("vtm", j % 4))], [B(("mvS", tg0))])
            jg = cnt["gt"]; cnt["gt"] += 1
            cp(DVE, gtm[jg % 2], pst[:, 256:264], [pbuf], [B(("gtm", jg % 2))])
            store(gS[tg0:tg0 + 128, :], gtm[jg % 2], [B(("gtm", jg % 2))], [B(("gS", tg0))])
            pst = psf[2]; pbuf = B(("psf", 2))
            if not isc:
                shift_tm(pst, C_DV, 256, pbuf)
            for k in range(KC):
                mm(pst[:, 0:256], lhs(k), win[:, k, C_DV:C_DV + 256], (k == 0) and isc, k == KC - 1, [B("win"), hb], [pbuf])
            j = cnt["vt"]; cnt["vt"] += 1
            cp(ACT, vtm[j % 4], pst[:, 0:256], [pbuf], [B(("vtm", j % 4))])
            store(dvS[:, tg0:tg0 + 128, :].rearrange("h t e -> t h e"), vtm[j % 4].rearrange("p (h e) -> p h e", e=128), [B(("vtm", j % 4))], [B(("dvS", tg0))])

    stageA1(0)
    stageA2(0)
    for ti in range(len(tiles)):
        if ti + 1 < len(tiles):
            queueA1(ti + 1)
        stageB(ti)
        if ti == 0:
            make_shift_and_fold()
        if ti + 1 < len(tiles):
            stageA2(ti + 1)
    S.barrier()
    apos[0] = p1_mark
    if stage == 1:
        return finish(nc, S, out_d)


    st_q[0] = SP
    p2_mark = apos[0]
    KT = [carve_bf(TA) for _ in range(2)]; QT = [carve_bf(T) for _ in range(2)]
    VV = [carve_bf(NCH * 130).rearrange("p (c e) -> p c e", e=130) for _ in range(2)]
    for h in range(2):
        load(KT[h], dkT[h], [], [B(("KT", h))], q=POOL)
        load(QT[h], dqT[h], [], [B(("QT", h))], q=POOL)
        memset(DVE, VV[h][:, :, 128:130], 1.0, [B(("VV", h))])
        load(VV[h][:, :, 0:128], dvS[h].rearrange("(c s) e -> s c e", s=128), [], [B(("VV", h))], q=POOL)
    Pb = [carve_bf(1024) for _ in range(4)]
    spair = [psp[0], psp[1]]
    gdrow = carve(128)
    load(gdrow, gdr.partition_broadcast(128), [], [B("gdrow")])
    ts(DVE, gdrow, gdrow, 1.0 - LAM_INIT, None, ALU.mult, None, [B("gdrow")], [B("gdrow")])
    mod_alloc()
    mod_next = [2 * D // 512]
    w_up_v = w_up.rearrange("(k p) c -> p k c", p=128)
    w_dn_v = w_down.rearrange("(j p) c -> p j c", p=128)
    pcu = [carve_bf(KC * 256).rearrange("p (k c) -> p k c", c=256) for _ in range(2)]
    pcd = [carve_bf(HC * 128).rearrange("p (j c) -> p j c", c=128) for _ in range(1)]
    pcu_sem = [S.new_dsem(f"pcu{i}") for i in range(2)]
    pcd_sem = [S.new_dsem(f"pcd{i}") for i in range(1)]
    pc_jobs = [("u", j) for j in range(HC)] + [("d", c) for c in range(D // 128)]
    pc_cnt = {"u": 0, "d": 0}

    def precast_one():
        if not pc_jobs:
            return
        kind, idx = pc_jobs.pop(0)
        i_ = pc_cnt[kind] % (2 if kind == "u" else 1); pc_cnt[kind] += 1
        if kind == "u":
            w = pcu[i_]; wb = B(("pcu", i_)); ws = pcu_sem[i_]
            dma(POOL, w[:, :, 0:128], w_up_v[:, :, idx * 128:(idx + 1) * 128], ws, [], [wb])
            dma(POOL, w[:, :, 128:256], w_up_v[:, :, DFF + idx * 128:DFF + (idx + 1) * 128], ws, [], [wb], new_gen=False)
            store(wuS[idx], w.rearrange("p k c -> p (k c)"), [wb], [B(("wuS", idx))], q=SP)
        else:
            w = pcd[i_]; wb = B(("pcd", i_)); ws = pcd_sem[i_]
            dma(POOL, w, w_dn_v[:, :, idx * 128:(idx + 1) * 128], ws, [], [wb])
            store(wdS[idx], w.rearrange("p j c -> p (j c)"), [wb], [B(("wdS", idx))], q=SP)


    acc_bank = [psf[4], psf[5], psp[3][:, 0:512]]

    def acc_ap(idx):
        o = (idx % 3) * 132
        return acc_bank[idx // 3][:, o:o + 129], B(("acc", idx // 3))

    fo = [carve(128) for _ in range(2)]; frr = [carve(4) for _ in range(2)]; fsq = carve(128); fss = [carve(2) for _ in range(2)]
    fdt = [carve_bf(128) for _ in range(2)]; fd = [carve_bf(512) for _ in range(2)]
    fcnt = 0
    for h in range(2):
        for qt in range(T // 512):
            q0 = qt * 512

            def s_step(kb):
                sp = kb % 2; pp = kb % 4
                for m in range(2):
                    mm(spair[sp][:, m * 512:(m + 1) * 512], KT[h][m * 64:(m + 1) * 64, kb * 128:(kb + 1) * 128], QT[h][m * 64:(m + 1) * 64, q0:q0 + 512], True, True,
                       [B(("KT", h)), B(("QT", h))], [B(("psp", sp))])
                act(Pb[pp], spair[sp][:, :], AF.Exp, [B(("psp", sp))], [B(("Pb", pp))], scale=0.125)

            def pv_step(kb):
                pp = kb % 4
                for m in range(2):
                    for qb in range(4):
                        ap_, ab_ = acc_ap(m * 4 + qb)
                        mm(ap_, Pb[pp][:, m * 512 + qb * 128:m * 512 + (qb + 1) * 128], VV[h][:, kb, 0:129], (kb == 0) and ((m * 4 + qb) % 3 == 0), kb == NCH - 1, [B(("VV", h)), B(("Pb", pp))], [ab_])

            for kb in range(NCH):
                if kb % 33 == 5:
                    precast_one()
                s_step(kb)
                if kb > 1:
                    pv_step(kb - 2)
            pv_step(NCH - 2)
            pv_step(NCH - 1)
            if mod_next[0] < 6 * D // 512:
                mod_tile(mod_next[0], psf[3], B(("psp", 1)))
                mod_next[0] += 1
            jf = fcnt % 2; fcnt += 1
            for qb in range(4):
                i2 = qb % 2
                a0, ab0 = acc_ap(qb); a1, ab1 = acc_ap(4 + qb)
                rr = frr[i2]; rb = B(("frr", i2))
                S.op(DVE, (lambda rr, a0: lambda e: e.reciprocal(out=rr[:, 0:1], in_=a0[:, 128:129]))(rr, a0), reads=[ab0], writes=[rb])
                S.op(DVE, (lambda rr, a1: lambda e: e.reciprocal(out=rr[:, 1:2], in_=a1[:, 128:129]))(rr, a1), reads=[ab1], writes=[rb])
                tt(DVE, rr[:, 2:3], rr[:, 1:2], lamneg[:, 0:1], ALU.mult, [rb, B("lamneg")], [rb])
                act(fo[i2], a0[:, 0:128], AF.Copy, [ab0, rb], [B(("fo", i2))], scale=rr[:, 0:1])
                stt(DVE, fo[i2], a1[:, 0:128], rr[:, 2:3], fo[i2], ALU.mult, ALU.add, [ab1, rb, B(("fo", i2))], [B(("fo", i2))])
                sq_ = fss[i2]; sqb = B(("fss", i2))
                memset(DVE, sq_[:, 0:1], 0.0, [sqb])
                act(fsq, fo[i2], AF.Square, [B(("fo", i2)), sqb], [B("fsq"), sqb], accum=sq_[:, 0:1])
                act(sq_[:, 1:2], sq_[:, 0:1], AF.Ln, [sqb, B("epsc")], [sqb], scale=1.0 / 128, bias=epsc)
                act(sq_[:, 1:2], sq_[:, 1:2], AF.Exp, [sqb], [sqb], scale=-0.5)
                stt(DVE, fdt[i2], fo[i2], sq_[:, 1:2], gdrow, ALU.mult, ALU.mult, [B(("fo", i2)), sqb, B("gdrow")], [B(("fdt", i2))])
                tp(psb[1][:, i2 * 128:(i2 + 1) * 128], fdt[i2], identb, [B(("fdt", i2)), B("identb")], [B(("psbt", i2))])
                cp(ACT, fd[jf][:, qb * 128:(qb + 1) * 128], psb[1][:, i2 * 128:(i2 + 1) * 128], [B(("psbt", i2))], [B(("fd", jf))])
            store(mixL[q0 // 1024][256 + h * 128:256 + (h + 1) * 128, q0 % 1024:q0 % 1024 + 512], fd[jf], [B(("fd", jf))], [B(("mixd", h, qt))])
    while pc_jobs:
        precast_one()
    S.barrier()
    apos[0] = p2_mark
    if stage == 2:
        return finish(nc, S, out_d)

    st_q[0] = POOL
    Gz = carve(NCH * 8).rearrange("p (c g) -> p c g", g=8)
    bg = carve(NCH * 8).rearrange("p (c g) -> p c g", g=8)
    load(Gz, gS.rearrange("(c s) g -> s c g", s=128), [], [B("Gz")])
    load(bg, bgate.partition_broadcast(128).rearrange("p o (c g) -> p (o c) g", g=8), [], [B("bg")])
    tt(DVE, Gz, Gz, bg, ALU.add, [B("Gz"), B("bg")], [B("Gz")])
    NS = 4 * NCH
    Lg = carve(NS).rearrange("p (s c) -> p s c", c=NCH)
    for st in range(4):
        fcol = (st // 2) * 4 + 2 + (st % 2)
        act(Lg[:, st, :], Gz[:, :, fcol], AF.Exp, [B("Gz")], [B("Lg")], scale=-1.0)
    act(Lg, Lg, AF.Ln, [B("Lg")], [B("Lg")], bias=1.0)
    Lg2 = Lg.rearrange("p s c -> p (s c)")
    mm(psf[0][:, 0:2 * NCH], triF, Lg2[:, 0:2 * NCH], True, True, [B("triF"), B("Lg")], [B(("psf", 0))])
    mm(psf[0][:, 2 * NCH:NS], triB, Lg2[:, 2 * NCH:NS], True, True, [B("triB"), B("Lg")], [B(("psf", 0))])
    mm(psf[1][:, 0:NS], onesf, Lg2, True, True, [B("onesf"), B("Lg")], [B(("psf", 1))])
    cL = carve(NS).rearrange("p (s c) -> p s c", c=NCH)
    gg = carve(NS).rearrange("p (s c) -> p s c", c=NCH)
    cp(DVE, cL.rearrange("p s c -> p (s c)"), psf[0][:, 0:NS], [B(("psf", 0))], [B("cL")])
    for st in range(4):
        icol = (st // 2) * 4 + (st % 2)
        tt(DVE, gg[:, st, :], Gz[:, :, icol], cL[:, st, :], ALU.add, [B("Gz"), B("cL")], [B("gg")])
    gg2 = gg.rearrange("p s c -> p (s c)")
    Grow = carve(NS); Brow = carve(NS); gmax = carve(4)
    ts(DVE, Brow[0:1, :], psf[1][0:1, 0:NS], -1.0, None, ALU.mult, None, [B(("psf", 1))], [B("Brow")])
    for gi, (a0, a1) in enumerate([(0, 128), (128, 256), (256, NS)]):
        n = a1 - a0
        S.op(PE, (lambda a0, a1, n: lambda e: e.transpose(out=psf[2][0:n, 0:128], in_=gg2[:, a0:a1], identity=identf))(a0, a1, n), reads=[B("gg"), B("identf")], writes=[B(("psf", 2))])
        S.op(DVE, (lambda n, gi: lambda e: e.reduce_max(out=gmax[0:n, gi:gi + 1], in_=psf[2][0:n, 0:128], axis=AX.X))(n, gi), reads=[B(("psf", 2))], writes=[B("gmax")])
        S.op(PE, (lambda a0, a1, n, gi: lambda e: e.transpose(out=psf[3][0:1, a0:a1], in_=gmax[0:n, gi:gi + 1], identity=identf[0:n, 0:n]))(a0, a1, n, gi), reads=[B("gmax"), B("identf")], writes=[B(("psf", 3))])
    cp(DVE, Grow[0:1, :], psf[3][0:1, 0:NS], [B(("psf", 3))], [B("Grow")])
    Gp = carve(NS); Bp = carve(NS); mnx = carve(NS); min_ = carve(NS); Mp = carve(NS); Wp = carve(NS); Mrow = carve(NS); Wrow = carve(NS)

    def rowv(t, st):
        return t[0:1, st * NCH:(st + 1) * NCH]

    for st in range(4):
        for (src, dst, bn) in [(Grow, Gp, "Gp"), (Brow, Bp, "Bp")]:
            if st < 2:
                cp(DVE, rowv(dst, st), rowv(src, st), [B("Grow"), B("Brow")], [B(bn)])
            else:
                cp(DVE, rowv(dst, st)[:, 0:2], rowv(src, st)[:, 1::-1], [B("Grow"), B("Brow")], [B(bn)])
                cp(DVE, rowv(dst, st)[:, 2:NCH], rowv(src, st)[:, NCH - 1:1:-1], [B("Grow"), B("Brow")], [B(bn)])
        S.op(DVE, (lambda st: lambda e: e.tensor_tensor_scan(out=rowv(mnx, st), data0=rowv(Gp, st), data1=rowv(Bp, st), initial=0.0, op0=ALU.max, op1=ALU.add))(st),
             reads=[B("Gp"), B("Bp")], writes=[B("mnx")])
        memset(DVE, rowv(min_, st)[:, 0:1], 0.0, [B("min")])
        cp(DVE, rowv(min_, st)[:, 1:NCH], rowv(mnx, st)[:, 0:NCH - 1], [B("mnx")], [B("min")])
    tt(DVE, Mp[0:1, :], min_[0:1, :], Gp[0:1, :], ALU.max, [B("min"), B("Gp")], [B("Mp")])
    tt(DVE, Wp[0:1, :], min_[0:1, :], Mp[0:1, :], ALU.subtract, [B("min"), B("Mp")], [B("Wp")])
    act(Wp[0:1, :], Wp[0:1, :], AF.Exp, [B("Wp")], [B("Wp")])
    for st in range(4):
        for (src, dst, bn, sb) in [(Mp, Mrow, "Mrow", "Mp"), (Wp, Wrow, "Wrow", "Wp")]:
            if st < 2:
                cp(DVE, rowv(dst, st), rowv(src, st), [B(sb)], [B(bn)])
            else:
                cp(DVE, rowv(dst, st)[:, 0:2], rowv(src, st)[:, 1::-1], [B(sb)], [B(bn)])
                cp(DVE, rowv(dst, st)[:, 2:NCH], rowv(src, st)[:, NCH - 1:1:-1], [B(sb)], [B(bn)])
    mm(psf[4][:, 0:NS], onesf[0:1, :], Mrow[0:1, :], True, True, [B("onesf"), B("Mrow")], [B(("psf", 4))])
    mm(psf[5][:, 0:NS], onesf[0:1, :], Wrow[0:1, :], True, True, [B("onesf"), B("Wrow")], [B(("psf", 5))])
    Wrep = carve(NS).rearrange("p (s c) -> p s c", c=NCH)
    wcol = carve(NS).rearrange("p (s c) -> p s c", c=NCH)
    clamp = carve(NS).rearrange("p (s c) -> p s c", c=NCH)
    cp(DVE, Wrep.rearrange("p s c -> p (s c)"), psf[5][:, 0:NS], [B(("psf", 5))], [B("Wrep")])
    tt(DVE, wcol.rearrange("p s c -> p (s c)"), gg2, psf[4][:, 0:NS], ALU.subtract, [B("gg"), B(("psf", 4))], [B("wcol")])
    act(wcol, wcol, AF.Exp, [B("wcol")], [B("wcol")])
    tt(DVE, clamp.rearrange("p s c -> p (s c)"), cL.rearrange("p s c -> p (s c)"), psf[4][:, 0:NS], ALU.subtract, [B("cL"), B(("psf", 4))], [B("clamp")])
    act(clamp, clamp, AF.Exp, [B("clamp")], [B("clamp")])
    qTl = [carve_bf(T + 1) for _ in range(2)]; kTl = [carve_bf(T + 1) for _ in range(2)]; kTc = [carve_bf(TC + 1) for _ in range(2)]
    vaug = [carve_bf(NCH * 130).rearrange("p (c e) -> p c e", e=130) for _ in range(2)]
    ktok = [carve_bf(NCH * 128).rearrange("p (c e) -> p c e", e=128) for _ in range(2)]
    gmlr = carve(256)
    load(gmlr, gml.partition_broadcast(128), [], [B("gmlr")])
    for h in range(2):
        load(qTl[h], mqT_l[h], [], [B(("qTl", h))], q=POOL)
        load(kTl[h], mkT_l[h], [], [B(("kTl", h))], q=POOL)
        load(kTc[h], mkT_c[h], [], [B(("kTc", h))], q=POOL)
        memset(DVE, vaug[h][:, :, 128:130], 1.0, [B(("vaug", h))])
        load(vaug[h][:, :, 0:128], mvS[h].rearrange("(c s) e -> s c e", s=128), [], [B(("vaug", h))], q=POOL)

    def kchunk(h, cs):
        return kTc[h][:, 1 + cs * 128:1 + (cs + 1) * 128] if cs < 2 else kTl[h][:, 1 + (cs - 2) * 128:1 + (cs - 1) * 128]

    for h in range(2):
        for cs in range(NCH):
            pb = psb[cs % 2]
            tp(pb[:, 0:128], kchunk(h, cs), identb, [B(("kTl", h)), B(("kTc", h)), B("identb")], [B(("psb", cs % 2))])
            cp(ACT if cs % 2 == 0 else DVE, ktok[h][:, cs, :], pb[:, 0:128], [B(("psb", cs % 2))], [B(("ktok", h))])
    Cn = [carve(130) for _ in range(4)]; Cnb = [carve_bf(130) for _ in range(4)]
    NB = 8
    STb = [carve_bf(128) for _ in range(NB)]; kwb = [carve_bf(128) for _ in range(NB)]
    dena = [carve(2) for _ in range(NB)]; hbuf = [carve(128) for _ in range(NB)]; hfb = [carve(128) for _ in range(NB)]; obuf = [carve(128) for _ in range(NB)]
    hsq = carve(128); hss = [carve(2) for _ in range(NB)]; mtok = [carve_bf(128) for _ in range(NB)]; mTb = [carve_bf(128) for _ in range(NB)]
    perm66 = [1, 0] + [NCH + 1 - j for j in range(2, NCH)]
    pos_of = [{cs: cs for cs in range(NCH)}, {perm66[j]: j for j in range(NCH)}]
    for st in range(4):
        memset(DVE, Cn[st], 0.0, [B(("Cn", st))])
        memset(DVE, Cnb[st], 0.0, [B(("Cnb", st))])
    steps = []
    for j in range(NCH):
        for dr in range(2):
            cs = j if dr == 0 else perm66[j]
            for h in range(2):
                n_ = len(steps)
                steps.append(dict(j=j, dr=dr, cs=cs, h=h, st=dr * 2 + h, i2=n_ % 2, i4=n_ % NB, lat=cs >= 2, tl=(cs - 2) * 128,
                                  second=(cs >= 2 and pos_of[1 - dr][cs] < j)))

    def stepA(c):
        if not c["lat"]:
            return
        h, cs, st, i2, i4, tl = c["h"], c["cs"], c["st"], c["i2"], c["i4"], c["tl"]
        mask = maskF if c["dr"] == 0 else maskB
        mkb = B("maskF") if c["dr"] == 0 else B("maskB")
        qch = qTl[h][:, 1 + tl:1 + tl + 128]
        pq = psf[i2]
        mm(pq[:, 0:128], kchunk(h, cs), qch, True, True, [B(("kTl", h)), B(("qTl", h))], [B(("psf", i2))])
        stt(DVE, STb[i4], pq[:, 0:128], wcol[:, st, cs:cs + 1], mask, ALU.mult, ALU.mult, [B(("psf", i2)), B("wcol"), mkb], [B(("STb", i4))])

    def stepB(c):
        h, cs, st, i2, i4, tl, j, dr = c["h"], c["cs"], c["st"], c["i2"], c["i4"], c["tl"], c["j"], c["dr"]
        if c["lat"] and c["second"]:
            load(hfb[i4], hfS[h, tl:tl + 128, :], [B(("hfS", h, tl))], [B(("hfb", i4))])
            load(obuf[i4], oS[h, tl:tl + 128, :], [], [B(("obuf", i4))])
        if c["lat"]:
            qch = qTl[h][:, 1 + tl:1 + tl + 128]
            pn = psf[2 + i2]
            mm(pn[:, 0:129], STb[i4], vaug[h][:, cs, 0:129], True, False, [B(("STb", i4)), B(("vaug", h))], [B(("psf", 2 + i2))])
            mm(pn[:, 0:129], qch, Cnb[st][:, 0:129], False, True, [B(("qTl", h)), B(("Cnb", st))], [B(("psf", 2 + i2))])
            da = dena[i4]
            act(da[:, 0:1], pn[:, 128:129], AF.Abs, [B(("psf", 2 + i2))], [B(("dena", i4))])
            tt(DVE, da[:, 0:1], da[:, 0:1], clamp[:, st, cs:cs + 1], ALU.max, [B(("dena", i4)), B("clamp")], [B(("dena", i4))])
            S.op(DVE, (lambda da: lambda e: e.reciprocal(out=da[:, 1:2], in_=da[:, 0:1]))(da), reads=[B(("dena", i4))], writes=[B(("dena", i4))])
            act(hbuf[i4], pn[:, 0:128], AF.Copy, [B(("psf", 2 + i2)), B(("dena", i4))], [B(("hbuf", i4))], scale=da[:, 1:2])
            if not c["second"]:
                store(hfS[h, tl:tl + 128, :], hbuf[i4], [B(("hbuf", i4))], [B(("hfS", h, tl))])
        if j < NCH - 1:
            pu = psf[4 + i2]
            ts(DVE, kwb[i4], ktok[h][:, cs, :], wcol[:, st, cs:cs + 1], None, ALU.mult, None, [B(("ktok", h)), B("wcol")], [B(("kwb", i4))])
            mm(pu[:, 0:129], kwb[i4], vaug[h][:, cs, 0:129], True, True, [B(("kwb", i4)), B(("vaug", h))], [B(("psf", 4 + i2))])
            stt(DVE, Cn[st][:, 0:129], Cn[st][:, 0:129], Wrep[:, st, cs:cs + 1], pu[:, 0:129], ALU.mult, ALU.add,
                [B(("Cn", st)), B("Wrep"), B(("psf", 4 + i2))], [B(("Cn", st))])
            csn = (j + 1) if dr == 0 else perm66[j + 1]
            act(Cnb[st][:, 0:129], Cn[st][:, 0:129], AF.Copy, [B(("Cn", st)), B("Wrep")], [B(("Cnb", st))], scale=Wrep[:, st, csn:csn + 1])

    def stepC(c):
        if not (c["lat"] and c["second"]):
            return
        h, i2, i4, tl = c["h"], c["i2"], c["i4"], c["tl"]
        hs_ = hss[i4]
        tt(DVE, hbuf[i4], hbuf[i4], hfb[i4], ALU.add, [B(("hbuf", i4)), B(("hfb", i4))], [B(("hbuf", i4))])
        tt(DVE, obuf[i4], obuf[i4], gmlr[:, h * 128:(h + 1) * 128], ALU.mult, [B(("obuf", i4)), B("gmlr")], [B(("obuf", i4))])
        memset(DVE, hs_[:, 0:1], 0.0, [B(("hss", i4))])
        act(hsq, hbuf[i4], AF.Square, [B(("hbuf", i4)), B(("hss", i4))], [B("hsq"), B(("hss", i4))], accum=hs_[:, 0:1])
        act(hs_[:, 1:2], hs_[:, 0:1], AF.Ln, [B(("hss", i4)), B("epsc")], [B(("hss", i4))], scale=1.0 / 128, bias=epsc)
        act(hs_[:, 1:2], hs_[:, 1:2], AF.Exp, [B(("hss", i4))], [B(("hss", i4))], scale=-0.5)
        stt(DVE, mtok[i4], hbuf[i4], hs_[:, 1:2], obuf[i4], ALU.mult, ALU.mult, [B(("hbuf", i4)), B(("hss", i4)), B(("obuf", i4))], [B(("mtok", i4))])
        tp(psb[i2][:, 0:128], mtok[i4], identb, [B(("mtok", i4)), B("identb")], [B(("psb", i2))])
        cp(ACT, mTb[i4], psb[i2][:, 0:128], [B(("psb", i2))], [B(("mTb", i4))])
        store(mixL[tl // 1024][h * 128:(h + 1) * 128, tl % 1024:tl % 1024 + 128], mTb[i4], [B(("mTb", i4))], [B(("mixm", h, tl))])

    NSTEP = len(steps)
    LA, LC = 2, 5
    for n_ in range(NSTEP + LC):
        if n_ < NSTEP:
            stepA(steps[n_])
        if 0 <= n_ - LA < NSTEP:
            stepB(steps[n_ - LA])
        if 0 <= n_ - LC < NSTEP:
            stepC(steps[n_ - LC])
    S.barrier()
    apos[0] = p2_mark
    if stage == 3:
        return finish(nc, S, out_d)

    ccs = S.new_dsem("cc")
    for p in range(8):
        S.dma(POOL, (lambda p: lambda e: e.collective_compute("AllGather", ALU.bypass, replica_groups=[[0, 1, 2, 3], [4, 5, 6, 7]], ins=[mixL[p]], outs=[mixA[p]]))(p),
              ccs, reads=[], writes=[B(("mixA", p))], inc=1)

    st_q[0] = POOL
    ga1g = carve(D); ga2g = carve(D)
    mq_mark = apos[0]
    mixq = carve_bf(KC * (TQ + 2)).rearrange("p (k t) -> p k t", t=TQ + 2)
    p5_mark = apos[0]
    v96b = carve(128)
    for j, c0 in enumerate([3 * D, 4 * D]):
        load(v96b[j * 16:(j + 1) * 16, :], modrow[0:1, c0:c0 + D].rearrange("o (k p) -> (o k) p", p=128), [], [B("v96b")])
    S.op(PE, lambda e: e.transpose(out=psf[2][:, 0:32], in_=v96b[0:32, :], identity=identf[0:32, 0:32]), reads=[B("v96b"), B("identf")], writes=[B(("psf", 2))])
    cp(DVE, fmv[:, 64:96], psf[2][:, 0:32], [B(("psf", 2))], [B("fmv")])
    stt(DVE, A2, SC2, 1.0, gTs[:, 1, :], ALU.add, ALU.mult, [B("fmv"), B("gTs")], [B("A2")])
    grt = carve(D)
    for (dst, c0, row, nm) in [(ga1g, 2 * D, 0, "ga1g"), (ga2g, 5 * D, 1, "ga2g")]:
        load(dst, modrow[0:1, c0:c0 + D].partition_broadcast(128), [], [B(nm)])
        load(grt, grow[row:row + 1, :].partition_broadcast(128), [], [B("grt")])
        tt(DVE, dst, dst, grt, ALU.mult, [B(nm), B("grt")], [B(nm)])
    tmpq = [carve_bf(4 * (TQ + 2)).rearrange("p (k t) -> p k t", t=TQ + 2) for _ in range(2)]
    mall_v = [m_.rearrange("(k p) t -> p k t", p=128) for m_ in mixA]
    i = 0
    for qq in range(4):
        for kg in range(4):
            dstv = mixq[:, kg * 4:(kg + 1) * 4, :]
            tq_ = tmpq[i % 2]; tqb = B(("tmpq", i % 2)); ks = slice(kg * 4, (kg + 1) * 4)
            load(tq_[:, :, 1:1025], mall_v[2 * qq][:, ks, :], [B(("mixA", 2 * qq))], [tqb])
            load(tq_[:, :, 1025:2049], mall_v[2 * qq + 1][:, ks, :], [B(("mixA", 2 * qq + 1))], [tqb])
            pl, cl = (2 * qq - 1, 1023) if qq >= 1 else (0, 0)
            load(tq_[:, :, 0:1], mall_v[pl][:, ks, cl:cl + 1], [B(("mixA", pl))], [tqb])
            pr, cr = (2 * qq + 2, 0) if qq <= 2 else (7, 1023)
            load(tq_[:, :, 2049:2050], mall_v[pr][:, ks, cr:cr + 1], [B(("mixA", pr))], [tqb])
            if qq == 0:
                ts(DVE, dstv, tq_, qsel[:, 0:1], None, ALU.mult, None, [tqb, B("qsel")], [B(("mixq", kg))])
            else:
                stt(DVE, dstv, tq_, qsel[:, qq:qq + 1], dstv, ALU.mult, ALU.add, [tqb, B("qsel"), B(("mixq", kg))], [B(("mixq", kg))])
            i += 1
    S.barrier()
    apos[0] = p5_mark
    mixh = carve_bf(KC * 2).rearrange("p (k t) -> p k t", t=2)
    cp(DVE, mixh[:, :, 0:1], mixq[:, :, 0:1], [B("mixq")], [B("mixh")])
    cp(DVE, mixh[:, :, 1:2], mixq[:, :, TQ + 1:TQ + 2], [B("mixq")], [B("mixh")])
    wo = carve_bf(KC * D).rearrange("p (k c) -> p k c", c=D)
    wo_sem = S.new_dsem("wo")
    w_out_v = w_out.rearrange("(k p) c -> p k c", p=128)
    for k in range(KC):
        dma(POOL, wo[:, k, :], w_out_v[:, k, :], wo_sem, [], [B("wo")], new_gen=(k == 0))
    xt = [carve(D) for _ in range(2)]; xm = carve(D); ytmp = [carve(512) for _ in range(2)]; junk = carve_bf(D)
    xn2 = carve_bf(D); h2blk = [carve_bf(KC * 128).rearrange("p (k t) -> p k t", t=128) for _ in range(2)]
    ssq = carve(8); s1 = carve(4)
    h2S_v = h2S.rearrange("(k p) t -> p k t", p=128)
    for tb in range(17):
        nt = 128 if tb < 16 else 2
        x_ = xt[tb % 2]; xb_ = B(("xt", tb % 2))
        if tb < 16:
            load(x_, xq[1 + tb * 128:1 + (tb + 1) * 128, :], [], [xb_])
        else:
            load(x_[0:1, :], xq[0:1, :], [], [xb_])
            load(x_[1:2, :], xq[TQ + 1:TQ + 2, :], [], [xb_])
        for ct in range(4):
            for k in range(KC):
                lhs = mixq[:, k, 1 + tb * 128:1 + (tb + 1) * 128] if tb < 16 else mixh[:, k, :]
                mm(psf[ct][0:nt, :], lhs, wo[:, k, ct * 512:(ct + 1) * 512], k == 0, k == KC - 1, [B("mixq"), B("mixh"), B("wo")], [B(("psf", ct))])
        memset(DVE, ssq[:, 0:4], 0.0, [B("ssq")])
        for ct in range(4):
            act(junk[0:nt, 0:512], psf[ct][0:nt, :], AF.Square, [B(("psf", ct)), B("ssq")], [B("junk"), B("ssq")], accum=ssq[0:nt, ct:ct + 1])
        S.op(DVE, (lambda nt: lambda e: e.reduce_sum(out=s1[0:nt, 0:1], in_=ssq[0:nt, 0:4], axis=AX.X))(nt), reads=[B("ssq")], writes=[B("s1")])
        act(s1[0:nt, 1:2], s1[0:nt, 0:1], AF.Ln, [B("s1"), B("epsc")], [B("s1")], scale=1.0 / D, bias=epsc[0:nt, :])
        act(s1[0:nt, 1:2], s1[0:nt, 1:2], AF.Exp, [B("s1")], [B("s1")], scale=-0.5)
        for ct in range(4):
            yt = ytmp[ct % 2]; yb = B(("ytmp", ct % 2))
            stt(DVE, yt[0:nt, :], psf[ct][0:nt, :], s1[0:nt, 1:2], ga1g[0:nt, ct * 512:(ct + 1) * 512], ALU.mult, ALU.mult, [B(("psf", ct)), B("s1"), B("ga1g")], [yb])
            tt(DVE, xm[0:nt, ct * 512:(ct + 1) * 512], yt[0:nt, :], x_[0:nt, ct * 512:(ct + 1) * 512], ALU.add, [yb, xb_], [B("xm")])
        if tb < 16:
            store(xmidS[tb * 128:(tb + 1) * 128, :], xm, [B("xm")], [B(("xmidS", tb))])
        memset(DVE, ssq[:, 4:5], 0.0, [B("ssq")])
        act(junk[0:nt, :], xm[0:nt, :], AF.Square, [B("xm"), B("ssq")], [B("junk"), B("ssq")], accum=ssq[0:nt, 4:5])
        act(s1[0:nt, 2:3], ssq[0:nt, 4:5], AF.Ln, [B("ssq"), B("epsc")], [B("s1")], scale=1.0 / D, bias=epsc[0:nt, :])
        act(s1[0:nt, 2:3], s1[0:nt, 2:3], AF.Exp, [B("s1")], [B("s1")], scale=-0.5)
        ts(DVE, xn2[0:nt, :], xm[0:nt, :], s1[0:nt, 2:3], None, ALU.mult, None, [B("xm"), B("s1")], [B("xn2")])
        hb_ = h2blk[tb % 2]; hbb = B(("h2blk", tb % 2))
        for g in range(2):
            pb = psb[g]
            for kk in range(8):
                k = g * 8 + kk
                S.op(PE, (lambda pb, kk, k, nt: lambda e: e.transpose(out=pb[:, kk * 128:kk * 128 + nt], in_=xn2[0:nt, k * 128:(k + 1) * 128], identity=identb[0:nt, 0:nt]))(pb, kk, k, nt),
                     reads=[B("xn2"), B("identb")], writes=[B(("psb", g))])
            for kk in range(8):
                k = g * 8 + kk
                act(hb_[:, k, 0:nt], pb[:, kk * 128:kk * 128 + nt], AF.Identity, [B(("psb", g)), B("A2"), B("fmv")], [hbb], scale=A2[:, k:k + 1], bias=SH2[:, k:k + 1])
        if tb < 16:
            store(h2S_v[:, :, 1 + tb * 128:1 + (tb + 1) * 128], hb_, [hbb], [B(("h2S", tb))])
        else:
            ts(DVE, hb_[:, :, 0:1], hb_[:, :, 0:1], hmask[:, 0:1], None, ALU.mult, None, [hbb, B("hmask")], [hbb])
            ts(DVE, hb_[:, :, 1:2], hb_[:, :, 1:2], hmask[:, 1:2], None, ALU.mult, None, [hbb, B("hmask")], [hbb])
            store(h2S_v[:, :, 0:1], hb_[:, :, 0:1], [hbb], [B(("h2S", 16))])
            store(h2S_v[:, :, TQ + 1:TQ + 2], hb_[:, :, 1:2], [hbb], [B(("h2S", 17))])
    S.barrier()
    apos[0] = mq_mark
    if stage == 4:
        return finish(nc, S, out_d)
    st_q[0] = SP
    cfs = carve(2 * HC * 4).rearrange("p (c j) -> p c j", j=4)
    load(cfs, cfw, [], [B("cfs")])
    h2t = [carve_bf(KC * 514).rearrange("p (k t) -> p k t", t=514) for _ in range(2)]
    gTt = carve_bf(HC * 512).rearrange("p (j t) -> p j t", t=512)
    wu = [carve_bf(KC * 256).rearrange("p (k c) -> p k c", c=256) for _ in range(2)]
    wu_sem = [S.new_dsem(f"wu{i}") for i in range(2)]
    wd = [carve_bf(HC * 128).rearrange("p (j c) -> p j c", c=128) for _ in range(2)]
    wd_sem = [S.new_dsem(f"wd{i}") for i in range(2)]
    wuc_sem = [S.new_dsem(f"wuc{i}") for i in range(2)]
    wdc_sem = [S.new_dsem(f"wdc{i}") for i in range(2)]
    y2 = [carve(D) for _ in range(4)]
    ub = [carve(520) for _ in range(2)]; tb_ = [carve(512) for _ in range(2)]; sgb = carve(512)
    xmr = carve(D)
    junkc = carve_bf(D); ssqc = carve(8); s1c = carve(4)
    w_up_v = w_up.rearrange("(k p) c -> p k c", p=128)
    w_dn_v = w_down.rearrange("(j p) c -> p j c", p=128)
    wi = 0; di = 0
    for tt_ in range(4):
        h2 = h2t[tt_ % 2]; h2b = B(("h2t", tt_ % 2))
        load(h2, h2S_v[:, :, tt_ * 512:tt_ * 512 + 514], [B(("h2S", x)) for x in range(18)], [h2b])
        for j in range(HC):
            w = wu[wi % 2]; wb = B(("wu", wi % 2)); ws = wu_sem[wi % 2]; wi += 1
            dma(SP, w.rearrange("p k c -> p (k c)"), wuS[j], wuc_sem[(wi - 1) % 2], [], [wb])
            for part in range(2):
                pa = psf[part * 2]; pk = psf[part * 2 + 1]
                for k in range(KC):
                    mm(pa[:, 0:512], w[:, k, part * 128:(part + 1) * 128], h2[:, k, 0:512], k == 0, k == KC - 1, [wb, h2b], [B(("psf", part * 2))])
                for k in range(KC):
                    mm(pk[:, 0:2], w[:, k, part * 128:(part + 1) * 128], h2[:, k, 512:514], k == 0, k == KC - 1, [wb, h2b], [B(("psf", part * 2 + 1))])
                u = ub[part]; ubb = B(("ub", part))
                cp(ACT, u[:, 0:512], pa[:, 0:512], [B(("psf", part * 2))], [ubb])
                cp(ACT, u[:, 512:514], pk[:, 0:2], [B(("psf", part * 2 + 1))], [ubb])
                cw = cfs[:, part * HC + j, :]
                t_ = tb_[part]; tbb = B(("tb", part))
                ts(DVE, t_, u[:, 1:513], cw[:, 1:2], cw[:, 3:4], ALU.mult, ALU.add, [ubb, B("cfs")], [tbb])
                stt(DVE, t_, u[:, 0:512], cw[:, 0:1], t_, ALU.mult, ALU.add, [ubb, B("cfs"), tbb], [tbb])
                stt(DVE, t_, u[:, 2:514], cw[:, 2:3], t_, ALU.mult, ALU.add, [ubb, B("cfs"), tbb], [tbb])
            act(sgb, tb_[0], AF.Silu, [B(("tb", 0))], [B("sgb")])
            tt(DVE, gTt[:, j, :], sgb, tb_[1], ALU.mult, [B("sgb"), B(("tb", 1))], [B("gTt")])
            if stage == 6:
                d_h2 = nc.dram_tensor("d_h2", [128, KC, 514], BF16, kind="ExternalOutput").ap()
                d_wu = nc.dram_tensor("d_wu", [128, KC, 256], BF16, kind="ExternalOutput").ap()
                d_ub = nc.dram_tensor("d_ub", [2, 128, 514], F32, kind="ExternalOutput").ap()
                d_tb = nc.dram_tensor("d_tb", [3, 128, 512], F32, kind="ExternalOutput").ap()
                d_cf = nc.dram_tensor("d_cf", [128, 2 * HC, 4], F32, kind="ExternalOutput").ap()
                store(d_h2, h2, [h2b], [B("d1")]); store(d_wu, w, [wb], [B("d2")])
                store(d_ub[0], ub[0][:, 0:514], [B(("ub", 0))], [B("d3")]); store(d_ub[1], ub[1][:, 0:514], [B(("ub", 1))], [B("d4")])
                store(d_tb[0], tb_[0], [B(("tb", 0))], [B("d5")]); store(d_tb[1], tb_[1], [B(("tb", 1))], [B("d6")]); store(d_tb[2], sgb, [B("sgb")], [B("d7")])
                store(d_cf, cfs, [B("cfs")], [B("d8")])
                S.barrier()
                return finish(nc, S, out_d)
        if stage == 5 and tt_ == 0:
            gdbg = nc.dram_tensor("gdbg", [128, HC, 512], BF16, kind="ExternalOutput").ap()
            store(gdbg, gTt, [B("gTt")], [B("gdbg")])
        for cth in range(D // 128):
            wdd = wd[di % 2]; wdb = B(("wd", di % 2)); wds = wd_sem[di % 2]; di += 1
            dma(SP, wdd.rearrange("p j c -> p (j c)"), wdS[cth], wdc_sem[(di - 1) % 2], [], [wdb])
            for blk in range(4):
                pi = (cth * 4 + blk) % 4
                for j in range(HC):
                    mm(psf[pi][:, 0:128], gTt[:, j, blk * 128:(blk + 1) * 128], wdd[:, j, :], j == 0, j == HC - 1, [B("gTt"), wdb], [B(("psf", pi))])
                cp(ACT if blk % 2 == 0 else DVE, y2[blk][:, cth * 128:(cth + 1) * 128], psf[pi][:, 0:128], [B(("psf", pi))], [B(("y2", blk))])
        if stage == 5 and tt_ == 0:
            ydbg = nc.dram_tensor("ydbg", [4, 128, D], F32, kind="ExternalOutput").ap()
            for blk in range(4):
                store(ydbg[blk], y2[blk], [B(("y2", blk))], [B(("ydbg", blk))])
            S.barrier()
            return finish(nc, S, out_d)
        for blk in range(4):
            r0 = tt_ * 512 + blk * 128
            load(xmr, xmidS[r0:r0 + 128, :], [B(("xmidS", r0 // 128))], [B("xmr")])
            memset(DVE, ssqc[:, 5:6], 0.0, [B("ssq")])
            act(junkc, y2[blk], AF.Square, [B(("y2", blk)), B("ssq")], [B("junk"), B("ssq")], accum=ssqc[:, 5:6])
            act(s1c[:, 3:4], ssqc[:, 5:6], AF.Ln, [B("ssq"), B("epsc")], [B("s1")], scale=1.0 / D, bias=epsc)
            act(s1c[:, 3:4], s1c[:, 3:4], AF.Exp, [B("s1")], [B("s1")], scale=-0.5)
            stt(DVE, y2[blk], y2[blk], s1c[:, 3:4], ga2g, ALU.mult, ALU.mult, [B(("y2", blk)), B("s1"), B("ga2g")], [B(("y2", blk))])
            tt(POOL, y2[blk], y2[blk], xmr, ALU.add, [B(("y2", blk)), B("xmr")], [B(("y2", blk))])
            store(out_d[r0:r0 + 128, :], y2[blk], [B(("y2", blk))], [B(("out", r0))])
    S.barrier()
    return finish(nc, S, out_d)


def finish(nc, S, out_d):
    S.finalize()
    return nc


_PROG = {}


def _consts():
    c = {}
    c["identf"] = np.eye(128, dtype=np.float32)
    s = np.arange(128)
    c["maskF"] = (s[:, None] <= s[None, :]).astype(np.float32)
    c["maskB"] = (s[:, None] >= s[None, :]).astype(np.float32)
    c["triF"] = (s[:, None] <= s[None, :]).astype(np.float32)
    c["triB"] = (s[:, None] >= s[None, :]).astype(np.float32)
    partner = np.where((s % 32) < 16, s + 16, s - 16)
    pm = np.zeros((128, 128), np.float32)
    pm[partner, s] = 1.0
    c["perm"] = pm
    perm66 = np.array([1, 0] + [NCH + 1 - j for j in range(2, NCH)])
    jb = np.zeros((NCH, NCH), np.float32)
    jb[perm66, np.arange(NCH)] = 1.0
    c["jb"] = jb
    rows = T // 64
    row = np.repeat(np.arange(rows, dtype=np.float32), 64)
    col = np.tile(np.arange(64, dtype=np.float32), rows)
    inv = (10000.0 ** (-np.arange(16, dtype=np.float32) / 16)).astype(np.float32)
    d = np.arange(64)
    axis = d // 32; half = (d // 16) % 2; f = d % 16
    pos = np.where(axis[:, None] == 0, row[None, :], col[None, :]).astype(np.float32)
    ang = (pos * inv[f][:, None]).astype(np.float32)
    cos = np.cos(ang).astype(np.float32); sin = np.sin(ang).astype(np.float32)
    sgn = np.where(half == 0, -1.0, 1.0).astype(np.float32)[:, None]
    c["cos"] = np.ascontiguousarray(np.concatenate([cos, cos], 0))
    c["sin"] = np.ascontiguousarray(np.concatenate([sin * sgn, sin * sgn], 0))
    return c


def _core_inputs(core, inp, consts):
    b, r = core // 4, core % 4
    h0 = 2 * r
    f32 = np.float32
    m = dict(consts)
    x = inp["x"]
    m["xb"] = np.ascontiguousarray(x[b])
    m["ctxb"] = np.ascontiguousarray(inp["ctx"][b])
    xq = np.zeros((TQ + 2, D), f32)
    lo, hi = r * TQ - 1, r * TQ + TQ + 1
    slo, shi = max(lo, 0), min(hi, T)
    xq[slo - lo:shi - lo] = x[b, slo:shi]
    m["xq"] = xq
    cc = np.stack([inp["c"][b], inp["c_ctx"]], -1)
    m["cT"] = np.ascontiguousarray(cc.reshape(KC, 128, 2).transpose(1, 0, 2))
    m["w_mod"] = inp["w_mod"][0]
    m["b_mod"] = inp["b_mod"][0][None, :]
    gt = np.stack([inp["g_pre_mix"][0], inp["g_pre_ffn"][0]], 0)
    m["gT"] = np.ascontiguousarray(gt.reshape(2, KC, 128).transpose(2, 0, 1))
    m["grow"] = np.stack([inp["g_post_mix"][0], inp["g_post_ffn"][0]], 0)
    w = inp["w_in"][0]
    OMQ, OMK, OMV, OMO, OMG = 0, 1024, 2048, 3072, 4096
    ODQ = OMG + 32; ODK = ODQ + 1024; ODV = ODK + 1024
    cols = []
    for base in (OMQ, OMK, ODQ, ODK, OMO, OMV):
        cols += list(range(base + h0 * 128, base + h0 * 128 + 256))
    gcols = [OMG + g * 8 + h0 + hh for g in range(4) for hh in range(2)]
    cols += gcols
    cols += list(range(ODV + h0 * 128, ODV + h0 * 128 + 256))
    m["w_in"] = np.ascontiguousarray(w[:, cols])
    cw = inp["conv_qk_w"][0]; cb = inp["conv_qk_b"][0]
    convw = np.zeros((128, 4, 4), f32)
    for ci, base in enumerate([h0 * 128, h0 * 128 + 128, 1024 + h0 * 128, 1024 + h0 * 128 + 128]):
        convw[:, ci, 0:3] = cw[:, base:base + 128].T
        convw[:, ci, 3] = cb[base:base + 128]
    m["convw"] = convw
    bg = inp["b_gate"][0]
    m["bgate"] = np.tile(np.array([[bg[g, h0 + hh] for g in range(4) for hh in range(2)]], f32), (1, NCH))
    m["gml"] = np.ascontiguousarray(inp["g_mlstm"][0][h0 * 128:h0 * 128 + 256][None, :])
    m["gdf"] = np.ascontiguousarray(inp["g_diff"][0][:, None])
    m["gdr"] = np.ascontiguousarray(inp["g_diff"][0][None, :])
    m["lamv"] = np.concatenate([inp["lambda_q1"][0], inp["lambda_k1"][0], inp["lambda_q2"][0], inp["lambda_k2"][0]])[None, :].astype(f32)
    rows = []
    for rr in range(4):
        rows += list(range(2 * rr * 128, 2 * rr * 128 + 256)) + list(range(1024 + 2 * rr * 128, 1024 + 2 * rr * 128 + 256))
    m["w_out"] = np.ascontiguousarray(inp["w_out"][0][rows, :])
    m["w_up"] = inp["w_up"][0]
    fw = inp["conv_ffn_w"][0]; fb = inp["conv_ffn_b"][0]
    cf = np.concatenate([fw.T, fb[:, None]], 1)
    m["cfw"] = np.ascontiguousarray(cf.reshape(2 * HC, 128, 4).transpose(1, 0, 2))
    m["w_down"] = inp["w_down"][0]
    hm = np.zeros((128, 2), f32)
    hm[:, 0] = 1.0 if r > 0 else 0.0
    hm[:, 1] = 1.0 if r < 3 else 0.0
    m["hmask"] = hm
    qs = np.zeros((128, 4), f32); qs[:, r] = 1.0
    m["qsel"] = qs
    return {k: np.ascontiguousarray(v, dtype=v.dtype) for k, v in m.items()}


def kernel(**inputs):
    inp = {k: np.asarray(v) for k, v in inputs.items()}
    stage = int(os.environ.get("MK_STAGE", "99"))
    if stage not in _PROG:
        _PROG[stage] = build_program(stage)
    nc = _PROG[stage]
    consts = _consts()
    maps = [_core_inputs(c, inp, consts) for c in range(8)]
    res = run_bass_kernel_spmd(nc, maps, core_ids=list(range(8)))
    if stage < 99:
        return res
    out = np.zeros((2, T, D), np.float32)
    for c in range(8):
        b, r = c // 4, c % 4
        out[b, r * TQ:(r + 1) * TQ] = res.results[c]["out"]
    return out
```

```python
import os
import numpy as np
import ml_dtypes
import concourse.bass as bass
import concourse.mybir as mybir
from concourse.bass_utils import run_bass_kernel_spmd

F32 = mybir.dt.float32
BF16 = mybir.dt.bfloat16
AF = mybir.ActivationFunctionType
ALU = mybir.AluOpType
AX = mybir.AxisListType

PE, ACT, DVE, POOL, SP = "pe", "act", "dve", "pool", "sp"
ENGS = [PE, ACT, DVE, POOL, SP]

D = 2048
KC = 16
T = 8192
TC = 256
TA = T + TC
NCH = TA // 128
DFF = 5632
HC = DFF // 128
EPS = 1e-6
NCOL = 1800
C_MQ, C_MK, C_DQ, C_DK, C_MO, C_MV, C_G, C_DV = 0, 256, 512, 768, 1024, 1280, 1536, 1544
TQ = 2048
LAM_INIT = 0.2


class Buf:
    __slots__ = ("name", "w", "r")

    def __init__(self, name=""):
        self.name = name
        self.w = None
        self.r = []


class DSem:
    def __init__(self, sem):
        self.sem = sem
        self.total = 0


class Sched:
    def __init__(self, nc):
        self.nc = nc
        self.ops = {e: [] for e in ENGS}
        self.dsems = []
        self.bufs = {}

    def buf(self, key):
        b = self.bufs.get(key)
        if b is None:
            b = Buf(str(key))
            self.bufs[key] = b
        return b

    def new_dsem(self, name):
        d = DSem(self.nc.alloc_semaphore(name))
        self.dsems.append(d)
        return d

    def _deps(self, eng, reads, writes, dsem=None, is_dma=False):
        deps = []
        for b in reads:
            if b.w is not None:
                deps.append(b.w)
        for b in writes:
            if b.w is not None:
                if not (dsem is not None and b.w[0] == "d" and b.w[1] is dsem):
                    deps.append(b.w)
            deps.extend(b.r)
        out = []
        for d in deps:
            if d[0] == "e" and d[1] == eng and not is_dma:
                if eng == PE:
                    continue
                if not any((b.w is d) for b in reads):
                    continue
            out.append(d)
        return out

    def op(self, eng, fn, reads=(), writes=()):
        deps = self._deps(eng, reads, writes)
        idx = len(self.ops[eng])
        ev = ("e", eng, idx)
        self.ops[eng].append(dict(fn=fn, deps=deps, kind="c", flag=False))
        for b in reads:
            b.r = [x for x in b.r if not (x[0] == "e" and x[1] == eng)]
            b.r.append(ev)
        for b in writes:
            b.w = ev
            b.r = []
        return ev

    def dma(self, q, fn, dsem, reads=(), writes=(), new_gen=True, inc=16):
        deps = self._deps(q, reads, writes, dsem=dsem, is_dma=True)
        if new_gen and dsem.total > 0:
            deps.append(("d", dsem, dsem.total))
        dsem.total += inc
        ev = ("d", dsem, dsem.total)
        self.ops[q].append(dict(fn=fn, deps=deps, kind="d", dsem=dsem, flag=False, inc=inc))
        for b in reads:
            b.r = [x for x in b.r if not (x[0] == "d" and x[1] is dsem)]
            b.r.append(ev)
        for b in writes:
            b.w = ev
            b.r = []
        return ev

    def barrier(self):
        evs = []
        for e in ENGS:
            for i in range(len(self.ops[e]) - 1, -1, -1):
                if self.ops[e][i]["kind"] == "c":
                    evs.append(("e", e, i))
                    break
        for d in self.dsems:
            if d.total > 0:
                evs.append(("d", d, d.total))
        for e in ENGS:
            self.ops[e].append(dict(fn=None, deps=[x for x in evs if not (x[0] == "e" and x[1] == e)], kind="w", flag=False))
        for b in self.bufs.values():
            b.w = None
            b.r = []

    def wait_all(self, eng, evs):
        self.ops[eng].append(dict(fn=None, deps=list(evs), kind="w", flag=False))

    def finalize(self):
        nc = self.nc
        ops = self.ops
        EPOCH = 30000
        for e in ENGS:
            for o in ops[e]:
                for d in o["deps"]:
                    if d[0] == "e":
                        ops[d[1]][d[2]]["flag"] = True
        val = {}
        esem = {}
        for e in ENGS:
            c = 0
            for i, o in enumerate(ops[e]):
                if o["kind"] == "c" and o["flag"]:
                    val[(e, i)] = (c // EPOCH, c % EPOCH + 1)
                    c += 1
            esem[e] = [nc.alloc_semaphore(f"es_{e}{k}") for k in range(c // EPOCH + 1)]

        def replay(e, eng):
            seen = {}
            for oi_, o in enumerate(ops[e]):
                o["idx"] = oi_
                need = {}
                for d in o["deps"]:
                    if d[0] == "e":
                        key = ("e", d[1])
                        v = val[(d[1], d[2])]
                        sem = esem[d[1]][v[0]]
                    else:
                        key = ("d", id(d[1]))
                        v = (0, d[2])
                        sem = d[1].sem
                    if seen.get(key, (0, 0)) >= v:
                        continue
                    if key not in need or need[key][1] < v:
                        need[key] = (sem, v)
                for key, (sem, v) in need.items():
                    eng.wait_ge(sem, v[1])
                    seen[key] = v
                if o["fn"] is None:
                    continue
                ins = o["fn"](eng)
                if o["kind"] == "d":
                    ins.then_inc(o["dsem"].sem, o["inc"])
                elif o["flag"]:
                    ins.then_inc(esem[e][val[(e, o["idx"])][0]], 1)

        with nc.Block() as block:
            @block.tensor
            def _(eng):
                replay(PE, eng)

            @block.scalar
            def _(eng):
                replay(ACT, eng)

            @block.vector
            def _(eng):
                replay(DVE, eng)

            @block.gpsimd
            def _(eng):
                replay(POOL, eng)

            @block.sync
            def _(eng):
                replay(SP, eng)


def build_program(stage=99):
    nc = bass.Bass("TRN2", target_bir_lowering=False)
    S = Sched(nc)
    B = S.buf

    def IN(name, shape, dt=F32):
        return nc.dram_tensor(name, shape, dt, kind="ExternalInput").ap()

    def SCR(name, shape, dt=F32):
        kind = "ExternalOutput" if (stage < 99 and name in DBG_OUT.get(stage, ())) else "Internal"
        return nc.dram_tensor(name, shape, dt, kind=kind).ap()

    xb = IN("xb", [T, D]); ctxb = IN("ctxb", [TC, D]); xq = IN("xq", [TQ + 2, D])
    cT = IN("cT", [128, KC, 2]); w_mod = IN("w_mod", [D, 6 * D]); b_mod = IN("b_mod", [1, 6 * D])
    gT = IN("gT", [128, 2, KC]); grow = IN("grow", [2, D])
    w_in = IN("w_in", [D, NCOL]); convw = IN("convw", [128, 4, 4]); bgate = IN("bgate", [1, 8 * NCH])
    gml = IN("gml", [1, 256]); gdf = IN("gdf", [128, 1]); gdr = IN("gdr", [1, 128]); lamv = IN("lamv", [1, 256])
    w_out = IN("w_out", [D, D]); w_up = IN("w_up", [D, 2 * DFF]); cfw = IN("cfw", [128, 2 * HC, 4]); w_down = IN("w_down", [DFF, D])
    identf_d = IN("identf", [128, 128]); maskF_d = IN("maskF", [128, 128]); maskB_d = IN("maskB", [128, 128])
    triF_d = IN("triF", [128, 128]); triB_d = IN("triB", [128, 128]); perm_d = IN("perm", [128, 128]); jb_d = IN("jb", [NCH, NCH])
    cos_d = IN("cos", [128, T]); sin_d = IN("sin", [128, T]); hmask_d = IN("hmask", [128, 2]); qsel_d = IN("qsel", [128, 4])
    out_d = nc.dram_tensor("out", [TQ, D], F32, kind="ExternalOutput").ap()

    DBG_OUT = {1: ("modrow", "mqT_l", "mkT_c", "oS", "mvS", "gS", "dqT", "dkT", "dvS"), 2: tuple(f"mixL{p}" for p in range(8)), 3: tuple(f"mixL{p}" for p in range(8)), 4: ("h2S", "xmidS"), 6: ("h2S", "xmidS")}
    modrow = SCR("modrow", [2, 6 * D])
    mqT_c = SCR("mqT_c", [2, 128, TC + 1], BF16); mqT_l = SCR("mqT_l", [2, 128, T + 1], BF16)
    mkT_c = SCR("mkT_c", [2, 128, TC + 1], BF16); mkT_l = SCR("mkT_l", [2, 128, T + 1], BF16)
    oS = SCR("oS", [2, T, 128]); mvS = SCR("mvS", [2, TA, 128], BF16); gS = SCR("gS", [TA, 8])
    dqT = SCR("dqT", [2, 128, T], BF16); dkT = SCR("dkT", [2, 128, TA], BF16); dvS = SCR("dvS", [2, TA, 128], BF16)
    hfS = SCR("hfS", [2, T, 128])
    mixL = [SCR(f"mixL{p}", [512, 1024], BF16) for p in range(8)]
    mixA = [SCR(f"mixA{p}", [2048, 1024], BF16) for p in range(8)]
    xmidS = SCR("xmidS", [TQ, D])
    h2S = SCR("h2S", [D, TQ + 2], BF16)
    wuS = SCR("wuS", [HC, 128, KC * 256], BF16)
    wdS = SCR("wdS", [D // 128, 128, HC * 128], BF16)

    big = nc.alloc_sbuf_tensor("big", [128, 52000], F32)
    apos = [0]

    def carve(n, dt=F32):
        o = apos[0]
        apos[0] += (n + 7) // 8 * 8
        assert apos[0] <= 52000, apos[0]
        v = big[:, o:o + n]
        return v if dt is F32 else v.bitcast(dt)

    def carve_bf(nel):
        return carve((nel + 1) // 2, BF16)[:, 0:nel]

    psp = [nc.alloc_psum_tensor(f"psp{i}", [128, 1024], F32) for i in range(4)]
    psf = [psp[0][:, 0:512], psp[0][:, 512:1024], psp[1][:, 0:512], psp[1][:, 512:1024], psp[2][:, 0:512], psp[2][:, 512:1024]]
    psb = [psp[3][:, 0:512].bitcast(BF16), psp[3][:, 512:1024].bitcast(BF16)]

    def mm(out, lhsT, rhs, start, stop, R, W):
        S.op(PE, lambda e: e.matmul(out, lhsT=lhsT, rhs=rhs, start=start, stop=stop), reads=R, writes=W)

    def tp(out, in_, ident, R, W):
        S.op(PE, lambda e: e.transpose(out=out, in_=in_, identity=ident), reads=R, writes=W)

    def act(out, in_, func, R, W, bias=None, scale=None, accum=None, eng=ACT):
        kw = {}
        if bias is not None:
            kw["bias"] = bias
        if scale is not None:
            kw["scale"] = scale
        if accum is not None:
            kw["accum_out"] = accum
        S.op(eng, lambda e: e.activation(out=out, in_=in_, func=func, **kw), reads=R, writes=W)

    def cp(eng, out, in_, R, W):
        if eng == ACT:
            S.op(ACT, lambda e: e.copy(out=out, in_=in_), reads=R, writes=W)
        else:
            S.op(eng, lambda e: e.tensor_copy(out=out, in_=in_), reads=R, writes=W)

    def tt(eng, out, a, b, op, R, W):
        S.op(eng, lambda e: e.tensor_tensor(out=out, in0=a, in1=b, op=op), reads=R, writes=W)

    def ts(eng, out, a, s1, s2, op0, op1, R, W):
        if s2 is None:
            S.op(eng, lambda e: e.tensor_scalar(out=out, in0=a, scalar1=s1, scalar2=None, op0=op0), reads=R, writes=W)
        else:
            S.op(eng, lambda e: e.tensor_scalar(out=out, in0=a, scalar1=s1, scalar2=s2, op0=op0, op1=op1), reads=R, writes=W)

    def stt(eng, out, a, sc, b, op0, op1, R, W):
        S.op(eng, lambda e: e.scalar_tensor_tensor(out=out, in0=a, scalar=sc, in1=b, op0=op0, op1=op1), reads=R, writes=W)

    def memset(eng, out, v, W):
        S.op(eng, lambda e: e.memset(out, v), writes=W)

    def dma(q, out, in_, dsem, R, W, new_gen=True):
        return S.dma(q, lambda e: e.dma_start(out=out, in_=in_, allow_slow_non_contiguous=True), dsem, reads=R, writes=W, new_gen=new_gen)

    st_sems = [S.new_dsem(f"st{i}") for i in range(8)]
    st_i = [0]

    stp_sems = [S.new_dsem(f"stp{i}") for i in range(8)]
    st_q = [SP]

    def store(out, in_, R, W, q=None):
        q = q or st_q[0]
        pool_ = st_sems if q == SP else stp_sems
        d = pool_[st_i[0] % len(pool_)]
        st_i[0] += 1
        return dma(q, out, in_, d, R, W)

    ld_sems = [S.new_dsem(f"ld{i}") for i in range(8)]
    ld_i = [0]

    ldp_sems = [S.new_dsem(f"ldp{i}") for i in range(4)]

    def load(out, in_, R, W, q=SP):
        pool_ = ld_sems if q == SP else ldp_sems
        d = pool_[ld_i[0] % len(pool_)]
        ld_i[0] += 1
        return dma(q, out, in_, d, R, W)

    c_mark = apos[0]
    identf = carve(128); identb = carve_bf(128)
    maskF = carve(128); maskB = carve(128); triF = carve(128); triB = carve(128); permM = carve(128)
    jb = carve(NCH)
    onesf = carve(128); onesb = carve_bf(128)
    cTs = carve(KC * 2).rearrange("p (k j) -> p k j", j=2)
    gTs = carve(2 * KC).rearrange("p (a k) -> p a k", k=KC)
    fmv = carve(96)
    A1 = carve(KC); A1c = carve(KC); A2 = carve(KC)
    convs = carve(16).rearrange("p (c j) -> p c j", j=4)
    gdfs = carve(1); gdsc = carve(1)
    lam4 = carve(256); lamc = carve(2); lamneg = carve(1)
    hmask = carve(2); qsel = carve(4)
    epsc = carve(1)
    for (dst, src, nm) in [(identf, identf_d, "identf"), (maskF, maskF_d, "maskF"), (maskB, maskB_d, "maskB"), (triF, triF_d, "triF"),
                           (triB, triB_d, "triB"), (permM, perm_d, "perm"), (convs, convw, "convs"), (gdfs, gdf, "gdfs"),
                           (hmask, hmask_d, "hmask"), (qsel, qsel_d, "qsel"), (cTs, cT, "cTs"), (gTs, gT, "gTs")]:
        load(dst, src, [], [B(nm)])
    load(jb[0:NCH, :], jb_d, [], [B("jb")])
    load(lam4, lamv.partition_broadcast(128), [], [B("lam4")])
    cp(DVE, identb, identf, [B("identf")], [B("identb")])
    memset(DVE, onesf, 1.0, [B("onesf")])
    memset(DVE, onesb, 1.0, [B("onesb")])
    memset(DVE, epsc, EPS, [B("epsc")])
    lamt = carve(128)
    tt(DVE, lamt[:, 0:64], lam4[:, 0:64], lam4[:, 64:128], ALU.mult, [B("lam4")], [B("lamt")])
    tt(DVE, lamt[:, 64:128], lam4[:, 128:192], lam4[:, 192:256], ALU.mult, [B("lam4")], [B("lamt")])
    S.op(DVE, lambda e: e.reduce_sum(out=lamc, in_=lamt.rearrange("p (a b) -> p a b", b=64), axis=AX.X), reads=[B("lamt")], writes=[B("lamc")])
    act(lamc, lamc, AF.Exp, [B("lamc")], [B("lamc")])
    tt(DVE, lamneg, lamc[:, 1:2], lamc[:, 0:1], ALU.subtract, [B("lamc")], [B("lamneg")])
    ts(DVE, lamneg, lamneg, -LAM_INIT, None, ALU.add, None, [B("lamneg")], [B("lamneg")])
    ts(DVE, gdsc, gdfs, 1.0 - LAM_INIT, None, ALU.mult, None, [B("gdfs")], [B("gdsc")])

    scs = carve(KC * 2).rearrange("p (k j) -> p k j", j=2)
    act(scs, cTs, AF.Silu, [B("cTs")], [B("scs")])
    p0_mark = apos[0]
    NWT = 256
    wm_sem = [S.new_dsem(f"wm{i}") for i in range(2)]
    w_mod_v = w_mod.rearrange("(k p) c -> p k c", p=128)
    modbuf = {}

    def mod_alloc():
        modbuf["wm"] = [carve(KC * NWT).rearrange("p (k c) -> p k c", c=NWT) for _ in range(2)]
        modbuf["bm"] = [carve(512) for _ in range(2)]
        modbuf["mrow"] = [carve(512) for _ in range(2)]

    mod_issued = set()

    def mod_dma(ct):
        if ct in mod_issued or ct >= 6 * D // 512:
            return
        mod_issued.add(ct)
        wm = modbuf["wm"]; bm = modbuf["bm"]
        for hf in range(2):
            i = ct * 2 + hf
            c0 = i * NWT
            dma(SP, wm[i % 2], w_mod_v[:, :, c0:c0 + NWT], wm_sem[i % 2], [], [B(("wm", i % 2))])
        load(bm[ct % 2][0:2, :], b_mod[:, ct * 512:(ct + 1) * 512].partition_broadcast(2), [], [B(("bm", ct % 2))])

    def mod_tile(ct, pst, pbuf, prefetch=True):
        wm = modbuf["wm"]; bm = modbuf["bm"]; mrow = modbuf["mrow"]
        mod_dma(ct)
        for hf in range(2):
            i = ct * 2 + hf
            for k in range(KC):
                mm(pst[0:2, hf * NWT:(hf + 1) * NWT], scs[:, k, :], wm[i % 2][:, k, :], k == 0, k == KC - 1,
                   [B("scs"), B(("wm", i % 2))], [pbuf])
        tt(DVE, mrow[ct % 2][0:2, :], pst[0:2, :], bm[ct % 2][0:2, :], ALU.add, [pbuf, B(("bm", ct % 2))], [B(("mrow", ct % 2))])
        store(modrow[:, ct * 512:(ct + 1) * 512], mrow[ct % 2][0:2, :], [B(("mrow", ct % 2))], [B(("modrow", ct))])
        if prefetch:
            mod_dma(ct + 1)

    mod_alloc()
    for ct in range(2 * D // 512):
        mod_tile(ct, psf[ct % 2], B(("psf", ct % 2)), prefetch=(ct + 1 < 2 * D // 512))
    v96 = carve(128)
    for j, (row, c0) in enumerate([(0, 0), (0, D), (1, 0), (1, D)]):
        load(v96[j * 16:(j + 1) * 16, :], modrow[row:row + 1, c0:c0 + D].rearrange("o (k p) -> (o k) p", p=128), [B(("modrow", x)) for x in range(8)], [B("v96")])
    tp_out = psf[2]
    S.op(PE, lambda e: e.transpose(out=tp_out[:, 0:64], in_=v96[0:64, :], identity=identf[0:64, 0:64]), reads=[B("v96"), B("identf")], writes=[B(("psf", 2))])
    cp(DVE, fmv[:, 0:64], tp_out[:, 0:64], [B(("psf", 2))], [B("fmv")])
    SH1, SC1, CSH1, CSC1, SH2, SC2 = [fmv[:, j * 16:(j + 1) * 16] for j in range(6)]
    stt(DVE, A1, SC1, 1.0, gTs[:, 0, :], ALU.add, ALU.mult, [B("fmv"), B("gTs")], [B("A1")])
    stt(DVE, A1c, CSC1, 1.0, gTs[:, 0, :], ALU.add, ALU.mult, [B("fmv"), B("gTs")], [B("A1c")])
    S.barrier()
    apos[0] = p0_mark

    st_q[0] = POOL
    p1_mark = apos[0]
    win = carve_bf(KC * NCOL).rearrange("p (k c) -> p k c", c=NCOL)
    win_sem = S.new_dsem("win")
    w_in_v = w_in.rearrange("(k p) c -> p k c", p=128)
    for k in range(KC):
        dma(POOL, win[:, k, :], w_in_v[:, k, :], win_sem, [], [B("win")], new_gen=(k == 0))
    xr = [carve(D) for _ in range(3)]
    xr_sem = [S.new_dsem(f"xr{i}") for i in range(3)]
    xn = [carve_bf(D) for _ in range(4)]
    h1T = [carve_bf(KC * 512).rearrange("p (k t) -> p k t", t=512) for _ in range(2)]
    ss = carve(4); rstd = carve(4)
    stg = [[carve(520) for _ in range(2)] for _ in range(4)]
    ctmp = [carve(512) for _ in range(2)]
    csig = [carve(512) for _ in range(2)]
    obf = [carve_bf(512) for _ in range(4)]
    qf = [carve(512) for _ in range(2)]
    cosb = [carve(512) for _ in range(2)]; sinb = [carve(512) for _ in range(2)]
    rt1 = [carve(512) for _ in range(2)]; rt2 = [carve(512) for _ in range(2)]
    otm = [carve(256) for _ in range(2)]
    vtm = [carve_bf(256) for _ in range(4)]
    gtm = [carve(8) for _ in range(2)]
    flush = carve(8)
    cnt = {"blk": 0, "o": 0, "q": 0, "r": 0, "ot": 0, "vt": 0, "gt": 0, "ct": 0}
    ones512 = carve_bf(512)
    memset(DVE, ones512, 1.0, [B("ones512")])
    sh1hl = carve_bf(2 * KC).rearrange("p (j k) -> p j k", k=KC)
    sh1t = carve(KC)
    c1f = carve(NCOL); c1hl = carve_bf(2 * NCOL).rearrange("p (j c) -> p j c", c=NCOL)
    cp(DVE, sh1hl[:, 0, :], SH1, [B("fmv")], [B("sh1hl")])
    tt(DVE, sh1t, SH1, sh1hl[:, 0, :], ALU.subtract, [B("fmv"), B("sh1hl")], [B("sh1t")])
    cp(DVE, sh1hl[:, 1, :], sh1t, [B("sh1t")], [B("sh1hl")])

    def make_shift_and_fold():
        for gi, c0 in enumerate(range(0, NCOL, 512)):
            n = min(512, NCOL - c0)
            pst = psf[gi % 4]; pbuf = B(("psf", gi % 4))
            i_ = 0
            for j in range(2):
                for k in range(KC):
                    mm(pst[0:1, 0:n], sh1hl[:, j, k:k + 1], win[:, k, c0:c0 + n], i_ == 0, i_ == 2 * KC - 1, [B("sh1hl"), B("win")], [pbuf])
                    i_ += 1
            cp(ACT, c1f[0:1, c0:c0 + n], pst[0:1, 0:n], [pbuf], [B("c1f")])
        cp(DVE, c1hl[0:1, 0, :], c1f[0:1, :], [B("c1f")], [B("c1hl")])
        tt(DVE, c1f[0:1, :], c1f[0:1, :], c1hl[0:1, 0, :], ALU.subtract, [B("c1f"), B("c1hl")], [B("c1f")])
        cp(DVE, c1hl[0:1, 1, :], c1f[0:1, :], [B("c1f")], [B("c1hl")])
        for k in range(KC):
            ts(DVE, win[:, k, :], win[:, k, :], A1[:, k:k + 1], None, ALU.mult, None, [B("win"), B("A1")], [B("win")])

    def shift_fm(pst, c0, nt, pbuf):
        for j in range(1):
            mm(pst[:, 0:nt], c1hl[0:1, j, c0:c0 + 128], ones512[0:1, 0:nt], j == 0, False, [B("c1hl"), B("ones512")], [pbuf])

    def shift_tm(pst, c0, n, pbuf):
        for j in range(1):
            mm(pst[:, 0:n], ones512[0:1, 0:128], c1hl[0:1, j, c0:c0 + n], j == 0, False, [B("c1hl"), B("ones512")], [pbuf])

    tiles = [(True, 0, TC)] + [(False, i * 512, 512) for i in range(T // 512)]

    pendA1 = []

    def stageA1_block(ti, bl):
        isc, t0, nt = tiles[ti]
        src = ctxb if isc else xb
        i = cnt["blk"]; cnt["blk"] += 1
        xs = xr[i % 3]
        dma(SP, xs, src[t0 + bl * 128:t0 + (bl + 1) * 128, :], xr_sem[i % 3], [], [B(("xr", i % 3))])
        memset(DVE, ss[:, 0:1], 0.0, [B("ss")])
        act(xn[bl], xs, AF.Square, [B(("xr", i % 3)), B("ss")], [B(("xn", bl)), B("ss")], accum=ss[:, 0:1])
        act(rstd[:, 0:1], ss[:, 0:1], AF.Ln, [B("ss"), B("epsc")], [B("rstd")], scale=1.0 / D, bias=epsc)
        act(rstd[:, 0:1], rstd[:, 0:1], AF.Exp, [B("rstd")], [B("rstd")], scale=-0.5)
        ts(DVE, xn[bl], xs, rstd[:, 0:1], None, ALU.mult, None, [B(("xr", i % 3)), B("rstd")], [B(("xn", bl))])

    def stageA1(ti):
        for bl in range(tiles[ti][2] // 128):
            stageA1_block(ti, bl)

    def queueA1(ti):
        for bl in range(tiles[ti][2] // 128):
            pendA1.append((ti, bl))

    def popA1():
        if pendA1:
            stageA1_block(*pendA1.pop(0))

    def stageA2(ti):
        isc, t0, nt = tiles[ti]
        hT = h1T[ti % 2]
        Asc = A1c if isc else A1
        Ash = CSH1 if isc else SH1
        for bl in range(nt // 128):
            for g in range(2):
                pb = psb[g]
                for kk in range(8):
                    k = g * 8 + kk
                    tp(pb[:, kk * 128:(kk + 1) * 128], xn[bl][:, k * 128:(k + 1) * 128], identb, [B(("xn", bl)), B("identb")], [B(("psb", g))])
                if not isc:
                    cp(ACT if g == 0 else DVE, hT[:, g * 8:(g + 1) * 8, bl * 128:(bl + 1) * 128], pb.rearrange("p (k t) -> p k t", t=128), [B(("psb", g))], [B(("h1T", ti % 2))])
                    continue
                for kk in range(8):
                    k = g * 8 + kk
                    eng = ACT if kk % 2 == 0 else DVE
                    if eng == ACT:
                        act(hT[:, k, bl * 128:(bl + 1) * 128], pb[:, kk * 128:(kk + 1) * 128], AF.Identity, [B(("psb", g)), B("A1"), B("A1c"), B("fmv")], [B(("h1T", ti % 2))],
                            scale=Asc[:, k:k + 1], bias=Ash[:, k:k + 1])
                    else:
                        ts(DVE, hT[:, k, bl * 128:(bl + 1) * 128], pb[:, kk * 128:(kk + 1) * 128], Asc[:, k:k + 1], Ash[:, k:k + 1], ALU.mult, ALU.add,
                           [B(("psb", g)), B("A1"), B("A1c"), B("fmv")], [B(("h1T", ti % 2))])

    def conv_chunk(ci, ti, pst, psbuf, isc, t0, nt, first, last):
        sg = stg[ci][ti % 2]; sgp = stg[ci][(ti - 1) % 2]
        sb = B(("stg", ci, ti % 2)); sbp = B(("stg", ci, (ti - 1) % 2))
        pnt = tiles[ti - 1][2] if ti > 0 else 0
        if first:
            memset(DVE, sg[:, 0:2], 0.0, [sb])
        else:
            cp(DVE, sg[:, 0:2], sgp[:, pnt:pnt + 2], [sbp], [sb])
        cp(ACT, sg[:, 2:2 + nt], pst[:, 0:nt], [psbuf], [sb])
        w0, w1, w2, bb = [convs[:, ci, j:j + 1] for j in range(4)]
        head = ci % 2
        isk = ci >= 2
        dst = (mkT_c if isc else mkT_l) if isk else (mqT_c if isc else mqT_l)

        def conv_out(n, src0, col0, zero_next=False):
            j = cnt["ct"]; cnt["ct"] += 1
            t = ctmp[j % 2]; tb = B(("ctmp", j % 2))
            ts(DVE, t[:, 0:n], sg[:, src0 + 1:src0 + 1 + n], w1, bb, ALU.mult, ALU.add, [sb, B("convs")], [tb])
            stt(DVE, t[:, 0:n], sg[:, src0:src0 + n], w0, t[:, 0:n], ALU.mult, ALU.add, [sb, B("convs"), tb], [tb])
            if not zero_next:
                stt(DVE, t[:, 0:n], sg[:, src0 + 2:src0 + 2 + n], w2, t[:, 0:n], ALU.mult, ALU.add, [sb, B("convs"), tb], [tb])
            oi = cnt["o"]; cnt["o"] += 1
            ob = obf[oi % 4]; obb = B(("obf", oi % 4))
            if not isk:
                act(ob[:, 0:n], t[:, 0:n], AF.Silu, [tb], [obb])
            else:
                sgm = csig[j % 2]; sgb = B(("csig", j % 2))
                act(sgm[:, 0:n], t[:, 0:n], AF.Sigmoid, [tb], [sgb])
                stt(DVE, ob[:, 0:n], t[:, 0:n], 128.0 ** -0.5, sgm[:, 0:n], ALU.mult, ALU.mult, [tb, sgb], [obb])
            store(dst[head, :, col0:col0 + n], ob[:, 0:n], [obb], [B((("k" if isk else "q"), head, isc, col0))])

        conv_out(nt, 0, t0)
        if last:
            conv_out(1, nt, t0 + nt, zero_next=True)

    def stageB(ti):
        isc, t0, nt = tiles[ti]
        hT = h1T[ti % 2]; hb = B(("h1T", ti % 2))
        first = ti in (0, 1)
        last = ti in (0, len(tiles) - 1)
        fm = [(C_MQ, "c", 0), (C_MQ + 128, "c", 1), (C_MK, "c", 2), (C_MK + 128, "c", 3),
              (C_DQ, "dq", 0), (C_DQ + 128, "dq", 1), (C_DK, "dk", 0), (C_DK + 128, "dk", 1)]
        for fi, (c0, kind, idx) in enumerate(fm):
            if isc and kind == "dq":
                continue
            pi = fi % 4
            pst = psf[pi]; pbuf = B(("psf", pi))
            if not isc:
                shift_fm(pst, c0, nt, pbuf)
            for k in range(KC):
                mm(pst[:, 0:nt], win[:, k, c0:c0 + 128], hT[:, k, 0:nt], (k == 0) and isc, k == KC - 1, [B("win"), hb], [pbuf])
            if kind == "c":
                conv_chunk(idx, ti, pst, pbuf, isc, t0, nt, first, last)
            else:
                dst = dqT if kind == "dq" else dkT
                col0 = t0 if (kind == "dq" or isc) else TC + t0
                oi = cnt["o"]; cnt["o"] += 1
                ob = obf[oi % 4]; obb = B(("obf", oi % 4))
                if isc:
                    cp(ACT, ob[:, 0:nt], pst[:, 0:nt], [pbuf], [obb])
                else:
                    j = cnt["r"]; cnt["r"] += 1
                    q_ = qf[j % 2]; qb = B(("qf", j % 2))
                    cp(ACT, q_, pst[:, 0:nt], [pbuf], [qb])
                    if fi == 4:
                        jj = ti % 2
                        load(cosb[jj], cos_d[:, t0:t0 + nt], [], [B(("cos", jj))])
                        load(sinb[jj], sin_d[:, t0:t0 + nt], [], [B(("sin", jj))])
                    jj = ti % 2
                    prot = psf[4 + (j % 2)]; prb = B(("psf", 4 + (j % 2)))
                    mm(prot[:, 0:nt], permM, q_, True, True, [B("perm"), qb], [prb])
                    tt(DVE, rt1[j % 2], q_, cosb[jj], ALU.mult, [qb, B(("cos", jj))], [B(("rt1", j % 2))])
                    tt(DVE, rt2[j % 2], prot[:, 0:nt], sinb[jj], ALU.mult, [prb, B(("sin", jj))], [B(("rt2", j % 2))])
                    tt(DVE, ob[:, 0:nt], rt1[j % 2], rt2[j % 2], ALU.add, [B(("rt1", j % 2)), B(("rt2", j % 2))], [obb])
                store(dst[idx, :, col0:col0 + nt], ob[:, 0:nt], [obb], [B((kind, idx, isc, t0))])
            if fi % 2 == 1:
                popA1()
        while pendA1:
            popA1()
        for bl in range(nt // 128):
            tg0 = (0 if isc else TC) + t0 + bl * 128
            lhs = lambda k: hT[:, k, bl * 128:(bl + 1) * 128]
            if not isc:
                pst = psf[0]; pbuf = B(("psf", 0))
                shift_tm(pst, C_MO, 256, pbuf)
                for k in range(KC):
                    mm(pst[:, 0:256], lhs(k), win[:, k, C_MO:C_MO + 256], False, k == KC - 1, [B("win"), hb], [pbuf])
                j = cnt["ot"]; cnt["ot"] += 1
                act(otm[j % 2], pst[:, 0:256], AF.Sigmoid, [pbuf], [B(("otm", j % 2))])
                tl = t0 + bl * 128
                store(oS[:, tl:tl + 128, :].rearrange("h t e -> t h e"), otm[j % 2].rearrange("p (h e) -> p h e", e=128), [B(("otm", j % 2))], [B(("oS", tl))])
            pst = psf[1]; pbuf = B(("psf", 1))
            if not isc:
                shift_tm(pst, C_MV, 264, pbuf)
            for k in range(KC):
                mm(pst[:, 0:264], lhs(k), win[:, k, C_MV:C_MV + 264], (k == 0) and isc, k == KC - 1, [B("win"), hb], [pbuf])
            j = cnt["vt"]; cnt["vt"] += 1
            cp(DVE, vtm[j % 4], pst[:, 0:256], [pbuf], [B(("vtm", j % 4))])
            store(mvS[:, tg0:tg0 + 128, :].rearrange("h t e -> t h e"), vtm[j % 4].rearrange("p (h e) -> p h e", e=128), [B(("vtm", j % 4))], [B(("mvS", tg0))])
            jg = cnt["gt"]; cnt["gt"] += 1
            cp(DVE, gtm[jg % 2], pst[:, 256:264], [pbuf], [B(("gtm", jg % 2))])
            store(gS[tg0:tg0 + 128, :], gtm[jg % 2], [B(("gtm", jg % 2))], [B(("gS", tg0))])
            pst = psf[2]; pbuf = B(("psf", 2))
            if not isc:
                shift_tm(pst, C_DV, 256, pbuf)
            for k in range(KC):
                mm(pst[:, 0:256], lhs(k), win[:, k, C_DV:C_DV + 256], (k == 0) and isc, k == KC - 1, [B("win"), hb], [pbuf])
            j = cnt["vt"]; cnt["vt"] += 1
            cp(ACT, vtm[j % 4], pst[:, 0:256], [pbuf], [B(("vtm", j % 4))])
            store(dvS[:, tg0:tg0 + 128, :].rearrange("h t e -> t h e"), vtm[j % 4].rearrange("p (h e) -> p h e", e=128), [B(("vtm", j % 4))], [B(("dvS", tg0))])

    stageA1(0)
    stageA2(0)
    for ti in range(len(tiles)):
        if ti + 1 < len(tiles):
            queueA1(ti + 1)
        stageB(ti)
        if ti == 0:
            make_shift_and_fold()
        if ti + 1 < len(tiles):
            stageA2(ti + 1)
    S.barrier()
    apos[0] = p1_mark
    if stage == 1:
        return finish(nc, S, out_d)


    st_q[0] = SP
    p2_mark = apos[0]
    KT = [carve_bf(TA) for _ in range(2)]; QT = [carve_bf(T) for _ in range(2)]
    VV = [carve_bf(NCH * 130).rearrange("p (c e) -> p c e", e=130) for _ in range(2)]
    for h in range(2):
        load(KT[h], dkT[h], [], [B(("KT", h))], q=POOL)
        load(QT[h], dqT[h], [], [B(("QT", h))], q=POOL)
        memset(DVE, VV[h][:, :, 128:130], 1.0, [B(("VV", h))])
        load(VV[h][:, :, 0:128], dvS[h].rearrange("(c s) e -> s c e", s=128), [], [B(("VV", h))], q=POOL)
    Pb = [carve_bf(1024) for _ in range(4)]
    spair = [psp[0], psp[1]]
    gdrow = carve(128)
    load(gdrow, gdr.partition_broadcast(128), [], [B("gdrow")])
    ts(DVE, gdrow, gdrow, 1.0 - LAM_INIT, None, ALU.mult, None, [B("gdrow")], [B("gdrow")])
    mod_alloc()
    mod_next = [2 * D // 512]
    w_up_v = w_up.rearrange("(k p) c -> p k c", p=128)
    w_dn_v = w_down.rearrange("(j p) c -> p j c", p=128)
    pcu = [carve_bf(KC * 256).rearrange("p (k c) -> p k c", c=256) for _ in range(2)]
    pcd = [carve_bf(HC * 128).rearrange("p (j c) -> p j c", c=128) for _ in range(1)]
    pcu_sem = [S.new_dsem(f"pcu{i}") for i in range(2)]
    pcd_sem = [S.new_dsem(f"pcd{i}") for i in range(1)]
    pc_jobs = [("u", j) for j in range(HC)] + [("d", c) for c in range(D // 128)]
    pc_cnt = {"u": 0, "d": 0}

    def precast_one():
        if not pc_jobs:
            return
        kind, idx = pc_jobs.pop(0)
        i_ = pc_cnt[kind] % (2 if kind == "u" else 1); pc_cnt[kind] += 1
        if kind == "u":
            w = pcu[i_]; wb = B(("pcu", i_)); ws = pcu_sem[i_]
            dma(POOL, w[:, :, 0:128], w_up_v[:, :, idx * 128:(idx + 1) * 128], ws, [], [wb])
            dma(POOL, w[:, :, 128:256], w_up_v[:, :, DFF + idx * 128:DFF + (idx + 1) * 128], ws, [], [wb], new_gen=False)
            store(wuS[idx], w.rearrange("p k c -> p (k c)"), [wb], [B(("wuS", idx))], q=SP)
        else:
            w = pcd[i_]; wb = B(("pcd", i_)); ws = pcd_sem[i_]
            dma(POOL, w, w_dn_v[:, :, idx * 128:(idx + 1) * 128], ws, [], [wb])
            store(wdS[idx], w.rearrange("p j c -> p (j c)"), [wb], [B(("wdS", idx))], q=SP)


    acc_bank = [psf[4], psf[5], psp[3][:, 0:512]]

    def acc_ap(idx):
        o = (idx % 3) * 132
        return acc_bank[idx // 3][:, o:o + 129], B(("acc", idx // 3))

    fo = [carve(128) for _ in range(2)]; frr = [carve(4) for _ in range(2)]; fsq = carve(128); fss = [carve(2) for _ in range(2)]
    fdt = [carve_bf(128) for _ in range(2)]; fd = [carve_bf(512) for _ in range(2)]
    fcnt = 0
    for h in range(2):
        for qt in range(T // 512):
            q0 = qt * 512

            def s_step(kb):
                sp = kb % 2; pp = kb % 4
                for m in range(2):
                    mm(spair[sp][:, m * 512:(m + 1) * 512], KT[h][m * 64:(m + 1) * 64, kb * 128:(kb + 1) * 128], QT[h][m * 64:(m + 1) * 64, q0:q0 + 512], True, True,
                       [B(("KT", h)), B(("QT", h))], [B(("psp", sp))])
                act(Pb[pp], spair[sp][:, :], AF.Exp, [B(("psp", sp))], [B(("Pb", pp))], scale=0.125)

            def pv_step(kb):
                pp = kb % 4
                for m in range(2):
                    for qb in range(4):
                        ap_, ab_ = acc_ap(m * 4 + qb)
                        mm(ap_, Pb[pp][:, m * 512 + qb * 128:m * 512 + (qb + 1) * 128], VV[h][:, kb, 0:129], (kb == 0) and ((m * 4 + qb) % 3 == 0), kb == NCH - 1, [B(("VV", h)), B(("Pb", pp))], [ab_])

            for kb in range(NCH):
                if kb % 33 == 5:
                    precast_one()
                s_step(kb)
                if kb > 1:
                    pv_step(kb - 2)
            pv_step(NCH - 2)
            pv_step(NCH - 1)
            if mod_next[0] < 6 * D // 512:
                mod_tile(mod_next[0], psf[3], B(("psp", 1)))
                mod_next[0] += 1
            jf = fcnt % 2; fcnt += 1
            for qb in range(4):
                i2 = qb % 2
                a0, ab0 = acc_ap(qb); a1, ab1 = acc_ap(4 + qb)
                rr = frr[i2]; rb = B(("frr", i2))
                S.op(DVE, (lambda rr, a0: lambda e: e.reciprocal(out=rr[:, 0:1], in_=a0[:, 128:129]))(rr, a0), reads=[ab0], writes=[rb])
                S.op(DVE, (lambda rr, a1: lambda e: e.reciprocal(out=rr[:, 1:2], in_=a1[:, 128:129]))(rr, a1), reads=[ab1], writes=[rb])
                tt(DVE, rr[:, 2:3], rr[:, 1:2], lamneg[:, 0:1], ALU.mult, [rb, B("lamneg")], [rb])
                act(fo[i2], a0[:, 0:128], AF.Copy, [ab0, rb], [B(("fo", i2))], scale=rr[:, 0:1])
                stt(DVE, fo[i2], a1[:, 0:128], rr[:, 2:3], fo[i2], ALU.mult, ALU.add, [ab1, rb, B(("fo", i2))], [B(("fo", i2))])
                sq_ = fss[i2]; sqb = B(("fss", i2))
                memset(DVE, sq_[:, 0:1], 0.0, [sqb])
                act(fsq, fo[i2], AF.Square, [B(("fo", i2)), sqb], [B("fsq"), sqb], accum=sq_[:, 0:1])
                act(sq_[:, 1:2], sq_[:, 0:1], AF.Ln, [sqb, B("epsc")], [sqb], scale=1.0 / 128, bias=epsc)
                act(sq_[:, 1:2], sq_[:, 1:2], AF.Exp, [sqb], [sqb], scale=-0.5)
                stt(DVE, fdt[i2], fo[i2], sq_[:, 1:2], gdrow, ALU.mult, ALU.mult, [B(("fo", i2)), sqb, B("gdrow")], [B(("fdt", i2))])
                tp(psb[1][:, i2 * 128:(i2 + 1) * 128], fdt[i2], identb, [B(("fdt", i2)), B("identb")], [B(("psbt", i2))])
                cp(ACT, fd[jf][:, qb * 128:(qb + 1) * 128], psb[1][:, i2 * 128:(i2 + 1) * 128], [B(("psbt", i2))], [B(("fd", jf))])
            store(mixL[q0 // 1024][256 + h * 128:256 + (h + 1) * 128, q0 % 1024:q0 % 1024 + 512], fd[jf], [B(("fd", jf))], [B(("mixd", h, qt))])
    while pc_jobs:
        precast_one()
    S.barrier()
    apos[0] = p2_mark
    if stage == 2:
        return finish(nc, S, out_d)

    st_q[0] = POOL
    Gz = carve(NCH * 8).rearrange("p (c g) -> p c g", g=8)
    bg = carve(NCH * 8).rearrange("p (c g) -> p c g", g=8)
    load(Gz, gS.rearrange("(c s) g -> s c g", s=128), [], [B("Gz")])
    load(bg, bgate.partition_broadcast(128).rearrange("p o (c g) -> p (o c) g", g=8), [], [B("bg")])
    tt(DVE, Gz, Gz, bg, ALU.add, [B("Gz"), B("bg")], [B("Gz")])
    NS = 4 * NCH
    Lg = carve(NS).rearrange("p (s c) -> p s c", c=NCH)
    for st in range(4):
        fcol = (st // 2) * 4 + 2 + (st % 2)
        act(Lg[:, st, :], Gz[:, :, fcol], AF.Exp, [B("Gz")], [B("Lg")], scale=-1.0)
    act(Lg, Lg, AF.Ln, [B("Lg")], [B("Lg")], bias=1.0)
    Lg2 = Lg.rearrange("p s c -> p (s c)")
    mm(psf[0][:, 0:2 * NCH], triF, Lg2[:, 0:2 * NCH], True, True, [B("triF"), B("Lg")], [B(("psf", 0))])
    mm(psf[0][:, 2 * NCH:NS], triB, Lg2[:, 2 * NCH:NS], True, True, [B("triB"), B("Lg")], [B(("psf", 0))])
    mm(psf[1][:, 0:NS], onesf, Lg2, True, True, [B("onesf"), B("Lg")], [B(("psf", 1))])
    cL = carve(NS).rearrange("p (s c) -> p s c", c=NCH)
    gg = carve(NS).rearrange("p (s c) -> p s c", c=NCH)
    cp(DVE, cL.rearrange("p s c -> p (s c)"), psf[0][:, 0:NS], [B(("psf", 0))], [B("cL")])
    for st in range(4):
        icol = (st // 2) * 4 + (st % 2)
        tt(DVE, gg[:, st, :], Gz[:, :, icol], cL[:, st, :], ALU.add, [B("Gz"), B("cL")], [B("gg")])
    gg2 = gg.rearrange("p s c -> p (s c)")
    Grow = carve(NS); Brow = carve(NS); gmax = carve(4)
    ts(DVE, Brow[0:1, :], psf[1][0:1, 0:NS], -1.0, None, ALU.mult, None, [B(("psf", 1))], [B("Brow")])
    for gi, (a0, a1) in enumerate([(0, 128), (128, 256), (256, NS)]):
        n = a1 - a0
        S.op(PE, (lambda a0, a1, n: lambda e: e.transpose(out=psf[2][0:n, 0:128], in_=gg2[:, a0:a1], identity=identf))(a0, a1, n), reads=[B("gg"), B("identf")], writes=[B(("psf", 2))])
        S.op(DVE, (lambda n, gi: lambda e: e.reduce_max(out=gmax[0:n, gi:gi + 1], in_=psf[2][0:n, 0:128], axis=AX.X))(n, gi), reads=[B(("psf", 2))], writes=[B("gmax")])
        S.op(PE, (lambda a0, a1, n, gi: lambda e: e.transpose(out=psf[3][0:1, a0:a1], in_=gmax[0:n, gi:gi + 1], identity=identf[0:n, 0:n]))(a0, a1, n, gi), reads=[B("gmax"), B("identf")], writes=[B(("psf", 3))])
    cp(DVE, Grow[0:1, :], psf[3][0:1, 0:NS], [B(("psf", 3))], [B("Grow")])
    Gp = carve(NS); Bp = carve(NS); mnx = carve(NS); min_ = carve(NS); Mp = carve(NS); Wp = carve(NS); Mrow = carve(NS); Wrow = carve(NS)

    def rowv(t, st):
        return t[0:1, st * NCH:(st + 1) * NCH]

    for st in range(4):
        for (src, dst, bn) in [(Grow, Gp, "Gp"), (Brow, Bp, "Bp")]:
            if st < 2:
                cp(DVE, rowv(dst, st), rowv(src, st), [B("Grow"), B("Brow")], [B(bn)])
            else:
                cp(DVE, rowv(dst, st)[:, 0:2], rowv(src, st)[:, 1::-1], [B("Grow"), B("Brow")], [B(bn)])
                cp(DVE, rowv(dst, st)[:, 2:NCH], rowv(src, st)[:, NCH - 1:1:-1], [B("Grow"), B("Brow")], [B(bn)])
        S.op(DVE, (lambda st: lambda e: e.tensor_tensor_scan(out=rowv(mnx, st), data0=rowv(Gp, st), data1=rowv(Bp, st), initial=0.0, op0=ALU.max, op1=ALU.add))(st),
             reads=[B("Gp"), B("Bp")], writes=[B("mnx")])
        memset(DVE, rowv(min_, st)[:, 0:1], 0.0, [B("min")])
        cp(DVE, rowv(min_, st)[:, 1:NCH], rowv(mnx, st)[:, 0:NCH - 1], [B("mnx")], [B("min")])
    tt(DVE, Mp[0:1, :], min_[0:1, :], Gp[0:1, :], ALU.max, [B("min"), B("Gp")], [B("Mp")])
    tt(DVE, Wp[0:1, :], min_[0:1, :], Mp[0:1, :], ALU.subtract, [B("min"), B("Mp")], [B("Wp")])
    act(Wp[0:1, :], Wp[0:1, :], AF.Exp, [B("Wp")], [B("Wp")])
    for st in range(4):
        for (src, dst, bn, sb) in [(Mp, Mrow, "Mrow", "Mp"), (Wp, Wrow, "Wrow", "Wp")]:
            if st < 2:
                cp(DVE, rowv(dst, st), rowv(src, st), [B(sb)], [B(bn)])
            else:
                cp(DVE, rowv(dst, st)[:, 0:2], rowv(src, st)[:, 1::-1], [B(sb)], [B(bn)])
                cp(DVE, rowv(dst, st)[:, 2:NCH], rowv(src, st)[:, NCH - 1:1:-1], [B(sb)], [B(bn)])
    mm(psf[4][:, 0:NS], onesf[0:1, :], Mrow[0:1, :], True, True, [B("onesf"), B("Mrow")], [B(("psf", 4))])
    mm(psf[5][:, 0:NS], onesf[0:1, :], Wrow[0:1, :], True, True, [B("onesf"), B("Wrow")], [B(("psf", 5))])
    Wrep = carve(NS).rearrange("p (s c) -> p s c", c=NCH)
    wcol = carve(NS).rearrange("p (s c) -> p s c", c=NCH)
    clamp = carve(NS).rearrange("p (s c) -> p s c", c=NCH)
    cp(DVE, Wrep.rearrange("p s c -> p (s c)"), psf[5][:, 0:NS], [B(("psf", 5))], [B("Wrep")])
    tt(DVE, wcol.rearrange("p s c -> p (s c)"), gg2, psf[4][:, 0:NS], ALU.subtract, [B("gg"), B(("psf", 4))], [B("wcol")])
    act(wcol, wcol, AF.Exp, [B("wcol")], [B("wcol")])
    tt(DVE, clamp.rearrange("p s c -> p (s c)"), cL.rearrange("p s c -> p (s c)"), psf[4][:, 0:NS], ALU.subtract, [B("cL"), B(("psf", 4))], [B("clamp")])
    act(clamp, clamp, AF.Exp, [B("clamp")], [B("clamp")])
    qTl = [carve_bf(T + 1) for _ in range(2)]; kTl = [carve_bf(T + 1) for _ in range(2)]; kTc = [carve_bf(TC + 1) for _ in range(2)]
    vaug = [carve_bf(NCH * 130).rearrange("p (c e) -> p c e", e=130) for _ in range(2)]
    ktok = [carve_bf(NCH * 128).rearrange("p (c e) -> p c e", e=128) for _ in range(2)]
    gmlr = carve(256)
    load(gmlr, gml.partition_broadcast(128), [], [B("gmlr")])
    for h in range(2):
        load(qTl[h], mqT_l[h], [], [B(("qTl", h))], q=POOL)
        load(kTl[h], mkT_l[h], [], [B(("kTl", h))], q=POOL)
        load(kTc[h], mkT_c[h], [], [B(("kTc", h))], q=POOL)
        memset(DVE, vaug[h][:, :, 128:130], 1.0, [B(("vaug", h))])
        load(vaug[h][:, :, 0:128], mvS[h].rearrange("(c s) e -> s c e", s=128), [], [B(("vaug", h))], q=POOL)

    def kchunk(h, cs):
        return kTc[h][:, 1 + cs * 128:1 + (cs + 1) * 128] if cs < 2 else kTl[h][:, 1 + (cs - 2) * 128:1 + (cs - 1) * 128]

    for h in range(2):
        for cs in range(NCH):
            pb = psb[cs % 2]
            tp(pb[:, 0:128], kchunk(h, cs), identb, [B(("kTl", h)), B(("kTc", h)), B("identb")], [B(("psb", cs % 2))])
            cp(ACT if cs % 2 == 0 else DVE, ktok[h][:, cs, :], pb[:, 0:128], [B(("psb", cs % 2))], [B(("ktok", h))])
    Cn = [carve(130) for _ in range(4)]; Cnb = [carve_bf(130) for _ in range(4)]
    NB = 8
    STb = [carve_bf(128) for _ in range(NB)]; kwb = [carve_bf(128) for _ in range(NB)]
    dena = [carve(2) for _ in range(NB)]; hbuf = [carve(128) for _ in range(NB)]; hfb = [carve(128) for _ in range(NB)]; obuf = [carve(128) for _ in range(NB)]
    hsq = carve(128); hss = [carve(2) for _ in range(NB)]; mtok = [carve_bf(128) for _ in range(NB)]; mTb = [carve_bf(128) for _ in range(NB)]
    perm66 = [1, 0] + [NCH + 1 - j for j in range(2, NCH)]
    pos_of = [{cs: cs for cs in range(NCH)}, {perm66[j]: j for j in range(NCH)}]
    for st in range(4):
        memset(DVE, Cn[st], 0.0, [B(("Cn", st))])
        memset(DVE, Cnb[st], 0.0, [B(("Cnb", st))])
    steps = []
    for j in range(NCH):
        for dr in range(2):
            cs = j if dr == 0 else perm66[j]
            for h in range(2):
                n_ = len(steps)
                steps.append(dict(j=j, dr=dr, cs=cs, h=h, st=dr * 2 + h, i2=n_ % 2, i4=n_ % NB, lat=cs >= 2, tl=(cs - 2) * 128,
                                  second=(cs >= 2 and pos_of[1 - dr][cs] < j)))

    def stepA(c):
        if not c["lat"]:
            return
        h, cs, st, i2, i4, tl = c["h"], c["cs"], c["st"], c["i2"], c["i4"], c["tl"]
        mask = maskF if c["dr"] == 0 else maskB
        mkb = B("maskF") if c["dr"] == 0 else B("maskB")
        qch = qTl[h][:, 1 + tl:1 + tl + 128]
        pq = psf[i2]
        mm(pq[:, 0:128], kchunk(h, cs), qch, True, True, [B(("kTl", h)), B(("qTl", h))], [B(("psf", i2))])
        stt(DVE, STb[i4], pq[:, 0:128], wcol[:, st, cs:cs + 1], mask, ALU.mult, ALU.mult, [B(("psf", i2)), B("wcol"), mkb], [B(("STb", i4))])

    def stepB(c):
        h, cs, st, i2, i4, tl, j, dr = c["h"], c["cs"], c["st"], c["i2"], c["i4"], c["tl"], c["j"], c["dr"]
        if c["lat"] and c["second"]:
            load(hfb[i4], hfS[h, tl:tl + 128, :], [B(("hfS", h, tl))], [B(("hfb", i4))])
            load(obuf[i4], oS[h, tl:tl + 128, :], [], [B(("obuf", i4))])
        if c["lat"]:
            qch = qTl[h][:, 1 + tl:1 + tl + 128]
            pn = psf[2 + i2]
            mm(pn[:, 0:129], STb[i4], vaug[h][:, cs, 0:129], True, False, [B(("STb", i4)), B(("vaug", h))], [B(("psf", 2 + i2))])
            mm(pn[:, 0:129], qch, Cnb[st][:, 0:129], False, True, [B(("qTl", h)), B(("Cnb", st))], [B(("psf", 2 + i2))])
            da = dena[i4]
            act(da[:, 0:1], pn[:, 128:129], AF.Abs, [B(("psf", 2 + i2))], [B(("dena", i4))])
            tt(DVE, da[:, 0:1], da[:, 0:1], clamp[:, st, cs:cs + 1], ALU.max, [B(("dena", i4)), B("clamp")], [B(("dena", i4))])
            S.op(DVE, (lambda da: lambda e: e.reciprocal(out=da[:, 1:2], in_=da[:, 0:1]))(da), reads=[B(("dena", i4))], writes=[B(("dena", i4))])
            act(hbuf[i4], pn[:, 0:128], AF.Copy, [B(("psf", 2 + i2)), B(("dena", i4))], [B(("hbuf", i4))], scale=da[:, 1:2])
            if not c["second"]:
                store(hfS[h, tl:tl + 128, :], hbuf[i4], [B(("hbuf", i4))], [B(("hfS", h, tl))])
        if j < NCH - 1:
            pu = psf[4 + i2]
            ts(DVE, kwb[i4], ktok[h][:, cs, :], wcol[:, st, cs:cs + 1], None, ALU.mult, None, [B(("ktok", h)), B("wcol")], [B(("kwb", i4))])
            mm(pu[:, 0:129], kwb[i4], vaug[h][:, cs, 0:129], True, True, [B(("kwb", i4)), B(("vaug", h))], [B(("psf", 4 + i2))])
            stt(DVE, Cn[st][:, 0:129], Cn[st][:, 0:129], Wrep[:, st, cs:cs + 1], pu[:, 0:129], ALU.mult, ALU.add,
                [B(("Cn", st)), B("Wrep"), B(("psf", 4 + i2))], [B(("Cn", st))])
            csn = (j + 1) if dr == 0 else perm66[j + 1]
            act(Cnb[st][:, 0:129], Cn[st][:, 0:129], AF.Copy, [B(("Cn", st)), B("Wrep")], [B(("Cnb", st))], scale=Wrep[:, st, csn:csn + 1])

    def stepC(c):
        if not (c["lat"] and c["second"]):
            return
        h, i2, i4, tl = c["h"], c["i2"], c["i4"], c["tl"]
        hs_ = hss[i4]
        tt(DVE, hbuf[i4], hbuf[i4], hfb[i4], ALU.add, [B(("hbuf", i4)), B(("hfb", i4))], [B(("hbuf", i4))])
        tt(DVE, obuf[i4], obuf[i4], gmlr[:, h * 128:(h + 1) * 128], ALU.mult, [B(("obuf", i4)), B("gmlr")], [B(("obuf", i4))])
        memset(DVE, hs_[:, 0:1], 0.0, [B(("hss", i4))])
        act(hsq, hbuf[i4], AF.Square, [B(("hbuf", i4)), B(("hss", i4))], [B("hsq"), B(("hss", i4))], accum=hs_[:, 0:1])
        act(hs_[:, 1:2], hs_[:, 0:1], AF.Ln, [B(("hss", i4)), B("epsc")], [B(("hss", i4))], scale=1.0 / 128, bias=epsc)
        act(hs_[:, 1:2], hs_[:, 1:2], AF.Exp, [B(("hss", i4))], [B(("hss", i4))], scale=-0.5)
        stt(DVE, mtok[i4], hbuf[i4], hs_[:, 1:2], obuf[i4], ALU.mult, ALU.mult, [B(("hbuf", i4)), B(("hss", i4)), B(("obuf", i4))], [B(("mtok", i4))])
        tp(psb[i2][:, 0:128], mtok[i4], identb, [B(("mtok", i4)), B("identb")], [B(("psb", i2))])
        cp(ACT, mTb[i4], psb[i2][:, 0:128], [B(("psb", i2))], [B(("mTb", i4))])
        store(mixL[tl // 1024][h * 128:(h + 1) * 128, tl % 1024:tl % 1024 + 128], mTb[i4], [B(("mTb", i4))], [B(("mixm", h, tl))])

    NSTEP = len(steps)
    ccs = S.new_dsem("cc")

    def gather_piece(p):
        rd = [B(("mixm", h_, tl_)) for h_ in range(2) for tl_ in range(p * 1024, (p + 1) * 1024, 128)]
        S.dma(POOL, (lambda p: lambda e: e.collective_compute("AllGather", ALU.bypass, replica_groups=[[0, 1, 2, 3], [4, 5, 6, 7]], ins=[mixL[p]], outs=[mixA[p]]))(p),
              ccs, reads=rd, writes=[B(("mixA", p))], inc=1)

    gather_at = {4 * 42 + 3: 3, 4 * 46 + 3: 4, 4 * 50 + 3: 2, 4 * 54 + 3: 5, 4 * 58 + 3: 1, 4 * 62 + 3: 6}
    LA, LC = 2, 5
    for n_ in range(NSTEP + LC):
        if n_ < NSTEP:
            stepA(steps[n_])
        if 0 <= n_ - LA < NSTEP:
            stepB(steps[n_ - LA])
        if 0 <= n_ - LC < NSTEP:
            stepC(steps[n_ - LC])
            if (n_ - LC) in gather_at:
                gather_piece(gather_at[n_ - LC])
    S.barrier()
    apos[0] = p2_mark
    if stage == 3:
        return finish(nc, S, out_d)

    ev_keep = {p: ("d", ccs, k + 1) for k, p in enumerate([3, 4, 2, 5, 1, 6])}
    for p in (0, 7):
        S.dma(POOL, (lambda p: lambda e: e.collective_compute("AllGather", ALU.bypass, replica_groups=[[0, 1, 2, 3], [4, 5, 6, 7]], ins=[mixL[p]], outs=[mixA[p]]))(p),
              ccs, reads=[], writes=[B(("mixA", p))], inc=1)
    for p, ev in ev_keep.items():
        B(("mixA", p)).w = ev

    st_q[0] = POOL
    ga1g = carve(D); ga2g = carve(D)
    mq_mark = apos[0]
    mixq = carve_bf(KC * (TQ + 2)).rearrange("p (k t) -> p k t", t=TQ + 2)
    p5_mark = apos[0]
    v96b = carve(128)
    for j, c0 in enumerate([3 * D, 4 * D]):
        load(v96b[j * 16:(j + 1) * 16, :], modrow[0:1, c0:c0 + D].rearrange("o (k p) -> (o k) p", p=128), [], [B("v96b")])
    S.op(PE, lambda e: e.transpose(out=psf[2][:, 0:32], in_=v96b[0:32, :], identity=identf[0:32, 0:32]), reads=[B("v96b"), B("identf")], writes=[B(("psf", 2))])
    cp(DVE, fmv[:, 64:96], psf[2][:, 0:32], [B(("psf", 2))], [B("fmv")])
    stt(DVE, A2, SC2, 1.0, gTs[:, 1, :], ALU.add, ALU.mult, [B("fmv"), B("gTs")], [B("A2")])
    grt = carve(D)
    for (dst, c0, row, nm) in [(ga1g, 2 * D, 0, "ga1g"), (ga2g, 5 * D, 1, "ga2g")]:
        load(dst, modrow[0:1, c0:c0 + D].partition_broadcast(128), [], [B(nm)])
        load(grt, grow[row:row + 1, :].partition_broadcast(128), [], [B("grt")])
        tt(DVE, dst, dst, grt, ALU.mult, [B(nm), B("grt")], [B(nm)])
    tmpq = [carve_bf(4 * (TQ + 2)).rearrange("p (k t) -> p k t", t=TQ + 2) for _ in range(2)]
    mall_v = [m_.rearrange("(k p) t -> p k t", p=128) for m_ in mixA]
    i = 0
    for qq in range(4):
        for kg in range(4):
            dstv = mixq[:, kg * 4:(kg + 1) * 4, :]
            tq_ = tmpq[i % 2]; tqb = B(("tmpq", i % 2)); ks = slice(kg * 4, (kg + 1) * 4)
            load(tq_[:, :, 1:1025], mall_v[2 * qq][:, ks, :], [B(("mixA", 2 * qq))], [tqb])
            load(tq_[:, :, 1025:2049], mall_v[2 * qq + 1][:, ks, :], [B(("mixA", 2 * qq + 1))], [tqb])
            pl, cl = (2 * qq - 1, 1023) if qq >= 1 else (0, 0)
            load(tq_[:, :, 0:1], mall_v[pl][:, ks, cl:cl + 1], [B(("mixA", pl))], [tqb])
            pr, cr = (2 * qq + 2, 0) if qq <= 2 else (7, 1023)
            load(tq_[:, :, 2049:2050], mall_v[pr][:, ks, cr:cr + 1], [B(("mixA", pr))], [tqb])
            if qq == 0:
                ts(DVE, dstv, tq_, qsel[:, 0:1], None, ALU.mult, None, [tqb, B("qsel")], [B(("mixq", kg))])
            else:
                stt(DVE, dstv, tq_, qsel[:, qq:qq + 1], dstv, ALU.mult, ALU.add, [tqb, B("qsel"), B(("mixq", kg))], [B(("mixq", kg))])
            i += 1
    S.barrier()
    apos[0] = p5_mark
    mixh = carve_bf(KC * 2).rearrange("p (k t) -> p k t", t=2)
    cp(DVE, mixh[:, :, 0:1], mixq[:, :, 0:1], [B("mixq")], [B("mixh")])
    cp(DVE, mixh[:, :, 1:2], mixq[:, :, TQ + 1:TQ + 2], [B("mixq")], [B("mixh")])
    wo = carve_bf(KC * D).rearrange("p (k c) -> p k c", c=D)
    wo_sem = S.new_dsem("wo")
    w_out_v = w_out.rearrange("(k p) c -> p k c", p=128)
    for k in range(KC):
        dma(POOL, wo[:, k, :], w_out_v[:, k, :], wo_sem, [], [B("wo")], new_gen=(k == 0))
    xt = [carve(D) for _ in range(2)]; xm = carve(D); ytmp = [carve(512) for _ in range(2)]; junk = carve_bf(D)
    xn2 = carve_bf(D); h2blk = [carve_bf(KC * 128).rearrange("p (k t) -> p k t", t=128) for _ in range(2)]
    ssq = carve(8); s1 = carve(4)
    h2S_v = h2S.rearrange("(k p) t -> p k t", p=128)
    for tb in range(17):
        nt = 128 if tb < 16 else 2
        x_ = xt[tb % 2]; xb_ = B(("xt", tb % 2))
        if tb < 16:
            load(x_, xq[1 + tb * 128:1 + (tb + 1) * 128, :], [], [xb_])
        else:
            load(x_[0:1, :], xq[0:1, :], [], [xb_])
            load(x_[1:2, :], xq[TQ + 1:TQ + 2, :], [], [xb_])
        for ct in range(4):
            for k in range(KC):
                lhs = mixq[:, k, 1 + tb * 128:1 + (tb + 1) * 128] if tb < 16 else mixh[:, k, :]
                mm(psf[ct][0:nt, :], lhs, wo[:, k, ct * 512:(ct + 1) * 512], k == 0, k == KC - 1, [B("mixq"), B("mixh"), B("wo")], [B(("psf", ct))])
        memset(DVE, ssq[:, 0:4], 0.0, [B("ssq")])
        for ct in range(4):
            act(junk[0:nt, 0:512], psf[ct][0:nt, :], AF.Square, [B(("psf", ct)), B("ssq")], [B("junk"), B("ssq")], accum=ssq[0:nt, ct:ct + 1])
        S.op(DVE, (lambda nt: lambda e: e.reduce_sum(out=s1[0:nt, 0:1], in_=ssq[0:nt, 0:4], axis=AX.X))(nt), reads=[B("ssq")], writes=[B("s1")])
        act(s1[0:nt, 1:2], s1[0:nt, 0:1], AF.Ln, [B("s1"), B("epsc")], [B("s1")], scale=1.0 / D, bias=epsc[0:nt, :])
        act(s1[0:nt, 1:2], s1[0:nt, 1:2], AF.Exp, [B("s1")], [B("s1")], scale=-0.5)
        for ct in range(4):
            yt = ytmp[ct % 2]; yb = B(("ytmp", ct % 2))
            stt(DVE, yt[0:nt, :], psf[ct][0:nt, :], s1[0:nt, 1:2], ga1g[0:nt, ct * 512:(ct + 1) * 512], ALU.mult, ALU.mult, [B(("psf", ct)), B("s1"), B("ga1g")], [yb])
            tt(DVE, xm[0:nt, ct * 512:(ct + 1) * 512], yt[0:nt, :], x_[0:nt, ct * 512:(ct + 1) * 512], ALU.add, [yb, xb_], [B("xm")])
        if tb < 16:
            store(xmidS[tb * 128:(tb + 1) * 128, :], xm, [B("xm")], [B(("xmidS", tb))])
        memset(DVE, ssq[:, 4:5], 0.0, [B("ssq")])
        act(junk[0:nt, :], xm[0:nt, :], AF.Square, [B("xm"), B("ssq")], [B("junk"), B("ssq")], accum=ssq[0:nt, 4:5])
        act(s1[0:nt, 2:3], ssq[0:nt, 4:5], AF.Ln, [B("ssq"), B("epsc")], [B("s1")], scale=1.0 / D, bias=epsc[0:nt, :])
        act(s1[0:nt, 2:3], s1[0:nt, 2:3], AF.Exp, [B("s1")], [B("s1")], scale=-0.5)
        ts(DVE, xn2[0:nt, :], xm[0:nt, :], s1[0:nt, 2:3], None, ALU.mult, None, [B("xm"), B("s1")], [B("xn2")])
        hb_ = h2blk[tb % 2]; hbb = B(("h2blk", tb % 2))
        for g in range(2):
            pb = psb[g]
            for kk in range(8):
                k = g * 8 + kk
                S.op(PE, (lambda pb, kk, k, nt: lambda e: e.transpose(out=pb[:, kk * 128:kk * 128 + nt], in_=xn2[0:nt, k * 128:(k + 1) * 128], identity=identb[0:nt, 0:nt]))(pb, kk, k, nt),
                     reads=[B("xn2"), B("identb")], writes=[B(("psb", g))])
            for kk in range(8):
                k = g * 8 + kk
                act(hb_[:, k, 0:nt], pb[:, kk * 128:kk * 128 + nt], AF.Identity, [B(("psb", g)), B("A2"), B("fmv")], [hbb], scale=A2[:, k:k + 1], bias=SH2[:, k:k + 1])
        if tb < 16:
            store(h2S_v[:, :, 1 + tb * 128:1 + (tb + 1) * 128], hb_, [hbb], [B(("h2S", tb))])
        else:
            ts(DVE, hb_[:, :, 0:1], hb_[:, :, 0:1], hmask[:, 0:1], None, ALU.mult, None, [hbb, B("hmask")], [hbb])
            ts(DVE, hb_[:, :, 1:2], hb_[:, :, 1:2], hmask[:, 1:2], None, ALU.mult, None, [hbb, B("hmask")], [hbb])
            store(h2S_v[:, :, 0:1], hb_[:, :, 0:1], [hbb], [B(("h2S", 16))])
            store(h2S_v[:, :, TQ + 1:TQ + 2], hb_[:, :, 1:2], [hbb], [B(("h2S", 17))])
    S.barrier()
    apos[0] = mq_mark
    if stage == 4:
        return finish(nc, S, out_d)
    st_q[0] = SP
    cfs = carve(2 * HC * 4).rearrange("p (c j) -> p c j", j=4)
    load(cfs, cfw, [], [B("cfs")])
    h2t = [carve_bf(KC * 514).rearrange("p (k t) -> p k t", t=514) for _ in range(2)]
    gTt = carve_bf(HC * 512).rearrange("p (j t) -> p j t", t=512)
    wu = [carve_bf(KC * 256).rearrange("p (k c) -> p k c", c=256) for _ in range(2)]
    wu_sem = [S.new_dsem(f"wu{i}") for i in range(2)]
    wd = [carve_bf(HC * 128).rearrange("p (j c) -> p j c", c=128) for _ in range(2)]
    wd_sem = [S.new_dsem(f"wd{i}") for i in range(2)]
    wuc_sem = [S.new_dsem(f"wuc{i}") for i in range(2)]
    wdc_sem = [S.new_dsem(f"wdc{i}") for i in range(2)]
    y2 = [carve(D) for _ in range(4)]
    ub = [carve(520) for _ in range(2)]; tb_ = [carve(512) for _ in range(2)]; sgb = carve(512)
    xmr = carve(D)
    junkc = carve_bf(D); ssqc = carve(8); s1c = carve(4)
    w_up_v = w_up.rearrange("(k p) c -> p k c", p=128)
    w_dn_v = w_down.rearrange("(j p) c -> p j c", p=128)
    wi = 0; di = 0
    for tt_ in range(4):
        h2 = h2t[tt_ % 2]; h2b = B(("h2t", tt_ % 2))
        load(h2, h2S_v[:, :, tt_ * 512:tt_ * 512 + 514], [B(("h2S", x)) for x in range(18)], [h2b])
        for j in range(HC):
            w = wu[wi % 2]; wb = B(("wu", wi % 2)); ws = wu_sem[wi % 2]; wi += 1
            dma(SP, w.rearrange("p k c -> p (k c)"), wuS[j], wuc_sem[(wi - 1) % 2], [], [wb])
            for part in range(2):
                pa = psf[part * 2]; pk = psf[part * 2 + 1]
                for k in range(KC):
                    mm(pa[:, 0:512], w[:, k, part * 128:(part + 1) * 128], h2[:, k, 0:512], k == 0, k == KC - 1, [wb, h2b], [B(("psf", part * 2))])
                for k in range(KC):
                    mm(pk[:, 0:2], w[:, k, part * 128:(part + 1) * 128], h2[:, k, 512:514], k == 0, k == KC - 1, [wb, h2b], [B(("psf", part * 2 + 1))])
                u = ub[part]; ubb = B(("ub", part))
                cp(ACT, u[:, 0:512], pa[:, 0:512], [B(("psf", part * 2))], [ubb])
                cp(ACT, u[:, 512:514], pk[:, 0:2], [B(("psf", part * 2 + 1))], [ubb])
                cw = cfs[:, part * HC + j, :]
                t_ = tb_[part]; tbb = B(("tb", part))
                ts(DVE, t_, u[:, 1:513], cw[:, 1:2], cw[:, 3:4], ALU.mult, ALU.add, [ubb, B("cfs")], [tbb])
                stt(DVE, t_, u[:, 0:512], cw[:, 0:1], t_, ALU.mult, ALU.add, [ubb, B("cfs"), tbb], [tbb])
                stt(DVE, t_, u[:, 2:514], cw[:, 2:3], t_, ALU.mult, ALU.add, [ubb, B("cfs"), tbb], [tbb])
            act(sgb, tb_[0], AF.Silu, [B(("tb", 0))], [B("sgb")])
            tt(DVE, gTt[:, j, :], sgb, tb_[1], ALU.mult, [B("sgb"), B(("tb", 1))], [B("gTt")])
            if stage == 6:
                d_h2 = nc.dram_tensor("d_h2", [128, KC, 514], BF16, kind="ExternalOutput").ap()
                d_wu = nc.dram_tensor("d_wu", [128, KC, 256], BF16, kind="ExternalOutput").ap()
                d_ub = nc.dram_tensor("d_ub", [2, 128, 514], F32, kind="ExternalOutput").ap()
                d_tb = nc.dram_tensor("d_tb", [3, 128, 512], F32, kind="ExternalOutput").ap()
                d_cf = nc.dram_tensor("d_cf", [128, 2 * HC, 4], F32, kind="ExternalOutput").ap()
                store(d_h2, h2, [h2b], [B("d1")]); store(d_wu, w, [wb], [B("d2")])
                store(d_ub[0], ub[0][:, 0:514], [B(("ub", 0))], [B("d3")]); store(d_ub[1], ub[1][:, 0:514], [B(("ub", 1))], [B("d4")])
                store(d_tb[0], tb_[0], [B(("tb", 0))], [B("d5")]); store(d_tb[1], tb_[1], [B(("tb", 1))], [B("d6")]); store(d_tb[2], sgb, [B("sgb")], [B("d7")])
                store(d_cf, cfs, [B("cfs")], [B("d8")])
                S.barrier()
                return finish(nc, S, out_d)
        if stage == 5 and tt_ == 0:
            gdbg = nc.dram_tensor("gdbg", [128, HC, 512], BF16, kind="ExternalOutput").ap()
            store(gdbg, gTt, [B("gTt")], [B("gdbg")])
        for cth in range(D // 128):
            wdd = wd[di % 2]; wdb = B(("wd", di % 2)); wds = wd_sem[di % 2]; di += 1
            dma(SP, wdd.rearrange("p j c -> p (j c)"), wdS[cth], wdc_sem[(di - 1) % 2], [], [wdb])
            for blk in range(4):
                pi = (cth * 4 + blk) % 4
                for j in range(HC):
                    mm(psf[pi][:, 0:128], gTt[:, j, blk * 128:(blk + 1) * 128], wdd[:, j, :], j == 0, j == HC - 1, [B("gTt"), wdb], [B(("psf", pi))])
                cp(ACT if blk % 2 == 0 else DVE, y2[blk][:, cth * 128:(cth + 1) * 128], psf[pi][:, 0:128], [B(("psf", pi))], [B(("y2", blk))])
        if stage == 5 and tt_ == 0:
            ydbg = nc.dram_tensor("ydbg", [4, 128, D], F32, kind="ExternalOutput").ap()
            for blk in range(4):
                store(ydbg[blk], y2[blk], [B(("y2", blk))], [B(("ydbg", blk))])
            S.barrier()
            return finish(nc, S, out_d)
        for blk in range(4):
            r0 = tt_ * 512 + blk * 128
            load(xmr, xmidS[r0:r0 + 128, :], [B(("xmidS", r0 // 128))], [B("xmr")])
            memset(DVE, ssqc[:, 5:6], 0.0, [B("ssq")])
            act(junkc, y2[blk], AF.Square, [B(("y2", blk)), B("ssq")], [B("junk"), B("ssq")], accum=ssqc[:, 5:6])
            act(s1c[:, 3:4], ssqc[:, 5:6], AF.Ln, [B("ssq"), B("epsc")], [B("s1")], scale=1.0 / D, bias=epsc)
            act(s1c[:, 3:4], s1c[:, 3:4], AF.Exp, [B("s1")], [B("s1")], scale=-0.5)
            stt(DVE, y2[blk], y2[blk], s1c[:, 3:4], ga2g, ALU.mult, ALU.mult, [B(("y2", blk)), B("s1"), B("ga2g")], [B(("y2", blk))])
            tt(POOL, y2[blk], y2[blk], xmr, ALU.add, [B(("y2", blk)), B("xmr")], [B(("y2", blk))])
            store(out_d[r0:r0 + 128, :], y2[blk], [B(("y2", blk))], [B(("out", r0))])
    S.barrier()
    return finish(nc, S, out_d)


def finish(nc, S, out_d):
    S.finalize()
    return nc


_PROG = {}


def _consts():
    c = {}
    c["identf"] = np.eye(128, dtype=np.float32)
    s = np.arange(128)
    c["maskF"] = (s[:, None] <= s[None, :]).astype(np.float32)
    c["maskB"] = (s[:, None] >= s[None, :]).astype(np.float32)
    c["triF"] = (s[:, None] <= s[None, :]).astype(np.float32)
    c["triB"] = (s[:, None] >= s[None, :]).astype(np.float32)
    partner = np.where((s % 32) < 16, s + 16, s - 16)
    pm = np.zeros((128, 128), np.float32)
    pm[partner, s] = 1.0
    c["perm"] = pm
    perm66 = np.array([1, 0] + [NCH + 1 - j for j in range(2, NCH)])
    jb = np.zeros((NCH, NCH), np.float32)
    jb[perm66, np.arange(NCH)] = 1.0
    c["jb"] = jb
    rows = T // 64
    row = np.repeat(np.arange(rows, dtype=np.float32), 64)
    col = np.tile(np.arange(64, dtype=np.float32), rows)
    inv = (10000.0 ** (-np.arange(16, dtype=np.float32) / 16)).astype(np.float32)
    d = np.arange(64)
    axis = d // 32; half = (d // 16) % 2; f = d % 16
    pos = np.where(axis[:, None] == 0, row[None, :], col[None, :]).astype(np.float32)
    ang = (pos * inv[f][:, None]).astype(np.float32)
    cos = np.cos(ang).astype(np.float32); sin = np.sin(ang).astype(np.float32)
    sgn = np.where(half == 0, -1.0, 1.0).astype(np.float32)[:, None]
    c["cos"] = np.ascontiguousarray(np.concatenate([cos, cos], 0))
    c["sin"] = np.ascontiguousarray(np.concatenate([sin * sgn, sin * sgn], 0))
    return c


def _core_inputs(core, inp, consts):
    b, r = core // 4, core % 4
    h0 = 2 * r
    f32 = np.float32
    m = dict(consts)
    x = inp["x"]
    m["xb"] = np.ascontiguousarray(x[b])
    m["ctxb"] = np.ascontiguousarray(inp["ctx"][b])
    xq = np.zeros((TQ + 2, D), f32)
    lo, hi = r * TQ - 1, r * TQ + TQ + 1
    slo, shi = max(lo, 0), min(hi, T)
    xq[slo - lo:shi - lo] = x[b, slo:shi]
    m["xq"] = xq
    cc = np.stack([inp["c"][b], inp["c_ctx"]], -1)
    m["cT"] = np.ascontiguousarray(cc.reshape(KC, 128, 2).transpose(1, 0, 2))
    m["w_mod"] = inp["w_mod"][0]
    m["b_mod"] = inp["b_mod"][0][None, :]
    gt = np.stack([inp["g_pre_mix"][0], inp["g_pre_ffn"][0]], 0)
    m["gT"] = np.ascontiguousarray(gt.reshape(2, KC, 128).transpose(2, 0, 1))
    m["grow"] = np.stack([inp["g_post_mix"][0], inp["g_post_ffn"][0]], 0)
    w = inp["w_in"][0]
    OMQ, OMK, OMV, OMO, OMG = 0, 1024, 2048, 3072, 4096
    ODQ = OMG + 32; ODK = ODQ + 1024; ODV = ODK + 1024
    cols = []
    for base in (OMQ, OMK, ODQ, ODK, OMO, OMV):
        cols += list(range(base + h0 * 128, base + h0 * 128 + 256))
    gcols = [OMG + g * 8 + h0 + hh for g in range(4) for hh in range(2)]
    cols += gcols
    cols += list(range(ODV + h0 * 128, ODV + h0 * 128 + 256))
    m["w_in"] = np.ascontiguousarray(w[:, cols])
    cw = inp["conv_qk_w"][0]; cb = inp["conv_qk_b"][0]
    convw = np.zeros((128, 4, 4), f32)
    for ci, base in enumerate([h0 * 128, h0 * 128 + 128, 1024 + h0 * 128, 1024 + h0 * 128 + 128]):
        convw[:, ci, 0:3] = cw[:, base:base + 128].T
        convw[:, ci, 3] = cb[base:base + 128]
    m["convw"] = convw
    bg = inp["b_gate"][0]
    m["bgate"] = np.tile(np.array([[bg[g, h0 + hh] for g in range(4) for hh in range(2)]], f32), (1, NCH))
    m["gml"] = np.ascontiguousarray(inp["g_mlstm"][0][h0 * 128:h0 * 128 + 256][None, :])
    m["gdf"] = np.ascontiguousarray(inp["g_diff"][0][:, None])
    m["gdr"] = np.ascontiguousarray(inp["g_diff"][0][None, :])
    m["lamv"] = np.concatenate([inp["lambda_q1"][0], inp["lambda_k1"][0], inp["lambda_q2"][0], inp["lambda_k2"][0]])[None, :].astype(f32)
    rows = []
    for rr in range(4):
        rows += list(range(2 * rr * 128, 2 * rr * 128 + 256)) + list(range(1024 + 2 * rr * 128, 1024 + 2 * rr * 128 + 256))
    m["w_out"] = np.ascontiguousarray(inp["w_out"][0][rows, :])
    m["w_up"] = inp["w_up"][0]
    fw = inp["conv_ffn_w"][0]; fb = inp["conv_ffn_b"][0]
    cf = np.concatenate([fw.T, fb[:, None]], 1)
    m["cfw"] = np.ascontiguousarray(cf.reshape(2 * HC, 128, 4).transpose(1, 0, 2))
    m["w_down"] = inp["w_down"][0]
    hm = np.zeros((128, 2), f32)
    hm[:, 0] = 1.0 if r > 0 else 0.0
    hm[:, 1] = 1.0 if r < 3 else 0.0
    m["hmask"] = hm
    qs = np.zeros((128, 4), f32); qs[:, r] = 1.0
    m["qsel"] = qs
    return {k: np.ascontiguousarray(v, dtype=v.dtype) for k, v in m.items()}


def kernel(**inputs):
    inp = {k: np.asarray(v) for k, v in inputs.items()}
    stage = int(os.environ.get("MK_STAGE", "99"))
    if stage not in _PROG:
        _PROG[stage] = build_program(stage)
    nc = _PROG[stage]
    consts = _consts()
    maps = [_core_inputs(c, inp, consts) for c in range(8)]
    res = run_bass_kernel_spmd(nc, maps, core_ids=list(range(8)))
    if stage < 99:
        return res
    out = np.zeros((2, T, D), np.float32)
    for c in range(8):
        b, r = c // 4, c % 4
        out[b, r * TQ:(r + 1) * TQ] = res.results[c]["out"]
    return out
```

```python
import os
import numpy as np
import ml_dtypes
import concourse.bass as bass
import concourse.mybir as mybir
from concourse.bass_utils import run_bass_kernel_spmd

F32 = mybir.dt.float32
BF16 = mybir.dt.bfloat16
AF = mybir.ActivationFunctionType
ALU = mybir.AluOpType
AX = mybir.AxisListType

PE, ACT, DVE, POOL, SP = "pe", "act", "dve", "pool", "sp"
ENGS = [PE, ACT, DVE, POOL, SP]

D = 2048
KC = 16
T = 8192
TC = 256
TA = T + TC
NCH = TA // 128
DFF = 5632
HC = DFF // 128
EPS = 1e-6
NCOL = 1800
C_MQ, C_MK, C_DQ, C_DK, C_MO, C_MV, C_G, C_DV = 0, 256, 512, 768, 1024, 1280, 1536, 1544
TQ = 2048
LAM_INIT = 0.2


class Buf:
    __slots__ = ("name", "w", "r")

    def __init__(self, name=""):
        self.name = name
        self.w = None
        self.r = []


class DSem:
    def __init__(self, sem):
        self.sem = sem
        self.total = 0


class Sched:
    def __init__(self, nc):
        self.nc = nc
        self.ops = {e: [] for e in ENGS}
        self.dsems = []
        self.bufs = {}

    def buf(self, key):
        b = self.bufs.get(key)
        if b is None:
            b = Buf(str(key))
            self.bufs[key] = b
        return b

    def new_dsem(self, name):
        d = DSem(self.nc.alloc_semaphore(name))
        self.dsems.append(d)
        return d

    def _deps(self, eng, reads, writes, dsem=None, is_dma=False):
        deps = []
        for b in reads:
            if b.w is not None:
                deps.append(b.w)
        for b in writes:
            if b.w is not None:
                if not (dsem is not None and b.w[0] == "d" and b.w[1] is dsem):
                    deps.append(b.w)
            deps.extend(b.r)
        out = []
        for d in deps:
            if d[0] == "e" and d[1] == eng and not is_dma:
                if eng == PE:
                    continue
                if not any((b.w is d) for b in reads):
                    continue
            out.append(d)
        return out

    def op(self, eng, fn, reads=(), writes=()):
        deps = self._deps(eng, reads, writes)
        idx = len(self.ops[eng])
        ev = ("e", eng, idx)
        self.ops[eng].append(dict(fn=fn, deps=deps, kind="c", flag=False))
        for b in reads:
            b.r = [x for x in b.r if not (x[0] == "e" and x[1] == eng)]
            b.r.append(ev)
        for b in writes:
            b.w = ev
            b.r = []
        return ev

    def dma(self, q, fn, dsem, reads=(), writes=(), new_gen=True, inc=16):
        deps = self._deps(q, reads, writes, dsem=dsem, is_dma=True)
        if new_gen and dsem.total > 0:
            deps.append(("d", dsem, dsem.total))
        dsem.total += inc
        ev = ("d", dsem, dsem.total)
        self.ops[q].append(dict(fn=fn, deps=deps, kind="d", dsem=dsem, flag=False, inc=inc))
        for b in reads:
            b.r = [x for x in b.r if not (x[0] == "d" and x[1] is dsem)]
            b.r.append(ev)
        for b in writes:
            b.w = ev
            b.r = []
        return ev

    def barrier(self):
        evs = []
        for e in ENGS:
            for i in range(len(self.ops[e]) - 1, -1, -1):
                if self.ops[e][i]["kind"] == "c":
                    evs.append(("e", e, i))
                    break
        for d in self.dsems:
            if d.total > 0:
                evs.append(("d", d, d.total))
        for e in ENGS:
            self.ops[e].append(dict(fn=None, deps=[x for x in evs if not (x[0] == "e" and x[1] == e)], kind="w", flag=False))
        for b in self.bufs.values():
            b.w = None
            b.r = []

    def wait_all(self, eng, evs):
        self.ops[eng].append(dict(fn=None, deps=list(evs), kind="w", flag=False))

    def finalize(self):
        nc = self.nc
        ops = self.ops
        EPOCH = 30000
        for e in ENGS:
            for o in ops[e]:
                for d in o["deps"]:
                    if d[0] == "e":
                        ops[d[1]][d[2]]["flag"] = True
        val = {}
        esem = {}
        for e in ENGS:
            c = 0
            for i, o in enumerate(ops[e]):
                if o["kind"] == "c" and o["flag"]:
                    val[(e, i)] = (c // EPOCH, c % EPOCH + 1)
                    c += 1
            esem[e] = [nc.alloc_semaphore(f"es_{e}{k}") for k in range(c // EPOCH + 1)]

        def replay(e, eng):
            seen = {}
            for oi_, o in enumerate(ops[e]):
                o["idx"] = oi_
                need = {}
                for d in o["deps"]:
                    if d[0] == "e":
                        key = ("e", d[1])
                        v = val[(d[1], d[2])]
                        sem = esem[d[1]][v[0]]
                    else:
                        key = ("d", id(d[1]))
                        v = (0, d[2])
                        sem = d[1].sem
                    if seen.get(key, (0, 0)) >= v:
                        continue
                    if key not in need or need[key][1] < v:
                        need[key] = (sem, v)
                for key, (sem, v) in need.items():
                    eng.wait_ge(sem, v[1])
                    seen[key] = v
                if o["fn"] is None:
                    continue
                ins = o["fn"](eng)
                if o["kind"] == "d":
                    ins.then_inc(o["dsem"].sem, o["inc"])
                elif o["flag"]:
                    ins.then_inc(esem[e][val[(e, o["idx"])][0]], 1)

        with nc.Block() as block:
            @block.tensor
            def _(eng):
                replay(PE, eng)

            @block.scalar
            def _(eng):
                replay(ACT, eng)

            @block.vector
            def _(eng):
                replay(DVE, eng)

            @block.gpsimd
            def _(eng):
                replay(POOL, eng)

            @block.sync
            def _(eng):
                replay(SP, eng)


def build_program(stage=99):
    nc = bass.Bass("TRN2", target_bir_lowering=False)
    S = Sched(nc)
    B = S.buf

    def IN(name, shape, dt=F32):
        return nc.dram_tensor(name, shape, dt, kind="ExternalInput").ap()

    def SCR(name, shape, dt=F32):
        kind = "ExternalOutput" if (stage < 99 and name in DBG_OUT.get(stage, ())) else "Internal"
        return nc.dram_tensor(name, shape, dt, kind=kind).ap()

    xb = IN("xb", [T, D]); ctxb = IN("ctxb", [TC, D]); xq = IN("xq", [TQ + 2, D])
    cT = IN("cT", [128, KC, 2]); w_mod = IN("w_mod", [D, 6 * D]); b_mod = IN("b_mod", [1, 6 * D])
    gT = IN("gT", [128, 2, KC]); grow = IN("grow", [2, D])
    w_in = IN("w_in", [D, NCOL]); convw = IN("convw", [128, 4, 4]); bgate = IN("bgate", [1, 8 * NCH])
    gml = IN("gml", [1, 256]); gdf = IN("gdf", [128, 1]); gdr = IN("gdr", [1, 128]); lamv = IN("lamv", [1, 256])
    w_out = IN("w_out", [D, D]); w_up = IN("w_up", [D, 2 * DFF]); cfw = IN("cfw", [128, 2 * HC, 4]); w_down = IN("w_down", [DFF, D])
    identf_d = IN("identf", [128, 128]); maskF_d = IN("maskF", [128, 128]); maskB_d = IN("maskB", [128, 128])
    triF_d = IN("triF", [128, 128]); triB_d = IN("triB", [128, 128]); perm_d = IN("perm", [128, 128]); jb_d = IN("jb", [NCH, NCH])
    cos_d = IN("cos", [128, T]); sin_d = IN("sin", [128, T]); hmask_d = IN("hmask", [128, 2]); qsel_d = IN("qsel", [128, 4])
    out_d = nc.dram_tensor("out", [TQ, D], F32, kind="ExternalOutput").ap()

    DBG_OUT = {1: ("modrow", "mqT_l", "mkT_c", "oS", "mvS", "gS", "dqT", "dkT", "dvS"), 2: tuple(f"mixL{p}" for p in range(8)), 3: tuple(f"mixL{p}" for p in range(8)), 4: ("h2S", "xmidS"), 6: ("h2S", "xmidS")}
    modrow = SCR("modrow", [2, 6 * D])
    mqT_c = SCR("mqT_c", [2, 128, TC + 1], BF16); mqT_l = SCR("mqT_l", [2, 128, T + 1], BF16)
    mkT_c = SCR("mkT_c", [2, 128, TC + 1], BF16); mkT_l = SCR("mkT_l", [2, 128, T + 1], BF16)
    oS = SCR("oS", [2, T, 128]); mvS = SCR("mvS", [2, TA, 128], BF16); gS = SCR("gS", [TA, 8])
    dqT = SCR("dqT", [2, 128, T], BF16); dkT = SCR("dkT", [2, 128, TA], BF16); dvS = SCR("dvS", [2, TA, 128], BF16)
    hfS = SCR("hfS", [2, T, 128])
    mixL = [SCR(f"mixL{p}", [512, 1024], BF16) for p in range(8)]
    mixA = [SCR(f"mixA{p}", [2048, 1024], BF16) for p in range(8)]
    xmidS = SCR("xmidS", [TQ, D])
    h2S = SCR("h2S", [D, TQ + 2], BF16)
    wuS = SCR("wuS", [HC, 128, KC * 256], BF16)
    wdS = SCR("wdS", [D // 128, 128, HC * 128], BF16)

    big = nc.alloc_sbuf_tensor("big", [128, 52000], F32)
    apos = [0]

    def carve(n, dt=F32):
        o = apos[0]
        apos[0] += (n + 7) // 8 * 8
        assert apos[0] <= 52000, apos[0]
        v = big[:, o:o + n]
        return v if dt is F32 else v.bitcast(dt)

    def carve_bf(nel):
        return carve((nel + 1) // 2, BF16)[:, 0:nel]

    psp = [nc.alloc_psum_tensor(f"psp{i}", [128, 1024], F32) for i in range(4)]
    psf = [psp[0][:, 0:512], psp[0][:, 512:1024], psp[1][:, 0:512], psp[1][:, 512:1024], psp[2][:, 0:512], psp[2][:, 512:1024]]
    psb = [psp[3][:, 0:512].bitcast(BF16), psp[3][:, 512:1024].bitcast(BF16)]

    def mm(out, lhsT, rhs, start, stop, R, W):
        S.op(PE, lambda e: e.matmul(out, lhsT=lhsT, rhs=rhs, start=start, stop=stop), reads=R, writes=W)

    def tp(out, in_, ident, R, W):
        S.op(PE, lambda e: e.transpose(out=out, in_=in_, identity=ident), reads=R, writes=W)

    def act(out, in_, func, R, W, bias=None, scale=None, accum=None, eng=ACT):
        kw = {}
        if bias is not None:
            kw["bias"] = bias
        if scale is not None:
            kw["scale"] = scale
        if accum is not None:
            kw["accum_out"] = accum
        S.op(eng, lambda e: e.activation(out=out, in_=in_, func=func, **kw), reads=R, writes=W)

    def cp(eng, out, in_, R, W):
        if eng == ACT:
            S.op(ACT, lambda e: e.copy(out=out, in_=in_), reads=R, writes=W)
        else:
            S.op(eng, lambda e: e.tensor_copy(out=out, in_=in_), reads=R, writes=W)

    def tt(eng, out, a, b, op, R, W):
        S.op(eng, lambda e: e.tensor_tensor(out=out, in0=a, in1=b, op=op), reads=R, writes=W)

    def ts(eng, out, a, s1, s2, op0, op1, R, W):
        if s2 is None:
            S.op(eng, lambda e: e.tensor_scalar(out=out, in0=a, scalar1=s1, scalar2=None, op0=op0), reads=R, writes=W)
        else:
            S.op(eng, lambda e: e.tensor_scalar(out=out, in0=a, scalar1=s1, scalar2=s2, op0=op0, op1=op1), reads=R, writes=W)

    def stt(eng, out, a, sc, b, op0, op1, R, W):
        S.op(eng, lambda e: e.scalar_tensor_tensor(out=out, in0=a, scalar=sc, in1=b, op0=op0, op1=op1), reads=R, writes=W)

    def memset(eng, out, v, W):
        S.op(eng, lambda e: e.memset(out, v), writes=W)

    def dma(q, out, in_, dsem, R, W, new_gen=True):
        return S.dma(q, lambda e: e.dma_start(out=out, in_=in_, allow_slow_non_contiguous=True), dsem, reads=R, writes=W, new_gen=new_gen)

    st_sems = [S.new_dsem(f"st{i}") for i in range(8)]
    st_i = [0]

    stp_sems = [S.new_dsem(f"stp{i}") for i in range(8)]
    st_q = [SP]

    def store(out, in_, R, W, q=None):
        q = q or st_q[0]
        pool_ = st_sems if q == SP else stp_sems
        d = pool_[st_i[0] % len(pool_)]
        st_i[0] += 1
        return dma(q, out, in_, d, R, W)

    ld_sems = [S.new_dsem(f"ld{i}") for i in range(8)]
    ld_i = [0]

    ldp_sems = [S.new_dsem(f"ldp{i}") for i in range(4)]

    def load(out, in_, R, W, q=SP):
        pool_ = ld_sems if q == SP else ldp_sems
        d = pool_[ld_i[0] % len(pool_)]
        ld_i[0] += 1
        return dma(q, out, in_, d, R, W)

    c_mark = apos[0]
    identf = carve(128); identb = carve_bf(128)
    maskF = carve(128); maskB = carve(128); triF = carve(128); triB = carve(128); permM = carve(128)
    jb = carve(NCH)
    onesf = carve(128); onesb = carve_bf(128)
    cTs = carve(KC * 2).rearrange("p (k j) -> p k j", j=2)
    gTs = carve(2 * KC).rearrange("p (a k) -> p a k", k=KC)
    fmv = carve(96)
    A1 = carve(KC); A1c = carve(KC); A2 = carve(KC)
    convs = carve(16).rearrange("p (c j) -> p c j", j=4)
    gdfs = carve(1); gdsc = carve(1)
    lam4 = carve(256); lamc = carve(2); lamneg = carve(1)
    hmask = carve(2); qsel = carve(4)
    epsc = carve(1)
    for (dst, src, nm) in [(identf, identf_d, "identf"), (maskF, maskF_d, "maskF"), (maskB, maskB_d, "maskB"), (triF, triF_d, "triF"),
                           (triB, triB_d, "triB"), (permM, perm_d, "perm"), (convs, convw, "convs"), (gdfs, gdf, "gdfs"),
                           (hmask, hmask_d, "hmask"), (qsel, qsel_d, "qsel"), (cTs, cT, "cTs"), (gTs, gT, "gTs")]:
        load(dst, src, [], [B(nm)])
    load(jb[0:NCH, :], jb_d, [], [B("jb")])
    load(lam4, lamv.partition_broadcast(128), [], [B("lam4")])
    cp(DVE, identb, identf, [B("identf")], [B("identb")])
    memset(DVE, onesf, 1.0, [B("onesf")])
    memset(DVE, onesb, 1.0, [B("onesb")])
    memset(DVE, epsc, EPS, [B("epsc")])
    lamt = carve(128)
    tt(DVE, lamt[:, 0:64], lam4[:, 0:64], lam4[:, 64:128], ALU.mult, [B("lam4")], [B("lamt")])
    tt(DVE, lamt[:, 64:128], lam4[:, 128:192], lam4[:, 192:256], ALU.mult, [B("lam4")], [B("lamt")])
    S.op(DVE, lambda e: e.reduce_sum(out=lamc, in_=lamt.rearrange("p (a b) -> p a b", b=64), axis=AX.X), reads=[B("lamt")], writes=[B("lamc")])
    act(lamc, lamc, AF.Exp, [B("lamc")], [B("lamc")])
    tt(DVE, lamneg, lamc[:, 1:2], lamc[:, 0:1], ALU.subtract, [B("lamc")], [B("lamneg")])
    ts(DVE, lamneg, lamneg, -LAM_INIT, None, ALU.add, None, [B("lamneg")], [B("lamneg")])
    ts(DVE, gdsc, gdfs, 1.0 - LAM_INIT, None, ALU.mult, None, [B("gdfs")], [B("gdsc")])

    scs = carve(KC * 2).rearrange("p (k j) -> p k j", j=2)
    act(scs, cTs, AF.Silu, [B("cTs")], [B("scs")])
    p1_mark = apos[0]
    win = carve_bf(KC * NCOL).rearrange("p (k c) -> p k c", c=NCOL)
    win_sem = S.new_dsem("win")
    w_in_v = w_in.rearrange("(k p) c -> p k c", p=128)
    for k in range(KC):
        dma(POOL, win[:, k, :], w_in_v[:, k, :], win_sem, [], [B("win")], new_gen=(k == 0))
    p0_mark = apos[0]
    NWT = 256
    wm_sem = [S.new_dsem(f"wm{i}") for i in range(2)]
    w_mod_v = w_mod.rearrange("(k p) c -> p k c", p=128)
    modbuf = {}

    def mod_alloc():
        modbuf["wm"] = [carve(KC * NWT).rearrange("p (k c) -> p k c", c=NWT) for _ in range(2)]
        modbuf["bm"] = [carve(512) for _ in range(2)]
        modbuf["mrow"] = [carve(512) for _ in range(2)]

    mod_issued = set()

    def mod_dma(ct):
        if ct in mod_issued or ct >= 6 * D // 512:
            return
        mod_issued.add(ct)
        wm = modbuf["wm"]; bm = modbuf["bm"]
        for hf in range(2):
            i = ct * 2 + hf
            c0 = i * NWT
            dma(SP, wm[i % 2], w_mod_v[:, :, c0:c0 + NWT], wm_sem[i % 2], [], [B(("wm", i % 2))])
        load(bm[ct % 2][0:2, :], b_mod[:, ct * 512:(ct + 1) * 512].partition_broadcast(2), [], [B(("bm", ct % 2))])

    def mod_tile(ct, pst, pbuf, prefetch=True):
        wm = modbuf["wm"]; bm = modbuf["bm"]; mrow = modbuf["mrow"]
        mod_dma(ct)
        for hf in range(2):
            i = ct * 2 + hf
            for k in range(KC):
                mm(pst[0:2, hf * NWT:(hf + 1) * NWT], scs[:, k, :], wm[i % 2][:, k, :], k == 0, k == KC - 1,
                   [B("scs"), B(("wm", i % 2))], [pbuf])
        tt(DVE, mrow[ct % 2][0:2, :], pst[0:2, :], bm[ct % 2][0:2, :], ALU.add, [pbuf, B(("bm", ct % 2))], [B(("mrow", ct % 2))])
        store(modrow[:, ct * 512:(ct + 1) * 512], mrow[ct % 2][0:2, :], [B(("mrow", ct % 2))], [B(("modrow", ct))])
        if prefetch:
            mod_dma(ct + 1)

    mod_alloc()
    for ct in range(2 * D // 512):
        mod_tile(ct, psf[ct % 2], B(("psf", ct % 2)), prefetch=(ct + 1 < 2 * D // 512))
    v96 = carve(128)
    for j, (row, c0) in enumerate([(0, 0), (0, D), (1, 0), (1, D)]):
        load(v96[j * 16:(j + 1) * 16, :], modrow[row:row + 1, c0:c0 + D].rearrange("o (k p) -> (o k) p", p=128), [B(("modrow", x)) for x in range(8)], [B("v96")])
    tp_out = psf[2]
    S.op(PE, lambda e: e.transpose(out=tp_out[:, 0:64], in_=v96[0:64, :], identity=identf[0:64, 0:64]), reads=[B("v96"), B("identf")], writes=[B(("psf", 2))])
    cp(DVE, fmv[:, 0:64], tp_out[:, 0:64], [B(("psf", 2))], [B("fmv")])
    SH1, SC1, CSH1, CSC1, SH2, SC2 = [fmv[:, j * 16:(j + 1) * 16] for j in range(6)]
    stt(DVE, A1, SC1, 1.0, gTs[:, 0, :], ALU.add, ALU.mult, [B("fmv"), B("gTs")], [B("A1")])
    stt(DVE, A1c, CSC1, 1.0, gTs[:, 0, :], ALU.add, ALU.mult, [B("fmv"), B("gTs")], [B("A1c")])
    S.barrier()
    apos[0] = p0_mark

    st_q[0] = POOL
    xr = [carve(D) for _ in range(3)]
    xr_sem = [S.new_dsem(f"xr{i}") for i in range(3)]
    xn = [carve_bf(D) for _ in range(4)]
    h1T = [carve_bf(KC * 512).rearrange("p (k t) -> p k t", t=512) for _ in range(2)]
    ss = carve(4); rstd = carve(4)
    stg = [[carve(520) for _ in range(2)] for _ in range(4)]
    ctmp = [carve(512) for _ in range(2)]
    csig = [carve(512) for _ in range(2)]
    obf = [carve_bf(512) for _ in range(4)]
    qf = [carve(512) for _ in range(2)]
    cosb = [carve(512) for _ in range(2)]; sinb = [carve(512) for _ in range(2)]
    rt1 = [carve(512) for _ in range(2)]; rt2 = [carve(512) for _ in range(2)]
    otm = [carve(256) for _ in range(2)]
    vtm = [carve_bf(256) for _ in range(4)]
    gtm = [carve(8) for _ in range(2)]
    flush = carve(8)
    cnt = {"blk": 0, "o": 0, "q": 0, "r": 0, "ot": 0, "vt": 0, "gt": 0, "ct": 0}
    ones512 = carve_bf(512)
    memset(DVE, ones512, 1.0, [B("ones512")])
    sh1hl = carve_bf(2 * KC).rearrange("p (j k) -> p j k", k=KC)
    sh1t = carve(KC)
    c1f = carve(NCOL); c1hl = carve_bf(2 * NCOL).rearrange("p (j c) -> p j c", c=NCOL)
    cp(DVE, sh1hl[:, 0, :], SH1, [B("fmv")], [B("sh1hl")])
    tt(DVE, sh1t, SH1, sh1hl[:, 0, :], ALU.subtract, [B("fmv"), B("sh1hl")], [B("sh1t")])
    cp(DVE, sh1hl[:, 1, :], sh1t, [B("sh1t")], [B("sh1hl")])

    def make_shift_and_fold():
        for gi, c0 in enumerate(range(0, NCOL, 512)):
            n = min(512, NCOL - c0)
            pst = psf[gi % 4]; pbuf = B(("psf", gi % 4))
            i_ = 0
            for j in range(2):
                for k in range(KC):
                    mm(pst[0:1, 0:n], sh1hl[:, j, k:k + 1], win[:, k, c0:c0 + n], i_ == 0, i_ == 2 * KC - 1, [B("sh1hl"), B("win")], [pbuf])
                    i_ += 1
            cp(ACT, c1f[0:1, c0:c0 + n], pst[0:1, 0:n], [pbuf], [B("c1f")])
        cp(DVE, c1hl[0:1, 0, :], c1f[0:1, :], [B("c1f")], [B("c1hl")])
        tt(DVE, c1f[0:1, :], c1f[0:1, :], c1hl[0:1, 0, :], ALU.subtract, [B("c1f"), B("c1hl")], [B("c1f")])
        cp(DVE, c1hl[0:1, 1, :], c1f[0:1, :], [B("c1f")], [B("c1hl")])
        for k in range(KC):
            ts(DVE, win[:, k, :], win[:, k, :], A1[:, k:k + 1], None, ALU.mult, None, [B("win"), B("A1")], [B("win")])

    def shift_fm(pst, c0, nt, pbuf):
        for j in range(1):
            mm(pst[:, 0:nt], c1hl[0:1, j, c0:c0 + 128], ones512[0:1, 0:nt], j == 0, False, [B("c1hl"), B("ones512")], [pbuf])

    def shift_tm(pst, c0, n, pbuf):
        for j in range(1):
            mm(pst[:, 0:n], ones512[0:1, 0:128], c1hl[0:1, j, c0:c0 + n], j == 0, False, [B("c1hl"), B("ones512")], [pbuf])

    tiles = [(True, 0, TC)] + [(False, i * 512, 512) for i in range(T // 512)]

    pendA1 = []

    def stageA1_block(ti, bl):
        isc, t0, nt = tiles[ti]
        src = ctxb if isc else xb
        i = cnt["blk"]; cnt["blk"] += 1
        xs = xr[i % 3]
        dma(SP, xs, src[t0 + bl * 128:t0 + (bl + 1) * 128, :], xr_sem[i % 3], [], [B(("xr", i % 3))])
        memset(DVE, ss[:, 0:1], 0.0, [B("ss")])
        act(xn[bl], xs, AF.Square, [B(("xr", i % 3)), B("ss")], [B(("xn", bl)), B("ss")], accum=ss[:, 0:1])
        act(rstd[:, 0:1], ss[:, 0:1], AF.Ln, [B("ss"), B("epsc")], [B("rstd")], scale=1.0 / D, bias=epsc)
        act(rstd[:, 0:1], rstd[:, 0:1], AF.Exp, [B("rstd")], [B("rstd")], scale=-0.5)
        ts(DVE, xn[bl], xs, rstd[:, 0:1], None, ALU.mult, None, [B(("xr", i % 3)), B("rstd")], [B(("xn", bl))])

    def stageA1(ti):
        for bl in range(tiles[ti][2] // 128):
            stageA1_block(ti, bl)

    def queueA1(ti):
        for bl in range(tiles[ti][2] // 128):
            pendA1.append((ti, bl))

    def popA1():
        if pendA1:
            stageA1_block(*pendA1.pop(0))

    def stageA2(ti):
        isc, t0, nt = tiles[ti]
        hT = h1T[ti % 2]
        Asc = A1c if isc else A1
        Ash = CSH1 if isc else SH1
        for bl in range(nt // 128):
            for g in range(2):
                pb = psb[g]
                for kk in range(8):
                    k = g * 8 + kk
                    tp(pb[:, kk * 128:(kk + 1) * 128], xn[bl][:, k * 128:(k + 1) * 128], identb, [B(("xn", bl)), B("identb")], [B(("psb", g))])
                if not isc:
                    cp(ACT if g == 0 else DVE, hT[:, g * 8:(g + 1) * 8, bl * 128:(bl + 1) * 128], pb.rearrange("p (k t) -> p k t", t=128), [B(("psb", g))], [B(("h1T", ti % 2))])
                    continue
                for kk in range(8):
                    k = g * 8 + kk
                    eng = ACT if kk % 2 == 0 else DVE
                    if eng == ACT:
                        act(hT[:, k, bl * 128:(bl + 1) * 128], pb[:, kk * 128:(kk + 1) * 128], AF.Identity, [B(("psb", g)), B("A1"), B("A1c"), B("fmv")], [B(("h1T", ti % 2))],
                            scale=Asc[:, k:k + 1], bias=Ash[:, k:k + 1])
                    else:
                        ts(DVE, hT[:, k, bl * 128:(bl + 1) * 128], pb[:, kk * 128:(kk + 1) * 128], Asc[:, k:k + 1], Ash[:, k:k + 1], ALU.mult, ALU.add,
                           [B(("psb", g)), B("A1"), B("A1c"), B("fmv")], [B(("h1T", ti % 2))])

    def conv_chunk(ci, ti, pst, psbuf, isc, t0, nt, first, last):
        sg = stg[ci][ti % 2]; sgp = stg[ci][(ti - 1) % 2]
        sb = B(("stg", ci, ti % 2)); sbp = B(("stg", ci, (ti - 1) % 2))
        pnt = tiles[ti - 1][2] if ti > 0 else 0
        if first:
            memset(DVE, sg[:, 0:2], 0.0, [sb])
        else:
            cp(DVE, sg[:, 0:2], sgp[:, pnt:pnt + 2], [sbp], [sb])
        cp(ACT, sg[:, 2:2 + nt], pst[:, 0:nt], [psbuf], [sb])
        w0, w1, w2, bb = [convs[:, ci, j:j + 1] for j in range(4)]
        head = ci % 2
        isk = ci >= 2
        dst = (mkT_c if isc else mkT_l) if isk else (mqT_c if isc else mqT_l)

        def conv_out(n, src0, col0, zero_next=False):
            j = cnt["ct"]; cnt["ct"] += 1
            t = ctmp[j % 2]; tb = B(("ctmp", j % 2))
            ts(DVE, t[:, 0:n], sg[:, src0 + 1:src0 + 1 + n], w1, bb, ALU.mult, ALU.add, [sb, B("convs")], [tb])
            stt(DVE, t[:, 0:n], sg[:, src0:src0 + n], w0, t[:, 0:n], ALU.mult, ALU.add, [sb, B("convs"), tb], [tb])
            if not zero_next:
                stt(DVE, t[:, 0:n], sg[:, src0 + 2:src0 + 2 + n], w2, t[:, 0:n], ALU.mult, ALU.add, [sb, B("convs"), tb], [tb])
            oi = cnt["o"]; cnt["o"] += 1
            ob = obf[oi % 4]; obb = B(("obf", oi % 4))
            if not isk:
                act(ob[:, 0:n], t[:, 0:n], AF.Silu, [tb], [obb])
            else:
                sgm = csig[j % 2]; sgb = B(("csig", j % 2))
                act(sgm[:, 0:n], t[:, 0:n], AF.Sigmoid, [tb], [sgb])
                stt(DVE, ob[:, 0:n], t[:, 0:n], 128.0 ** -0.5, sgm[:, 0:n], ALU.mult, ALU.mult, [tb, sgb], [obb])
            store(dst[head, :, col0:col0 + n], ob[:, 0:n], [obb], [B((("k" if isk else "q"), head, isc, col0))])

        conv_out(nt, 0, t0)
        if last:
            conv_out(1, nt, t0 + nt, zero_next=True)

    def stageB(ti):
        isc, t0, nt = tiles[ti]
        hT = h1T[ti % 2]; hb = B(("h1T", ti % 2))
        first = ti in (0, 1)
        last = ti in (0, len(tiles) - 1)
        fm = [(C_MQ, "c", 0), (C_MQ + 128, "c", 1), (C_MK, "c", 2), (C_MK + 128, "c", 3),
              (C_DQ, "dq", 0), (C_DQ + 128, "dq", 1), (C_DK, "dk", 0), (C_DK + 128, "dk", 1)]
        for fi, (c0, kind, idx) in enumerate(fm):
            if isc and kind == "dq":
                continue
            pi = fi % 4
            pst = psf[pi]; pbuf = B(("psf", pi))
            if not isc:
                shift_fm(pst, c0, nt, pbuf)
            for k in range(KC):
                mm(pst[:, 0:nt], win[:, k, c0:c0 + 128], hT[:, k, 0:nt], (k == 0) and isc, k == KC - 1, [B("win"), hb], [pbuf])
            if kind == "c":
                conv_chunk(idx, ti, pst, pbuf, isc, t0, nt, first, last)
            else:
                dst = dqT if kind == "dq" else dkT
                col0 = t0 if (kind == "dq" or isc) else TC + t0
                oi = cnt["o"]; cnt["o"] += 1
                ob = obf[oi % 4]; obb = B(("obf", oi % 4))
                if isc:
                    cp(ACT, ob[:, 0:nt], pst[:, 0:nt], [pbuf], [obb])
                else:
                    j = cnt["r"]; cnt["r"] += 1
                    q_ = qf[j % 2]; qb = B(("qf", j % 2))
                    cp(ACT, q_, pst[:, 0:nt], [pbuf], [qb])
                    if fi == 4:
                        jj = ti % 2
                        load(cosb[jj], cos_d[:, t0:t0 + nt], [], [B(("cos", jj))])
                        load(sinb[jj], sin_d[:, t0:t0 + nt], [], [B(("sin", jj))])
                    jj = ti % 2
                    prot = psf[4 + (j % 2)]; prb = B(("psf", 4 + (j % 2)))
                    mm(prot[:, 0:nt], permM, q_, True, True, [B("perm"), qb], [prb])
                    tt(DVE, rt1[j % 2], q_, cosb[jj], ALU.mult, [qb, B(("cos", jj))], [B(("rt1", j % 2))])
                    tt(DVE, rt2[j % 2], prot[:, 0:nt], sinb[jj], ALU.mult, [prb, B(("sin", jj))], [B(("rt2", j % 2))])
                    tt(DVE, ob[:, 0:nt], rt1[j % 2], rt2[j % 2], ALU.add, [B(("rt1", j % 2)), B(("rt2", j % 2))], [obb])
                store(dst[idx, :, col0:col0 + nt], ob[:, 0:nt], [obb], [B((kind, idx, isc, t0))])
            if fi % 2 == 1:
                popA1()
        while pendA1:
            popA1()
        for bl in range(nt // 128):
            tg0 = (0 if isc else TC) + t0 + bl * 128
            lhs = lambda k: hT[:, k, bl * 128:(bl + 1) * 128]
            if not isc:
                pst = psf[0]; pbuf = B(("psf", 0))
                shift_tm(pst, C_MO, 256, pbuf)
                for k in range(KC):
                    mm(pst[:, 0:256], lhs(k), win[:, k, C_MO:C_MO + 256], False, k == KC - 1, [B("win"), hb], [pbuf])
                j = cnt["ot"]; cnt["ot"] += 1
                act(otm[j % 2], pst[:, 0:256], AF.Sigmoid, [pbuf], [B(("otm", j % 2))])
                tl = t0 + bl * 128
                store(oS[:, tl:tl + 128, :].rearrange("h t e -> t h e"), otm[j % 2].rearrange("p (h e) -> p h e", e=128), [B(("otm", j % 2))], [B(("oS", tl))])
            pst = psf[1]; pbuf = B(("psf", 1))
            if not isc:
                shift_tm(pst, C_MV, 264, pbuf)
            for k in range(KC):
                mm(pst[:, 0:264], lhs(k), win[:, k, C_MV:C_MV + 264], (k == 0) and isc, k == KC - 1, [B("win"), hb], [pbuf])
            j = cnt["vt"]; cnt["vt"] += 1
            cp(DVE, vtm[j % 4], pst[:, 0:256], [pbuf], [B(("vtm", j % 4))])
            store(mvS[:, tg0:tg0 + 128, :].rearrange("h t e -> t h e"), vtm[j % 4].rearrange("p (h e) -> p h e", e=128), [B(("vtm", j % 4))], [B(("mvS", tg0))])
            jg = cnt["gt"]; cnt["gt"] += 1
            cp(DVE, gtm[jg % 2], pst[:, 256:264], [pbuf], [B(("gtm", jg % 2))])
            store(gS[tg0:tg0 + 128, :], gtm[jg % 2], [B(("gtm", jg % 2))], [B(("gS", tg0))])
            pst = psf[2]; pbuf = B(("psf", 2))
            if not isc:
                shift_tm(pst, C_DV, 256, pbuf)
            for k in range(KC):
                mm(pst[:, 0:256], lhs(k), win[:, k, C_DV:C_DV + 256], (k == 0) and isc, k == KC - 1, [B("win"), hb], [pbuf])
            j = cnt["vt"]; cnt["vt"] += 1
            cp(ACT, vtm[j % 4], pst[:, 0:256], [pbuf], [B(("vtm", j % 4))])
            store(dvS[:, tg0:tg0 + 128, :].rearrange("h t e -> t h e"), vtm[j % 4].rearrange("p (h e) -> p h e", e=128), [B(("vtm", j % 4))], [B(("dvS", tg0))])

    stageA1(0)
    stageA2(0)
    for ti in range(len(tiles)):
        if ti + 1 < len(tiles):
            queueA1(ti + 1)
        stageB(ti)
        if ti == 0:
            make_shift_and_fold()
        if ti + 1 < len(tiles):
            stageA2(ti + 1)
    S.barrier()
    apos[0] = p1_mark
    if stage == 1:
        return finish(nc, S, out_d)


    st_q[0] = SP
    p2_mark = apos[0]
    KT = [carve_bf(TA) for _ in range(2)]; QT = [carve_bf(T) for _ in range(2)]
    VV = [carve_bf(NCH * 130).rearrange("p (c e) -> p c e", e=130) for _ in range(2)]
    for h in range(2):
        load(KT[h], dkT[h], [], [B(("KT", h))], q=POOL)
        load(QT[h], dqT[h], [], [B(("QT", h))], q=POOL)
        memset(DVE, VV[h][:, :, 128:130], 1.0, [B(("VV", h))])
        load(VV[h][:, :, 0:128], dvS[h].rearrange("(c s) e -> s c e", s=128), [], [B(("VV", h))], q=POOL)
    Pb = [carve_bf(1024) for _ in range(4)]
    spair = [psp[0], psp[1]]
    gdrow = carve(128)
    load(gdrow, gdr.partition_broadcast(128), [], [B("gdrow")])
    ts(DVE, gdrow, gdrow, 1.0 - LAM_INIT, None, ALU.mult, None, [B("gdrow")], [B("gdrow")])
    mod_alloc()
    mod_next = [2 * D // 512]
    w_up_v = w_up.rearrange("(k p) c -> p k c", p=128)
    w_dn_v = w_down.rearrange("(j p) c -> p j c", p=128)
    pcu = [carve_bf(KC * 256).rearrange("p (k c) -> p k c", c=256) for _ in range(2)]
    pcd = [carve_bf(HC * 128).rearrange("p (j c) -> p j c", c=128) for _ in range(1)]
    pcu_sem = [S.new_dsem(f"pcu{i}") for i in range(2)]
    pcd_sem = [S.new_dsem(f"pcd{i}") for i in range(1)]
    pc_jobs = [("u", j) for j in range(HC)] + [("d", c) for c in range(D // 128)]
    pc_cnt = {"u": 0, "d": 0}

    def precast_one():
        if not pc_jobs:
            return
        kind, idx = pc_jobs.pop(0)
        i_ = pc_cnt[kind] % (2 if kind == "u" else 1); pc_cnt[kind] += 1
        if kind == "u":
            w = pcu[i_]; wb = B(("pcu", i_)); ws = pcu_sem[i_]
            dma(POOL, w[:, :, 0:128], w_up_v[:, :, idx * 128:(idx + 1) * 128], ws, [], [wb])
            dma(POOL, w[:, :, 128:256], w_up_v[:, :, DFF + idx * 128:DFF + (idx + 1) * 128], ws, [], [wb], new_gen=False)
            store(wuS[idx], w.rearrange("p k c -> p (k c)"), [wb], [B(("wuS", idx))], q=SP)
        else:
            w = pcd[i_]; wb = B(("pcd", i_)); ws = pcd_sem[i_]
            dma(POOL, w, w_dn_v[:, :, idx * 128:(idx + 1) * 128], ws, [], [wb])
            store(wdS[idx], w.rearrange("p j c -> p (j c)"), [wb], [B(("wdS", idx))], q=SP)


    acc_bank = [psf[4], psf[5], psp[3][:, 0:512]]

    def acc_ap(idx):
        o = (idx % 3) * 132
        return acc_bank[idx // 3][:, o:o + 129], B(("acc", idx // 3))

    fo = [carve(128) for _ in range(2)]; frr = [carve(4) for _ in range(2)]; fsq = carve(128); fss = [carve(2) for _ in range(2)]
    fdt = [carve_bf(128) for _ in range(2)]; fd = [carve_bf(512) for _ in range(2)]
    fcnt = 0
    for h in range(2):
        for qt in range(T // 512):
            q0 = qt * 512

            def s_step(kb):
                sp = kb % 2; pp = kb % 4
                for m in range(2):
                    mm(spair[sp][:, m * 512:(m + 1) * 512], KT[h][m * 64:(m + 1) * 64, kb * 128:(kb + 1) * 128], QT[h][m * 64:(m + 1) * 64, q0:q0 + 512], True, True,
                       [B(("KT", h)), B(("QT", h))], [B(("psp", sp))])
                act(Pb[pp], spair[sp][:, :], AF.Exp, [B(("psp", sp))], [B(("Pb", pp))], scale=0.125)

            def pv_step(kb):
                pp = kb % 4
                for m in range(2):
                    for qb in range(4):
                        ap_, ab_ = acc_ap(m * 4 + qb)
                        mm(ap_, Pb[pp][:, m * 512 + qb * 128:m * 512 + (qb + 1) * 128], VV[h][:, kb, 0:129], (kb == 0) and ((m * 4 + qb) % 3 == 0), kb == NCH - 1, [B(("VV", h)), B(("Pb", pp))], [ab_])

            for kb in range(NCH):
                if kb % 33 == 5:
                    precast_one()
                s_step(kb)
                if kb > 1:
                    pv_step(kb - 2)
            pv_step(NCH - 2)
            pv_step(NCH - 1)
            if mod_next[0] < 6 * D // 512:
                mod_tile(mod_next[0], psf[3], B(("psp", 1)))
                mod_next[0] += 1
            jf = fcnt % 2; fcnt += 1
            for qb in range(4):
                i2 = qb % 2
                a0, ab0 = acc_ap(qb); a1, ab1 = acc_ap(4 + qb)
                rr = frr[i2]; rb = B(("frr", i2))
                S.op(DVE, (lambda rr, a0: lambda e: e.reciprocal(out=rr[:, 0:1], in_=a0[:, 128:129]))(rr, a0), reads=[ab0], writes=[rb])
                S.op(DVE, (lambda rr, a1: lambda e: e.reciprocal(out=rr[:, 1:2], in_=a1[:, 128:129]))(rr, a1), reads=[ab1], writes=[rb])
                tt(DVE, rr[:, 2:3], rr[:, 1:2], lamneg[:, 0:1], ALU.mult, [rb, B("lamneg")], [rb])
                act(fo[i2], a0[:, 0:128], AF.Copy, [ab0, rb], [B(("fo", i2))], scale=rr[:, 0:1])
                stt(DVE, fo[i2], a1[:, 0:128], rr[:, 2:3], fo[i2], ALU.mult, ALU.add, [ab1, rb, B(("fo", i2))], [B(("fo", i2))])
                sq_ = fss[i2]; sqb = B(("fss", i2))
                memset(DVE, sq_[:, 0:1], 0.0, [sqb])
                act(fsq, fo[i2], AF.Square, [B(("fo", i2)), sqb], [B("fsq"), sqb], accum=sq_[:, 0:1])
                act(sq_[:, 1:2], sq_[:, 0:1], AF.Ln, [sqb, B("epsc")], [sqb], scale=1.0 / 128, bias=epsc)
                act(sq_[:, 1:2], sq_[:, 1:2], AF.Exp, [sqb], [sqb], scale=-0.5)
                stt(DVE, fdt[i2], fo[i2], sq_[:, 1:2], gdrow, ALU.mult, ALU.mult, [B(("fo", i2)), sqb, B("gdrow")], [B(("fdt", i2))])
                tp(psb[1][:, i2 * 128:(i2 + 1) * 128], fdt[i2], identb, [B(("fdt", i2)), B("identb")], [B(("psbt", i2))])
                cp(ACT, fd[jf][:, qb * 128:(qb + 1) * 128], psb[1][:, i2 * 128:(i2 + 1) * 128], [B(("psbt", i2))], [B(("fd", jf))])
            store(mixL[q0 // 1024][256 + h * 128:256 + (h + 1) * 128, q0 % 1024:q0 % 1024 + 512], fd[jf], [B(("fd", jf))], [B(("mixd", h, qt))])
    while pc_jobs:
        precast_one()
    S.barrier()
    apos[0] = p2_mark
    if stage == 2:
        return finish(nc, S, out_d)

    st_q[0] = POOL
    Gz = carve(NCH * 8).rearrange("p (c g) -> p c g", g=8)
    bg = carve(NCH * 8).rearrange("p (c g) -> p c g", g=8)
    load(Gz, gS.rearrange("(c s) g -> s c g", s=128), [], [B("Gz")])
    load(bg, bgate.partition_broadcast(128).rearrange("p o (c g) -> p (o c) g", g=8), [], [B("bg")])
    tt(DVE, Gz, Gz, bg, ALU.add, [B("Gz"), B("bg")], [B("Gz")])
    NS = 4 * NCH
    Lg = carve(NS).rearrange("p (s c) -> p s c", c=NCH)
    for st in range(4):
        fcol = (st // 2) * 4 + 2 + (st % 2)
        act(Lg[:, st, :], Gz[:, :, fcol], AF.Exp, [B("Gz")], [B("Lg")], scale=-1.0)
    act(Lg, Lg, AF.Ln, [B("Lg")], [B("Lg")], bias=1.0)
    Lg2 = Lg.rearrange("p s c -> p (s c)")
    mm(psf[0][:, 0:2 * NCH], triF, Lg2[:, 0:2 * NCH], True, True, [B("triF"), B("Lg")], [B(("psf", 0))])
    mm(psf[0][:, 2 * NCH:NS], triB, Lg2[:, 2 * NCH:NS], True, True, [B("triB"), B("Lg")], [B(("psf", 0))])
    mm(psf[1][:, 0:NS], onesf, Lg2, True, True, [B("onesf"), B("Lg")], [B(("psf", 1))])
    cL = carve(NS).rearrange("p (s c) -> p s c", c=NCH)
    gg = carve(NS).rearrange("p (s c) -> p s c", c=NCH)
    cp(DVE, cL.rearrange("p s c -> p (s c)"), psf[0][:, 0:NS], [B(("psf", 0))], [B("cL")])
    for st in range(4):
        icol = (st // 2) * 4 + (st % 2)
        tt(DVE, gg[:, st, :], Gz[:, :, icol], cL[:, st, :], ALU.add, [B("Gz"), B("cL")], [B("gg")])
    gg2 = gg.rearrange("p s c -> p (s c)")
    Grow = carve(NS); Brow = carve(NS); gmax = carve(4)
    ts(DVE, Brow[0:1, :], psf[1][0:1, 0:NS], -1.0, None, ALU.mult, None, [B(("psf", 1))], [B("Brow")])
    for gi, (a0, a1) in enumerate([(0, 128), (128, 256), (256, NS)]):
        n = a1 - a0
        S.op(PE, (lambda a0, a1, n: lambda e: e.transpose(out=psf[2][0:n, 0:128], in_=gg2[:, a0:a1], identity=identf))(a0, a1, n), reads=[B("gg"), B("identf")], writes=[B(("psf", 2))])
        S.op(DVE, (lambda n, gi: lambda e: e.reduce_max(out=gmax[0:n, gi:gi + 1], in_=psf[2][0:n, 0:128], axis=AX.X))(n, gi), reads=[B(("psf", 2))], writes=[B("gmax")])
        S.op(PE, (lambda a0, a1, n, gi: lambda e: e.transpose(out=psf[3][0:1, a0:a1], in_=gmax[0:n, gi:gi + 1], identity=identf[0:n, 0:n]))(a0, a1, n, gi), reads=[B("gmax"), B("identf")], writes=[B(("psf", 3))])
    cp(DVE, Grow[0:1, :], psf[3][0:1, 0:NS], [B(("psf", 3))], [B("Grow")])
    Gp = carve(NS); Bp = carve(NS); mnx = carve(NS); min_ = carve(NS); Mp = carve(NS); Wp = carve(NS); Mrow = carve(NS); Wrow = carve(NS)

    def rowv(t, st):
        return t[0:1, st * NCH:(st + 1) * NCH]

    for st in range(4):
        for (src, dst, bn) in [(Grow, Gp, "Gp"), (Brow, Bp, "Bp")]:
            if st < 2:
                cp(DVE, rowv(dst, st), rowv(src, st), [B("Grow"), B("Brow")], [B(bn)])
            else:
                cp(DVE, rowv(dst, st)[:, 0:2], rowv(src, st)[:, 1::-1], [B("Grow"), B("Brow")], [B(bn)])
                cp(DVE, rowv(dst, st)[:, 2:NCH], rowv(src, st)[:, NCH - 1:1:-1], [B("Grow"), B("Brow")], [B(bn)])
        S.op(DVE, (lambda st: lambda e: e.tensor_tensor_scan(out=rowv(mnx, st), data0=rowv(Gp, st), data1=rowv(Bp, st), initial=0.0, op0=ALU.max, op1=ALU.add))(st),
             reads=[B("Gp"), B("Bp")], writes=[B("mnx")])
        memset(DVE, rowv(min_, st)[:, 0:1], 0.0, [B("min")])
        cp(DVE, rowv(min_, st)[:, 1:NCH], rowv(mnx, st)[:, 0:NCH - 1], [B("mnx")], [B("min")])
    tt(DVE, Mp[0:1, :], min_[0:1, :], Gp[0:1, :], ALU.max, [B("min"), B("Gp")], [B("Mp")])
    tt(DVE, Wp[0:1, :], min_[0:1, :], Mp[0:1, :], ALU.subtract, [B("min"), B("Mp")], [B("Wp")])
    act(Wp[0:1, :], Wp[0:1, :], AF.Exp, [B("Wp")], [B("Wp")])
    for st in range(4):
        for (src, dst, bn, sb) in [(Mp, Mrow, "Mrow", "Mp"), (Wp, Wrow, "Wrow", "Wp")]:
            if st < 2:
                cp(DVE, rowv(dst, st), rowv(src, st), [B(sb)], [B(bn)])
            else:
                cp(DVE, rowv(dst, st)[:, 0:2], rowv(src, st)[:, 1::-1], [B(sb)], [B(bn)])
                cp(DVE, rowv(dst, st)[:, 2:NCH], rowv(src, st)[:, NCH - 1:1:-1], [B(sb)], [B(bn)])
    mm(psf[4][:, 0:NS], onesf[0:1, :], Mrow[0:1, :], True, True, [B("onesf"), B("Mrow")], [B(("psf", 4))])
    mm(psf[5][:, 0:NS], onesf[0:1, :], Wrow[0:1, :], True, True, [B("onesf"), B("Wrow")], [B(("psf", 5))])
    Wrep = carve(NS).rearrange("p (s c) -> p s c", c=NCH)
    wcol = carve(NS).rearrange("p (s c) -> p s c", c=NCH)
    clamp = carve(NS).rearrange("p (s c) -> p s c", c=NCH)
    cp(DVE, Wrep.rearrange("p s c -> p (s c)"), psf[5][:, 0:NS], [B(("psf", 5))], [B("Wrep")])
    tt(DVE, wcol.rearrange("p s c -> p (s c)"), gg2, psf[4][:, 0:NS], ALU.subtract, [B("gg"), B(("psf", 4))], [B("wcol")])
    act(wcol, wcol, AF.Exp, [B("wcol")], [B("wcol")])
    tt(DVE, clamp.rearrange("p s c -> p (s c)"), cL.rearrange("p s c -> p (s c)"), psf[4][:, 0:NS], ALU.subtract, [B("cL"), B(("psf", 4))], [B("clamp")])
    act(clamp, clamp, AF.Exp, [B("clamp")], [B("clamp")])
    qTl = [carve_bf(T + 1) for _ in range(2)]; kTl = [carve_bf(T + 1) for _ in range(2)]; kTc = [carve_bf(TC + 1) for _ in range(2)]
    vaug = [carve_bf(NCH * 130).rearrange("p (c e) -> p c e", e=130) for _ in range(2)]
    ktok = [carve_bf(NCH * 128).rearrange("p (c e) -> p c e", e=128) for _ in range(2)]
    gmlr = carve(256)
    load(gmlr, gml.partition_broadcast(128), [], [B("gmlr")])
    for h in range(2):
        load(qTl[h], mqT_l[h], [], [B(("qTl", h))], q=POOL)
        load(kTl[h], mkT_l[h], [], [B(("kTl", h))], q=POOL)
        load(kTc[h], mkT_c[h], [], [B(("kTc", h))], q=POOL)
        memset(DVE, vaug[h][:, :, 128:130], 1.0, [B(("vaug", h))])
        load(vaug[h][:, :, 0:128], mvS[h].rearrange("(c s) e -> s c e", s=128), [], [B(("vaug", h))], q=POOL)

    def kchunk(h, cs):
        return kTc[h][:, 1 + cs * 128:1 + (cs + 1) * 128] if cs < 2 else kTl[h][:, 1 + (cs - 2) * 128:1 + (cs - 1) * 128]

    for h in range(2):
        for cs in range(NCH):
            pb = psb[cs % 2]
            tp(pb[:, 0:128], kchunk(h, cs), identb, [B(("kTl", h)), B(("kTc", h)), B("identb")], [B(("psb", cs % 2))])
            cp(ACT if cs % 2 == 0 else DVE, ktok[h][:, cs, :], pb[:, 0:128], [B(("psb", cs % 2))], [B(("ktok", h))])
    Cn = [carve(130) for _ in range(4)]; Cnb = [carve_bf(130) for _ in range(4)]
    NB = 8
    STb = [carve_bf(128) for _ in range(NB)]; kwb = [carve_bf(128) for _ in range(NB)]
    dena = [carve(2) for _ in range(NB)]; hbuf = [carve(128) for _ in range(NB)]; hfb = [carve(128) for _ in range(NB)]; obuf = [carve(128) for _ in range(NB)]
    hsq = carve(128); hss = [carve(2) for _ in range(NB)]; mtok = [carve_bf(128) for _ in range(NB)]; mTb = [carve_bf(128) for _ in range(NB)]
    perm66 = [1, 0] + [NCH + 1 - j for j in range(2, NCH)]
    pos_of = [{cs: cs for cs in range(NCH)}, {perm66[j]: j for j in range(NCH)}]
    for st in range(4):
        memset(DVE, Cn[st], 0.0, [B(("Cn", st))])
        memset(DVE, Cnb[st], 0.0, [B(("Cnb", st))])
    steps = []
    for j in range(NCH):
        for dr in range(2):
            cs = j if dr == 0 else perm66[j]
            for h in range(2):
                n_ = len(steps)
                steps.append(dict(j=j, dr=dr, cs=cs, h=h, st=dr * 2 + h, i2=n_ % 2, i4=n_ % NB, lat=cs >= 2, tl=(cs - 2) * 128,
                                  second=(cs >= 2 and pos_of[1 - dr][cs] < j)))

    def stepA(c):
        if not c["lat"]:
            return
        h, cs, st, i2, i4, tl = c["h"], c["cs"], c["st"], c["i2"], c["i4"], c["tl"]
        mask = maskF if c["dr"] == 0 else maskB
        mkb = B("maskF") if c["dr"] == 0 else B("maskB")
        qch = qTl[h][:, 1 + tl:1 + tl + 128]
        pq = psf[i2]
        mm(pq[:, 0:128], kchunk(h, cs), qch, True, True, [B(("kTl", h)), B(("qTl", h))], [B(("psf", i2))])
        stt(DVE, STb[i4], pq[:, 0:128], wcol[:, st, cs:cs + 1], mask, ALU.mult, ALU.mult, [B(("psf", i2)), B("wcol"), mkb], [B(("STb", i4))])

    def stepB(c):
        h, cs, st, i2, i4, tl, j, dr = c["h"], c["cs"], c["st"], c["i2"], c["i4"], c["tl"], c["j"], c["dr"]
        if c["lat"] and c["second"]:
            load(hfb[i4], hfS[h, tl:tl + 128, :], [B(("hfS", h, tl))], [B(("hfb", i4))])
            load(obuf[i4], oS[h, tl:tl + 128, :], [], [B(("obuf", i4))])
        if c["lat"]:
            qch = qTl[h][:, 1 + tl:1 + tl + 128]
            pn = psf[2 + i2]
            mm(pn[:, 0:129], STb[i4], vaug[h][:, cs, 0:129], True, False, [B(("STb", i4)), B(("vaug", h))], [B(("psf", 2 + i2))])
            mm(pn[:, 0:129], qch, Cnb[st][:, 0:129], False, True, [B(("qTl", h)), B(("Cnb", st))], [B(("psf", 2 + i2))])
            da = dena[i4]
            act(da[:, 0:1], pn[:, 128:129], AF.Abs, [B(("psf", 2 + i2))], [B(("dena", i4))])
            tt(DVE, da[:, 0:1], da[:, 0:1], clamp[:, st, cs:cs + 1], ALU.max, [B(("dena", i4)), B("clamp")], [B(("dena", i4))])
            S.op(DVE, (lambda da: lambda e: e.reciprocal(out=da[:, 1:2], in_=da[:, 0:1]))(da), reads=[B(("dena", i4))], writes=[B(("dena", i4))])
            act(hbuf[i4], pn[:, 0:128], AF.Copy, [B(("psf", 2 + i2)), B(("dena", i4))], [B(("hbuf", i4))], scale=da[:, 1:2])
            if not c["second"]:
                store(hfS[h, tl:tl + 128, :], hbuf[i4], [B(("hbuf", i4))], [B(("hfS", h, tl))])
        if j < NCH - 1:
            pu = psf[4 + i2]
            ts(DVE, kwb[i4], ktok[h][:, cs, :], wcol[:, st, cs:cs + 1], None, ALU.mult, None, [B(("ktok", h)), B("wcol")], [B(("kwb", i4))])
            mm(pu[:, 0:129], kwb[i4], vaug[h][:, cs, 0:129], True, True, [B(("kwb", i4)), B(("vaug", h))], [B(("psf", 4 + i2))])
            stt(DVE, Cn[st][:, 0:129], Cn[st][:, 0:129], Wrep[:, st, cs:cs + 1], pu[:, 0:129], ALU.mult, ALU.add,
                [B(("Cn", st)), B("Wrep"), B(("psf", 4 + i2))], [B(("Cn", st))])
            csn = (j + 1) if dr == 0 else perm66[j + 1]
            act(Cnb[st][:, 0:129], Cn[st][:, 0:129], AF.Copy, [B(("Cn", st)), B("Wrep")], [B(("Cnb", st))], scale=Wrep[:, st, csn:csn + 1])

    def stepC(c):
        if not (c["lat"] and c["second"]):
            return
        h, i2, i4, tl = c["h"], c["i2"], c["i4"], c["tl"]
        hs_ = hss[i4]
        tt(DVE, hbuf[i4], hbuf[i4], hfb[i4], ALU.add, [B(("hbuf", i4)), B(("hfb", i4))], [B(("hbuf", i4))])
        tt(DVE, obuf[i4], obuf[i4], gmlr[:, h * 128:(h + 1) * 128], ALU.mult, [B(("obuf", i4)), B("gmlr")], [B(("obuf", i4))])
        memset(DVE, hs_[:, 0:1], 0.0, [B(("hss", i4))])
        act(hsq, hbuf[i4], AF.Square, [B(("hbuf", i4)), B(("hss", i4))], [B("hsq"), B(("hss", i4))], accum=hs_[:, 0:1])
        act(hs_[:, 1:2], hs_[:, 0:1], AF.Ln, [B(("hss", i4)), B("epsc")], [B(("hss", i4))], scale=1.0 / 128, bias=epsc)
        act(hs_[:, 1:2], hs_[:, 1:2], AF.Exp, [B(("hss", i4))], [B(("hss", i4))], scale=-0.5)
        stt(DVE, mtok[i4], hbuf[i4], hs_[:, 1:2], obuf[i4], ALU.mult, ALU.mult, [B(("hbuf", i4)), B(("hss", i4)), B(("obuf", i4))], [B(("mtok", i4))])
        tp(psb[i2][:, 0:128], mtok[i4], identb, [B(("mtok", i4)), B("identb")], [B(("psb", i2))])
        cp(ACT, mTb[i4], psb[i2][:, 0:128], [B(("psb", i2))], [B(("mTb", i4))])
        store(mixL[tl // 1024][h * 128:(h + 1) * 128, tl % 1024:tl % 1024 + 128], mTb[i4], [B(("mTb", i4))], [B(("mixm", h, tl))])

    NSTEP = len(steps)
    ccs = S.new_dsem("cc")

    def gather_piece(p):
        rd = [B(("mixm", h_, tl_)) for h_ in range(2) for tl_ in range(p * 1024, (p + 1) * 1024, 128)]
        S.dma(POOL, (lambda p: lambda e: e.collective_compute("AllGather", ALU.bypass, replica_groups=[[0, 1, 2, 3], [4, 5, 6, 7]], ins=[mixL[p]], outs=[mixA[p]]))(p),
              ccs, reads=rd, writes=[B(("mixA", p))], inc=1)

    gather_at = {4 * 42 + 3: 3, 4 * 46 + 3: 4, 4 * 50 + 3: 2, 4 * 54 + 3: 5, 4 * 58 + 3: 1, 4 * 62 + 3: 6}
    LA, LC = 2, 5
    for n_ in range(NSTEP + LC):
        if n_ < NSTEP:
            stepA(steps[n_])
        if 0 <= n_ - LA < NSTEP:
            stepB(steps[n_ - LA])
        if 0 <= n_ - LC < NSTEP:
            stepC(steps[n_ - LC])
            if (n_ - LC) in gather_at:
                gather_piece(gather_at[n_ - LC])
    S.barrier()
    apos[0] = p2_mark
    if stage == 3:
        return finish(nc, S, out_d)

    ev_keep = {p: ("d", ccs, k + 1) for k, p in enumerate([3, 4, 2, 5, 1, 6])}
    for p in (0, 7):
        S.dma(POOL, (lambda p: lambda e: e.collective_compute("AllGather", ALU.bypass, replica_groups=[[0, 1, 2, 3], [4, 5, 6, 7]], ins=[mixL[p]], outs=[mixA[p]]))(p),
              ccs, reads=[], writes=[B(("mixA", p))], inc=1)
    for p, ev in ev_keep.items():
        B(("mixA", p)).w = ev

    st_q[0] = POOL
    ga1g = carve(D); ga2g = carve(D)
    mq_mark = apos[0]
    mixq = carve_bf(KC * (TQ + 2)).rearrange("p (k t) -> p k t", t=TQ + 2)
    p5_mark = apos[0]
    v96b = carve(128)
    for j, c0 in enumerate([3 * D, 4 * D]):
        load(v96b[j * 16:(j + 1) * 16, :], modrow[0:1, c0:c0 + D].rearrange("o (k p) -> (o k) p", p=128), [], [B("v96b")])
    S.op(PE, lambda e: e.transpose(out=psf[2][:, 0:32], in_=v96b[0:32, :], identity=identf[0:32, 0:32]), reads=[B("v96b"), B("identf")], writes=[B(("psf", 2))])
    cp(DVE, fmv[:, 64:96], psf[2][:, 0:32], [B(("psf", 2))], [B("fmv")])
    stt(DVE, A2, SC2, 1.0, gTs[:, 1, :], ALU.add, ALU.mult, [B("fmv"), B("gTs")], [B("A2")])
    grt = carve(D)
    for (dst, c0, row, nm) in [(ga1g, 2 * D, 0, "ga1g"), (ga2g, 5 * D, 1, "ga2g")]:
        load(dst, modrow[0:1, c0:c0 + D].partition_broadcast(128), [], [B(nm)])
        load(grt, grow[row:row + 1, :].partition_broadcast(128), [], [B("grt")])
        tt(DVE, dst, dst, grt, ALU.mult, [B(nm), B("grt")], [B(nm)])
    tmpq = [carve_bf(4 * (TQ + 2)).rearrange("p (k t) -> p k t", t=TQ + 2) for _ in range(2)]
    mall_v = [m_.rearrange("(k p) t -> p k t", p=128) for m_ in mixA]
    i = 0
    for qq in range(4):
        for kg in range(4):
            dstv = mixq[:, kg * 4:(kg + 1) * 4, :]
            tq_ = tmpq[i % 2]; tqb = B(("tmpq", i % 2)); ks = slice(kg * 4, (kg + 1) * 4)
            load(tq_[:, :, 1:1025], mall_v[2 * qq][:, ks, :], [B(("mixA", 2 * qq))], [tqb])
            load(tq_[:, :, 1025:2049], mall_v[2 * qq + 1][:, ks, :], [B(("mixA", 2 * qq + 1))], [tqb])
            pl, cl = (2 * qq - 1, 1023) if qq >= 1 else (0, 0)
            load(tq_[:, :, 0:1], mall_v[pl][:, ks, cl:cl + 1], [B(("mixA", pl))], [tqb])
            pr, cr = (2 * qq + 2, 0) if qq <= 2 else (7, 1023)
            load(tq_[:, :, 2049:2050], mall_v[pr][:, ks, cr:cr + 1], [B(("mixA", pr))], [tqb])
            if qq == 0:
                ts(DVE, dstv, tq_, qsel[:, 0:1], None, ALU.mult, None, [tqb, B("qsel")], [B(("mixq", kg))])
            else:
                stt(DVE, dstv, tq_, qsel[:, qq:qq + 1], dstv, ALU.mult, ALU.add, [tqb, B("qsel"), B(("mixq", kg))], [B(("mixq", kg))])
            i += 1
    S.barrier()
    apos[0] = p5_mark
    mixh = carve_bf(KC * 2).rearrange("p (k t) -> p k t", t=2)
    cp(DVE, mixh[:, :, 0:1], mixq[:, :, 0:1], [B("mixq")], [B("mixh")])
    cp(DVE, mixh[:, :, 1:2], mixq[:, :, TQ + 1:TQ + 2], [B("mixq")], [B("mixh")])
    wo = carve_bf(KC * D).rearrange("p (k c) -> p k c", c=D)
    wo_sem = S.new_dsem("wo")
    w_out_v = w_out.rearrange("(k p) c -> p k c", p=128)
    for k in range(KC):
        dma(POOL, wo[:, k, :], w_out_v[:, k, :], wo_sem, [], [B("wo")], new_gen=(k == 0))
    xt = [carve(D) for _ in range(2)]; xm = carve(D); ytmp = [carve(512) for _ in range(2)]; junk = carve_bf(D)
    xn2 = carve_bf(D); h2blk = [carve_bf(KC * 128).rearrange("p (k t) -> p k t", t=128) for _ in range(2)]
    ssq = carve(8); s1 = carve(4)
    h2S_v = h2S.rearrange("(k p) t -> p k t", p=128)
    for tb in range(17):
        nt = 128 if tb < 16 else 2
        x_ = xt[tb % 2]; xb_ = B(("xt", tb % 2))
        if tb < 16:
            load(x_, xq[1 + tb * 128:1 + (tb + 1) * 128, :], [], [xb_])
        else:
            load(x_[0:1, :], xq[0:1, :], [], [xb_])
            load(x_[1:2, :], xq[TQ + 1:TQ + 2, :], [], [xb_])
        for ct in range(4):
            for k in range(KC):
                lhs = mixq[:, k, 1 + tb * 128:1 + (tb + 1) * 128] if tb < 16 else mixh[:, k, :]
                mm(psf[ct][0:nt, :], lhs, wo[:, k, ct * 512:(ct + 1) * 512], k == 0, k == KC - 1, [B("mixq"), B("mixh"), B("wo")], [B(("psf", ct))])
        memset(DVE, ssq[:, 0:4], 0.0, [B("ssq")])
        for ct in range(4):
            act(junk[0:nt, 0:512], psf[ct][0:nt, :], AF.Square, [B(("psf", ct)), B("ssq")], [B("junk"), B("ssq")], accum=ssq[0:nt, ct:ct + 1])
        S.op(DVE, (lambda nt: lambda e: e.reduce_sum(out=s1[0:nt, 0:1], in_=ssq[0:nt, 0:4], axis=AX.X))(nt), reads=[B("ssq")], writes=[B("s1")])
        act(s1[0:nt, 1:2], s1[0:nt, 0:1], AF.Ln, [B("s1"), B("epsc")], [B("s1")], scale=1.0 / D, bias=epsc[0:nt, :])
        act(s1[0:nt, 1:2], s1[0:nt, 1:2], AF.Exp, [B("s1")], [B("s1")], scale=-0.5)
        for ct in range(4):
            yt = ytmp[ct % 2]; yb = B(("ytmp", ct % 2))
            stt(DVE, yt[0:nt, :], psf[ct][0:nt, :], s1[0:nt, 1:2], ga1g[0:nt, ct * 512:(ct + 1) * 512], ALU.mult, ALU.mult, [B(("psf", ct)), B("s1"), B("ga1g")], [yb])
            tt(DVE, xm[0:nt, ct * 512:(ct + 1) * 512], yt[0:nt, :], x_[0:nt, ct * 512:(ct + 1) * 512], ALU.add, [yb, xb_], [B("xm")])
        if tb < 16:
            store(xmidS[tb * 128:(tb + 1) * 128, :], xm, [B("xm")], [B(("xmidS", tb))])
        memset(DVE, ssq[:, 4:5], 0.0, [B("ssq")])
        act(junk[0:nt, :], xm[0:nt, :], AF.Square, [B("xm"), B("ssq")], [B("junk"), B("ssq")], accum=ssq[0:nt, 4:5])
        act(s1[0:nt, 2:3], ssq[0:nt, 4:5], AF.Ln, [B("ssq"), B("epsc")], [B("s1")], scale=1.0 / D, bias=epsc[0:nt, :])
        act(s1[0:nt, 2:3], s1[0:nt, 2:3], AF.Exp, [B("s1")], [B("s1")], scale=-0.5)
        ts(DVE, xn2[0:nt, :], xm[0:nt, :], s1[0:nt, 2:3], None, ALU.mult, None, [B("xm"), B("s1")], [B("xn2")])
        hb_ = h2blk[tb % 2]; hbb = B(("h2blk", tb % 2))
        for g in range(2):
            pb = psb[g]
            for kk in range(8):
                k = g * 8 + kk
                S.op(PE, (lambda pb, kk, k, nt: lambda e: e.transpose(out=pb[:, kk * 128:kk * 128 + nt], in_=xn2[0:nt, k * 128:(k + 1) * 128], identity=identb[0:nt, 0:nt]))(pb, kk, k, nt),
                     reads=[B("xn2"), B("identb")], writes=[B(("psb", g))])
            for kk in range(8):
                k = g * 8 + kk
                act(hb_[:, k, 0:nt], pb[:, kk * 128:kk * 128 + nt], AF.Identity, [B(("psb", g)), B("A2"), B("fmv")], [hbb], scale=A2[:, k:k + 1], bias=SH2[:, k:k + 1])
        if tb < 16:
            store(h2S_v[:, :, 1 + tb * 128:1 + (tb + 1) * 128], hb_, [hbb], [B(("h2S", tb))])
        else:
            ts(DVE, hb_[:, :, 0:1], hb_[:, :, 0:1], hmask[:, 0:1], None, ALU.mult, None, [hbb, B("hmask")], [hbb])
            ts(DVE, hb_[:, :, 1:2], hb_[:, :, 1:2], hmask[:, 1:2], None, ALU.mult, None, [hbb, B("hmask")], [hbb])
            store(h2S_v[:, :, 0:1], hb_[:, :, 0:1], [hbb], [B(("h2S", 16))])
            store(h2S_v[:, :, TQ + 1:TQ + 2], hb_[:, :, 1:2], [hbb], [B(("h2S", 17))])
    S.barrier()
    apos[0] = mq_mark
    if stage == 4:
        return finish(nc, S, out_d)
    st_q[0] = SP
    cfs = carve(2 * HC * 4).rearrange("p (c j) -> p c j", j=4)
    load(cfs, cfw, [], [B("cfs")])
    h2t = [carve_bf(KC * 514).rearrange("p (k t) -> p k t", t=514) for _ in range(2)]
    gTt = carve_bf(HC * 512).rearrange("p (j t) -> p j t", t=512)
    wu = [carve_bf(KC * 256).rearrange("p (k c) -> p k c", c=256) for _ in range(2)]
    wu_sem = [S.new_dsem(f"wu{i}") for i in range(2)]
    wd = [carve_bf(HC * 128).rearrange("p (j c) -> p j c", c=128) for _ in range(2)]
    wd_sem = [S.new_dsem(f"wd{i}") for i in range(2)]
    wuc_sem = [S.new_dsem(f"wuc{i}") for i in range(2)]
    wdc_sem = [S.new_dsem(f"wdc{i}") for i in range(2)]
    y2 = [carve(D) for _ in range(4)]
    ub = [carve(520) for _ in range(2)]; tb_ = [carve(512) for _ in range(2)]; sgb = carve(512)
    xmr = carve(D)
    junkc = carve_bf(D); ssqc = carve(8); s1c = carve(4)
    w_up_v = w_up.rearrange("(k p) c -> p k c", p=128)
    w_dn_v = w_down.rearrange("(j p) c -> p j c", p=128)
    wi = 0; di = 0
    for tt_ in range(4):
        h2 = h2t[tt_ % 2]; h2b = B(("h2t", tt_ % 2))
        load(h2, h2S_v[:, :, tt_ * 512:tt_ * 512 + 514], [B(("h2S", x)) for x in range(18)], [h2b])
        for j in range(HC):
            w = wu[wi % 2]; wb = B(("wu", wi % 2)); ws = wu_sem[wi % 2]; wi += 1
            dma(SP, w.rearrange("p k c -> p (k c)"), wuS[j], wuc_sem[(wi - 1) % 2], [], [wb])
            for part in range(2):
                pa = psf[part * 2]; pk = psf[part * 2 + 1]
                for k in range(KC):
                    mm(pa[:, 0:512], w[:, k, part * 128:(part + 1) * 128], h2[:, k, 0:512], k == 0, k == KC - 1, [wb, h2b], [B(("psf", part * 2))])
                for k in range(KC):
                    mm(pk[:, 0:2], w[:, k, part * 128:(part + 1) * 128], h2[:, k, 512:514], k == 0, k == KC - 1, [wb, h2b], [B(("psf", part * 2 + 1))])
                u = ub[part]; ubb = B(("ub", part))
                cp(ACT, u[:, 0:512], pa[:, 0:512], [B(("psf", part * 2))], [ubb])
                cp(ACT, u[:, 512:514], pk[:, 0:2], [B(("psf", part * 2 + 1))], [ubb])
                cw = cfs[:, part * HC + j, :]
                t_ = tb_[part]; tbb = B(("tb", part))
                ts(DVE, t_, u[:, 1:513], cw[:, 1:2], cw[:, 3:4], ALU.mult, ALU.add, [ubb, B("cfs")], [tbb])
                stt(DVE, t_, u[:, 0:512], cw[:, 0:1], t_, ALU.mult, ALU.add, [ubb, B("cfs"), tbb], [tbb])
                stt(DVE, t_, u[:, 2:514], cw[:, 2:3], t_, ALU.mult, ALU.add, [ubb, B("cfs"), tbb], [tbb])
            act(sgb, tb_[0], AF.Silu, [B(("tb", 0))], [B("sgb")])
            tt(DVE, gTt[:, j, :], sgb, tb_[1], ALU.mult, [B("sgb"), B(("tb", 1))], [B("gTt")])
            if stage == 6:
                d_h2 = nc.dram_tensor("d_h2", [128, KC, 514], BF16, kind="ExternalOutput").ap()
                d_wu = nc.dram_tensor("d_wu", [128, KC, 256], BF16, kind="ExternalOutput").ap()
                d_ub = nc.dram_tensor("d_ub", [2, 128, 514], F32, kind="ExternalOutput").ap()
                d_tb = nc.dram_tensor("d_tb", [3, 128, 512], F32, kind="ExternalOutput").ap()
                d_cf = nc.dram_tensor("d_cf", [128, 2 * HC, 4], F32, kind="ExternalOutput").ap()
                store(d_h2, h2, [h2b], [B("d1")]); store(d_wu, w, [wb], [B("d2")])
                store(d_ub[0], ub[0][:, 0:514], [B(("ub", 0))], [B("d3")]); store(d_ub[1], ub[1][:, 0:514], [B(("ub", 1))], [B("d4")])
                store(d_tb[0], tb_[0], [B(("tb", 0))], [B("d5")]); store(d_tb[1], tb_[1], [B(("tb", 1))], [B("d6")]); store(d_tb[2], sgb, [B("sgb")], [B("d7")])
                store(d_cf, cfs, [B("cfs")], [B("d8")])
                S.barrier()
                return finish(nc, S, out_d)
        if stage == 5 and tt_ == 0:
            gdbg = nc.dram_tensor("gdbg", [128, HC, 512], BF16, kind="ExternalOutput").ap()
            store(gdbg, gTt, [B("gTt")], [B("gdbg")])
        for cth in range(D // 128):
            wdd = wd[di % 2]; wdb = B(("wd", di % 2)); wds = wd_sem[di % 2]; di += 1
            dma(SP, wdd.rearrange("p j c -> p (j c)"), wdS[cth], wdc_sem[(di - 1) % 2], [], [wdb])
            for blk in range(4):
                pi = (cth * 4 + blk) % 4
                for j in range(HC):
                    mm(psf[pi][:, 0:128], gTt[:, j, blk * 128:(blk + 1) * 128], wdd[:, j, :], j == 0, j == HC - 1, [B("gTt"), wdb], [B(("psf", pi))])
                cp(ACT if blk % 2 == 0 else DVE, y2[blk][:, cth * 128:(cth + 1) * 128], psf[pi][:, 0:128], [B(("psf", pi))], [B(("y2", blk))])
        if stage == 5 and tt_ == 0:
            ydbg = nc.dram_tensor("ydbg", [4, 128, D], F32, kind="ExternalOutput").ap()
            for blk in range(4):
                store(ydbg[blk], y2[blk], [B(("y2", blk))], [B(("ydbg", blk))])
            S.barrier()
            return finish(nc, S, out_d)
        for blk in range(4):
            r0 = tt_ * 512 + blk * 128
            load(xmr, xmidS[r0:r0 + 128, :], [B(("xmidS", r0 // 128))], [B("xmr")])
            memset(DVE, ssqc[:, 5:6], 0.0, [B("ssq")])
            act(junkc, y2[blk], AF.Square, [B(("y2", blk)), B("ssq")], [B("junk"), B("ssq")], accum=ssqc[:, 5:6])
            act(s1c[:, 3:4], ssqc[:, 5:6], AF.Ln, [B("ssq"), B("epsc")], [B("s1")], scale=1.0 / D, bias=epsc)
            act(s1c[:, 3:4], s1c[:, 3:4], AF.Exp, [B("s1")], [B("s1")], scale=-0.5)
            stt(DVE, y2[blk], y2[blk], s1c[:, 3:4], ga2g, ALU.mult, ALU.mult, [B(("y2", blk)), B("s1"), B("ga2g")], [B(("y2", blk))])
            tt(POOL, y2[blk], y2[blk], xmr, ALU.add, [B(("y2", blk)), B("xmr")], [B(("y2", blk))])
            store(out_d[r0:r0 + 128, :], y2[blk], [B(("y2", blk))], [B(("out", r0))])
    S.barrier()
    return finish(nc, S, out_d)


def finish(nc, S, out_d):
    S.finalize()
    return nc


_PROG = {}


def _consts():
    c = {}
    c["identf"] = np.eye(128, dtype=np.float32)
    s = np.arange(128)
    c["maskF"] = (s[:, None] <= s[None, :]).astype(np.float32)
    c["maskB"] = (s[:, None] >= s[None, :]).astype(np.float32)
    c["triF"] = (s[:, None] <= s[None, :]).astype(np.float32)
    c["triB"] = (s[:, None] >= s[None, :]).astype(np.float32)
    partner = np.where((s % 32) < 16, s + 16, s - 16)
    pm = np.zeros((128, 128), np.float32)
    pm[partner, s] = 1.0
    c["perm"] = pm
    perm66 = np.array([1, 0] + [NCH + 1 - j for j in range(2, NCH)])
    jb = np.zeros((NCH, NCH), np.float32)
    jb[perm66, np.arange(NCH)] = 1.0
    c["jb"] = jb
    rows = T // 64
    row = np.repeat(np.arange(rows, dtype=np.float32), 64)
    col = np.tile(np.arange(64, dtype=np.float32), rows)
    inv = (10000.0 ** (-np.arange(16, dtype=np.float32) / 16)).astype(np.float32)
    d = np.arange(64)
    axis = d // 32; half = (d // 16) % 2; f = d % 16
    pos = np.where(axis[:, None] == 0, row[None, :], col[None, :]).astype(np.float32)
    ang = (pos * inv[f][:, None]).astype(np.float32)
    cos = np.cos(ang).astype(np.float32); sin = np.sin(ang).astype(np.float32)
    sgn = np.where(half == 0, -1.0, 1.0).astype(np.float32)[:, None]
    c["cos"] = np.ascontiguousarray(np.concatenate([cos, cos], 0))
    c["sin"] = np.ascontiguousarray(np.concatenate([sin * sgn, sin * sgn], 0))
    return c


def _core_inputs(core, inp, consts):
    b, r = core // 4, core % 4
    h0 = 2 * r
    f32 = np.float32
    m = dict(consts)
    x = inp["x"]
    m["xb"] = np.ascontiguousarray(x[b])
    m["ctxb"] = np.ascontiguousarray(inp["ctx"][b])
    xq = np.zeros((TQ + 2, D), f32)
    lo, hi = r * TQ - 1, r * TQ + TQ + 1
    slo, shi = max(lo, 0), min(hi, T)
    xq[slo - lo:shi - lo] = x[b, slo:shi]
    m["xq"] = xq
    cc = np.stack([inp["c"][b], inp["c_ctx"]], -1)
    m["cT"] = np.ascontiguousarray(cc.reshape(KC, 128, 2).transpose(1, 0, 2))
    m["w_mod"] = inp["w_mod"][0]
    m["b_mod"] = inp["b_mod"][0][None, :]
    gt = np.stack([inp["g_pre_mix"][0], inp["g_pre_ffn"][0]], 0)
    m["gT"] = np.ascontiguousarray(gt.reshape(2, KC, 128).transpose(2, 0, 1))
    m["grow"] = np.stack([inp["g_post_mix"][0], inp["g_post_ffn"][0]], 0)
    w = inp["w_in"][0]
    OMQ, OMK, OMV, OMO, OMG = 0, 1024, 2048, 3072, 4096
    ODQ = OMG + 32; ODK = ODQ + 1024; ODV = ODK + 1024
    cols = []
    for base in (OMQ, OMK, ODQ, ODK, OMO, OMV):
        cols += list(range(base + h0 * 128, base + h0 * 128 + 256))
    gcols = [OMG + g * 8 + h0 + hh for g in range(4) for hh in range(2)]
    cols += gcols
    cols += list(range(ODV + h0 * 128, ODV + h0 * 128 + 256))
    m["w_in"] = np.ascontiguousarray(w[:, cols])
    cw = inp["conv_qk_w"][0]; cb = inp["conv_qk_b"][0]
    convw = np.zeros((128, 4, 4), f32)
    for ci, base in enumerate([h0 * 128, h0 * 128 + 128, 1024 + h0 * 128, 1024 + h0 * 128 + 128]):
        convw[:, ci, 0:3] = cw[:, base:base + 128].T
        convw[:, ci, 3] = cb[base:base + 128]
    m["convw"] = convw
    bg = inp["b_gate"][0]
    m["bgate"] = np.tile(np.array([[bg[g, h0 + hh] for g in range(4) for hh in range(2)]], f32), (1, NCH))
    m["gml"] = np.ascontiguousarray(inp["g_mlstm"][0][h0 * 128:h0 * 128 + 256][None, :])
    m["gdf"] = np.ascontiguousarray(inp["g_diff"][0][:, None])
    m["gdr"] = np.ascontiguousarray(inp["g_diff"][0][None, :])
    m["lamv"] = np.concatenate([inp["lambda_q1"][0], inp["lambda_k1"][0], inp["lambda_q2"][0], inp["lambda_k2"][0]])[None, :].astype(f32)
    rows = []
    for rr in range(4):
        rows += list(range(2 * rr * 128, 2 * rr * 128 + 256)) + list(range(1024 + 2 * rr * 128, 1024 + 2 * rr * 128 + 256))
    m["w_out"] = np.ascontiguousarray(inp["w_out"][0][rows, :])
    m["w_up"] = inp["w_up"][0]
    fw = inp["conv_ffn_w"][0]; fb = inp["conv_ffn_b"][0]
    cf = np.concatenate([fw.T, fb[:, None]], 1)
    m["cfw"] = np.ascontiguousarray(cf.reshape(2 * HC, 128, 4).transpose(1, 0, 2))
    m["w_down"] = inp["w_down"][0]
    hm = np.zeros((128, 2), f32)
    hm[:, 0] = 1.0 if r > 0 else 0.0
    hm[:, 1] = 1.0 if r < 3 else 0.0
    m["hmask"] = hm
    qs = np.zeros((128, 4), f32); qs[:, r] = 1.0
    m["qsel"] = qs
    return {k: np.ascontiguousarray(v, dtype=v.dtype) for k, v in m.items()}


def kernel(**inputs):
    inp = {k: np.asarray(v) for k, v in inputs.items()}
    stage = int(os.environ.get("MK_STAGE", "99"))
    if stage not in _PROG:
        _PROG[stage] = build_program(stage)
    nc = _PROG[stage]
    consts = _consts()
    maps = [_core_inputs(c, inp, consts) for c in range(8)]
    res = run_bass_kernel_spmd(nc, maps, core_ids=list(range(8)))
    if stage < 99:
        return res
    out = np.zeros((2, T, D), np.float32)
    for c in range(8):
        b, r = c // 4, c % 4
        out[b, r * TQ:(r + 1) * TQ] = res.results[c]["out"]
    return out
```
